# Optimizing a Trainium2 kernel written in Bass

```python
import math
import jax, jax.numpy as jnp
from jax import lax
import numpy as np

D_MODEL = 1024
BATCH = 4
SEQ = 4096
DEPTH = 1
DEC_BATCH = 4
DEC_SEQ = 8192
PAST_LEN = 128

HEAD_DIM = 64
N_Q_HEADS = 8
N_KV_HEADS = 2
GROUP = N_Q_HEADS // N_KV_HEADS
ATTN_W = N_Q_HEADS * HEAD_DIM
KV_W = N_KV_HEADS * HEAD_DIM
POOL_WINDOWS = (2, 4, 8, 16)
POOL_CH = 128
POOL_W = POOL_CH * len(POOL_WINDOWS)
MIX_W = ATTN_W + POOL_W
IN_W = ATTN_W + 2 * KV_W + POOL_W
D_FF = 4 * D_MODEL
PLE_DIM = 256
GRID_W = 64
ROPE_THETA = 10000.0
ROPE_PAIRS = HEAD_DIM // 4
Q_BLOCK = 128
EPS = 1e-6

kernel_name = "hybrid_gqa_pool_encoder"


def _rmsnorm(x, g):
    xf = x.astype(jnp.float32)
    y = xf * lax.rsqrt(jnp.mean(xf * xf, axis=-1, keepdims=True) + EPS)
    return (y * g.astype(jnp.float32)).astype(x.dtype)


def _grid_angles(T):
    rows = T // GRID_W
    row = jnp.repeat(jnp.arange(rows, dtype=jnp.float32), GRID_W)
    col = jnp.tile(jnp.arange(GRID_W, dtype=jnp.float32), rows)
    inv_freq = ROPE_THETA ** (-jnp.arange(ROPE_PAIRS, dtype=jnp.float32) / ROPE_PAIRS)
    return row[:, None] * inv_freq[None, :], col[:, None] * inv_freq[None, :]


def _rope_half(x, ang):
    x1, x2 = x[..., :ROPE_PAIRS], x[..., ROPE_PAIRS:]
    c = jnp.cos(ang)[None, :, None, :]
    s = jnp.sin(ang)[None, :, None, :]
    return jnp.concatenate([x1 * c - x2 * s, x2 * c + x1 * s], axis=-1)


def _axial_rope(x, ang_row, ang_col):
    xf = x.astype(jnp.float32)
    half = HEAD_DIM // 2
    y = jnp.concatenate([_rope_half(xf[..., :half], ang_row),
                         _rope_half(xf[..., half:], ang_col)], axis=-1)
    return y.astype(x.dtype)


def _attention(q, k, v, q_g, k_g):
    B, T = q.shape[0], q.shape[1]
    ang_row, ang_col = _grid_angles(T)
    q = _axial_rope(_rmsnorm(q, q_g), ang_row, ang_col)
    k = _axial_rope(_rmsnorm(k, k_g), ang_row, ang_col)
    scale = 1.0 / math.sqrt(HEAD_DIM)
    nb = T // Q_BLOCK
    qb = q.reshape(B, nb, Q_BLOCK, N_KV_HEADS, GROUP, HEAD_DIM).transpose(1, 0, 2, 3, 4, 5)

    def block(qi):
        s = jnp.einsum('bqkgd,bskd->bkgqs', qi, k).astype(jnp.float32) * scale
        p = jax.nn.softmax(s, axis=-1).astype(v.dtype)
        return jnp.einsum('bkgqs,bskd->bqkgd', p, v)

    o = lax.map(block, qb)
    return o.transpose(1, 0, 2, 3, 4, 5).reshape(B, T, ATTN_W)


def _pool_mixer(u, w_pool, pool_scale):
    B, T, C = u.shape
    uf = u.astype(jnp.float32)
    cs = jnp.concatenate([jnp.zeros((B, 1, C), jnp.float32), jnp.cumsum(uf, axis=1)], axis=1)
    t = jnp.arange(T)
    outs = []
    for g, w in enumerate(POOL_WINDOWS):
        half = w // 2
        lo = jnp.clip(t - half, 0, T)
        hi = jnp.clip(t + half, 0, T)
        csg = cs[:, :, g * POOL_CH:(g + 1) * POOL_CH]
        win_sum = jnp.take(csg, hi, axis=1) - jnp.take(csg, lo, axis=1)
        mean = win_sum / (hi - lo).astype(jnp.float32)[None, :, None]
        outs.append(mean - uf[:, :, g * POOL_CH:(g + 1) * POOL_CH])
    d = jnp.stack(outs, axis=2).astype(u.dtype)
    y = jnp.einsum('btgc,gcd->btgd', d, w_pool).reshape(B, T, C)
    return y * pool_scale


def _layer(h, p_i, norm_mix_g, w_in, q_norm_g, k_norm_g, w_pool, pool_scale, w_out,
           norm_mlp_g, w_up, w_down, norm_ple_g, w_ple_gate, w_ple_proj):
    B, T, _ = h.shape
    a = _rmsnorm(h, norm_mix_g)
    z = a @ w_in
    q = z[..., :ATTN_W].reshape(B, T, N_Q_HEADS, HEAD_DIM)
    k = z[..., ATTN_W:ATTN_W + KV_W].reshape(B, T, N_KV_HEADS, HEAD_DIM)
    v = z[..., ATTN_W + KV_W:ATTN_W + 2 * KV_W].reshape(B, T, N_KV_HEADS, HEAD_DIM)
    u = z[..., ATTN_W + 2 * KV_W:]
    o_attn = _attention(q, k, v, q_norm_g, k_norm_g)
    o_pool = _pool_mixer(u, w_pool, pool_scale)
    h = h + jnp.concatenate([o_attn, o_pool], axis=-1) @ w_out
    m = _rmsnorm(h, norm_mlp_g)
    h = h + jnp.square(jax.nn.relu(m @ w_up)) @ w_down
    gate = jax.nn.sigmoid((_rmsnorm(h, norm_ple_g) @ w_ple_gate).astype(jnp.float32)).astype(h.dtype)
    h = h + gate * (p_i @ w_ple_proj)
    return h


def _trunk(x, p, norm_mix_g, w_in, q_norm_g, k_norm_g, w_pool, pool_scale, w_out,
           norm_mlp_g, w_up, w_down, norm_ple_g, w_ple_gate, w_ple_proj, final_norm_g):
    h = x
    for i in range(DEPTH):
        h = _layer(h, p[i], norm_mix_g[i], w_in[i], q_norm_g[i], k_norm_g[i], w_pool[i],
                   pool_scale[i], w_out[i], norm_mlp_g[i], w_up[i], w_down[i],
                   norm_ple_g[i], w_ple_gate[i], w_ple_proj[i])
    return _rmsnorm(h, final_norm_g)


def setup_inputs(seed: int = 0) -> dict:
    key = jax.random.key(seed)
    ks = jax.random.split(key, 20)
    f32 = jnp.float32

    def nrm(k, shape, scale):
        return jax.random.normal(k, shape, f32) * scale

    def gain(k, shape):
        return 1.0 + 0.05 * jax.random.normal(k, shape, f32)

    return {
        "x_prompt": nrm(ks[0], (BATCH, SEQ, D_MODEL), 1.0),
        "x_sample": nrm(ks[1], (DEC_BATCH, DEC_SEQ, D_MODEL), 1.0),
        "p_prompt": nrm(ks[2], (DEPTH, BATCH, SEQ, PLE_DIM), 1.0),
        "p_sample": nrm(ks[3], (DEPTH, DEC_BATCH, DEC_SEQ, PLE_DIM), 1.0),
        "norm_mix_g": gain(ks[4], (DEPTH, D_MODEL)),
        "w_in": nrm(ks[5], (DEPTH, D_MODEL, IN_W), D_MODEL ** -0.5),
        "q_norm_g": gain(ks[6], (DEPTH, HEAD_DIM)),
        "k_norm_g": gain(ks[7], (DEPTH, HEAD_DIM)),
        "w_pool": nrm(ks[8], (DEPTH, len(POOL_WINDOWS), POOL_CH, POOL_CH), POOL_CH ** -0.5),
        "pool_scale": gain(ks[9], (DEPTH, POOL_W)),
        "w_out": nrm(ks[10], (DEPTH, MIX_W, D_MODEL), MIX_W ** -0.5),
        "norm_mlp_g": gain(ks[11], (DEPTH, D_MODEL)),
        "w_up": nrm(ks[12], (DEPTH, D_MODEL, D_FF), D_MODEL ** -0.5),
        "w_down": nrm(ks[13], (DEPTH, D_FF, D_MODEL), D_FF ** -0.5),
        "norm_ple_g": gain(ks[14], (DEPTH, D_MODEL)),
        "w_ple_gate": nrm(ks[15], (DEPTH, D_MODEL, D_MODEL), D_MODEL ** -0.5),
        "w_ple_proj": nrm(ks[16], (DEPTH, PLE_DIM, D_MODEL), PLE_DIM ** -0.5),
        "final_norm_g": gain(ks[17], (D_MODEL,)),
    }


def reference(x_prompt, x_sample, p_prompt, p_sample, norm_mix_g, w_in, q_norm_g, k_norm_g,
              w_pool, pool_scale, w_out, norm_mlp_g, w_up, w_down, norm_ple_g, w_ple_gate,
              w_ple_proj, final_norm_g):
    y_prompt = _trunk(x_prompt, p_prompt, norm_mix_g, w_in, q_norm_g, k_norm_g, w_pool, pool_scale,
                      w_out, norm_mlp_g, w_up, w_down, norm_ple_g, w_ple_gate, w_ple_proj, final_norm_g)
    y_sample = _trunk(x_sample, p_sample, norm_mix_g, w_in, q_norm_g, k_norm_g, w_pool, pool_scale,
                      w_out, norm_mlp_g, w_up, w_down, norm_ple_g, w_ple_gate, w_ple_proj, final_norm_g)
    return (y_prompt, y_sample)
```

```python
import numpy as np
import concourse.bass as bass
import concourse.mybir as mybir
from concourse.bass_utils import run_bass_kernel_spmd

F32 = mybir.dt.float32
BF16 = mybir.dt.bfloat16
ALU = mybir.AluOpType
AF = mybir.ActivationFunctionType

ENGS = ("pe", "act", "dve", "pool", "sp")
EPS = 1e-6
NRING = 6
NCHUNK = 45


class Buf:
    __slots__ = ("name", "writer", "readers")

    def __init__(self, name):
        self.name = name
        self.writer = None
        self.readers = []


class Op:
    __slots__ = ("eng", "fn", "deps", "signal", "count", "dma_key")

    def __init__(self, eng, fn, dma_key=None):
        self.eng = eng
        self.fn = fn
        self.deps = []
        self.signal = False
        self.count = None
        self.dma_key = dma_key


class Prog:
    def __init__(self, nc):
        self.nc = nc
        self.streams = {e: [] for e in ENGS}
        self.dma_keys = {}

    def add(self, eng, fn, reads=(), writes=(), dma_key=None, extra_deps=()):
        op = Op(eng, fn, dma_key)
        deps = []
        for b in reads:
            if b.writer is not None:
                deps.append((b.writer, "raw"))
        for b in writes:
            if b.writer is not None:
                deps.append((b.writer, "waw"))
            for r in b.readers:
                deps.append((r, "war"))
        for d in extra_deps:
            deps.append((d, "raw"))
        if dma_key is not None:
            prev = self.dma_keys.get(dma_key)
            if prev is not None:
                deps.append((prev, "raw"))
        seen = set()
        for d, kind in deps:
            if d is op or id(d) in seen:
                continue
            if d.dma_key is None and op.dma_key is None and d.eng == eng:
                if eng == "pe" or kind != "raw":
                    continue
            seen.add(id(d))
            op.deps.append(d)
            d.signal = True
        for b in reads:
            b.readers.append(op)
        for b in writes:
            b.writer = op
            b.readers = []
        self.streams[eng].append(op)
        if dma_key is not None:
            self.dma_keys[dma_key] = op
        return op

    def emit(self, final_waits=()):
        nc = self.nc
        sems = {e: nc.alloc_semaphore(name=f"s_{e}") for e in ENGS}
        ksems = {k: nc.alloc_semaphore(name=f"k_{k}") for k in self.dma_keys}
        kcnt = {k: 0 for k in self.dma_keys}
        for e in ENGS:
            c = 0
            for op in self.streams[e]:
                if op.dma_key is not None:
                    kcnt[op.dma_key] += 16
                    op.count = kcnt[op.dma_key]
                elif op.signal:
                    c += 1
                    op.count = c
        engobj = {"pe": "tensor", "act": "scalar", "dve": "vector", "pool": "gpsimd", "sp": "sync"}
        with nc.Block() as block:
            for e in ENGS:
                ops = self.streams[e]
                fw = list(final_waits) if e == "sp" else []

                def body(eng, ops=ops, e=e, fw=fw):
                    waited = {}

                    def do_wait(d):
                        key = ("k", d.dma_key) if d.dma_key is not None else ("e", d.eng)
                        if waited.get(key, 0) >= d.count:
                            return
                        waited[key] = d.count
                        sem = ksems[d.dma_key] if d.dma_key is not None else sems[d.eng]
                        eng.wait_ge(sem, d.count)

                    for op in ops:
                        for d in op.deps:
                            do_wait(d)
                        inst = op.fn(eng)
                        if op.dma_key is not None:
                            inst.then_inc(ksems[op.dma_key], 16)
                        elif op.signal:
                            inst.then_inc(sems[e], 1)
                    for d in fw:
                        do_wait(d)

                getattr(block, engobj[e])(body)


def build_program(jobs):
    nc = bass.Bass("TRN2", target_bir_lowering=False)
    P = Prog(nc)
    TMAX = max(T for T, _ in jobs)
    ntiles_total = sum(O // 512 for _, O in jobs)

    def din(name, shape):
        return nc.dram_tensor(name, list(shape), F32, kind="ExternalInput").ap()

    xkeys = [din(f"xkeys{j}", (T, 1024)) for j, (T, O) in enumerate(jobs)]
    xpad = [din(f"xpad{j}", (O + 256, 1024)) for j, (T, O) in enumerate(jobs)]
    pin = [din(f"p{j}", (O, 256)) for j, (T, O) in enumerate(jobs)]
    cosd = [din(f"cos{j}", (128, T)) for j, (T, O) in enumerate(jobs)]
    sind = [din(f"sin{j}", (128, T)) for j, (T, O) in enumerate(jobs)]
    bandd = [din(f"band{j}", (128, 2560)) for j, (T, O) in enumerate(jobs)]
    yout = [nc.dram_tensor(f"y{j}", [O, 1024], F32, kind="ExternalOutput").ap() for j, (T, O) in enumerate(jobs)]
    wts = din("wts", (NCHUNK, 128, 2048))
    wkvd = din("wkv", (128, 2048))
    cstbd = din("cstb", (128, 896))
    cstfd = din("cstf", (128, 32))
    fngd = din("fng", (128, 1024))
    wsc = nc.dram_tensor("wsc", [NCHUNK, 128, 2048], BF16, kind="Internal").ap()

    sb = nc.alloc_sbuf_tensor
    KT = sb("KT", [128, TMAX], BF16)
    VX = sb("VX", [128, TMAX // 128, 192], BF16)
    XS = [sb("XA", [128, 4, 1024], F32), sb("XB", [128, 4, 1024], F32)]
    XH = sb("XH", [128, 2, 1024], F32)
    AT = sb("AT", [128, 8, 768], BF16)
    QT = sb("QT", [128, 4, 512], BF16)
    U6 = sb("U6", [128, 6, 512], BF16)
    MIX = sb("MIX", [128, 8, 512], BF16)
    H = sb("H", [128, 32, 512], BF16)
    RING = sb("RING", [128, NRING, 2048], BF16)
    WKV = sb("WKV", [128, 2048], BF16)
    PTB = sb("PTB", [128, 4, 512], BF16)
    CB = sb("CB", [128, 896], BF16)
    BAND = sb("BAND", [128, 2560], BF16)
    FNG = sb("FNG", [128, 1024], F32)
    CF = sb("CF", [128, 32], F32)
    PBF = sb("PBF", [128, 4, 256], BF16)
    PTT = sb("PTT", [128, 2, 512], BF16)
    TMPALL = sb("TMPALL", [128, 5, 512], F32)
    Y32 = TMPALL[:, 0, :]
    RSTD = TMPALL[:, 1, :]
    T12 = TMPALL[:, 2:4, :]
    T1 = TMPALL[:, 2, :]
    T2 = TMPALL[:, 3, :]
    RELU = T12
    REC = TMPALL[:, 4, :]
    SQYB = sb("SQYB", [128, 2, 512], BF16)
    SQ = SQYB[:, 0, :]
    YB = SQYB[:, 1, :]
    JUNKS = [TMPALL[:, i, :].bitcast(BF16) for i in (1, 2, 3, 4, 0)] + [SQYB[:].rearrange("p a n -> p (a n)")]
    ST = sb("ST", [128, 48], F32)
    CS = sb("CSB", [128, 1024], F32)[:]
    PSALL = nc.alloc_psum_tensor("PSALL", [128, 4096], F32)
    PS = [PSALL[:, b * 512:(b + 1) * 512] for b in range(8)]

    def hview(hbase, i):
        return H[:, hbase + 2 * i:hbase + 2 * i + 2, :].rearrange("p a n -> p (a n)")

    DT = H[:, 12:16, :]
    OST = H[:, 20:28, :].rearrange("p a n -> p (a n)").bitcast(F32)
    YS = H[:, 16:32, :].rearrange("p a n -> p (a n)").bitcast(F32).rearrange("p (c d) -> p c d", c=4)
    GT = H[:, 20:28, :].rearrange("p a n -> p (a n)").bitcast(F32)
    IDENT = CB[:, 0:128]
    RT = CB[:, 128:256]
    BLK = CB[:, 256:384]
    WPOOL = CB[:, 384:896].rearrange("p (g d) -> p g d", g=4)

    b_kt = [Buf(f"kt{i}") for i in range(TMAX // 512)]
    b_vx = [Buf(f"vx{i}") for i in range(TMAX // 512)]
    b_xs = [[Buf(f"xa{i}") for i in range(4)], [Buf(f"xb{i}") for i in range(4)]]
    b_xh = [Buf("xh0"), Buf("xh1")]
    b_at = [Buf(f"at{i}") for i in range(8)]
    b_qt = [Buf(f"qt{i}") for i in range(4)]
    b_u = [Buf(f"u{i}") for i in range(6)]
    b_mix = [Buf(f"mix{i}") for i in range(8)]
    b_h = [Buf(f"h{i}") for i in range(32)]
    b_ring = [Buf(f"ring{i}") for i in range(NRING)]
    b_wsc = [Buf(f"wsc{i}") for i in range(NCHUNK)]
    b_wkv, b_cb, b_band, b_fng, b_cf = Buf("wkv"), Buf("cb"), Buf("band"), Buf("fng"), Buf("cf")
    b_pt = [Buf(f"pt{i}") for i in range(4)]
    b_pbf, b_ptt = Buf("pbf"), Buf("ptt")
    b_y32, b_sq, b_yb, b_rstd, b_t1, b_t2 = (Buf(n) for n in ("y32", "sq", "yb", "rstd", "t1", "t2"))
    b_rec, b_vones = Buf("rec"), Buf("vones")
    b_junks = [[b_rstd], [b_t1], [b_t2], [b_rec], [b_y32], [b_sq, b_yb]]
    b_st = [Buf("st0"), Buf("st1")]
    b_relu = [b_t1, b_t2]
    b_ps = [Buf(f"ps{i}") for i in range(8)]
    b_y = [Buf(f"yout{j}") for j in range(len(jobs))]
    b_hc = lambda hbase, i: [b_h[hbase + 2 * i], b_h[hbase + 2 * i + 1]]
    b_dt = lambda g: [b_h[12 + g]]
    b_cs = [Buf("cs0"), Buf("cs1")]
    b_gt = b_h[20:28]

    def dma(q, out, in_, key, reads=(), writes=(), extra=()):
        return P.add(q, lambda e: e.dma_start(out=out, in_=in_), reads, writes, dma_key=key, extra_deps=extra)

    def mm(out, lhsT, rhs, start, stop, reads, writes):
        return P.add("pe", lambda e: e.matmul(out, lhsT=lhsT, rhs=rhs, start=start, stop=stop), reads, writes)

    def tr(out, in_, reads, writes):
        return P.add("pe", lambda e: e.transpose(out=out, in_=in_, identity=IDENT), list(reads) + [b_cb], writes)

    def act(out, in_, func, reads, writes, scale=1.0, bias=None, accum=None):
        def f(e):
            kw = {}
            if bias is not None:
                kw["bias"] = bias
            if accum is not None:
                kw["accum_out"] = accum
            return e.activation(out=out, in_=in_, func=func, scale=scale, **kw)
        return P.add("act", f, reads, writes)

    def ts(eng, out, in0, s1, op0, reads, writes):
        return P.add(eng, lambda e: e.tensor_scalar(out=out, in0=in0, scalar1=s1, scalar2=None, op0=op0),
                     reads, writes)

    def tt(eng, out, in0, in1, op, reads, writes):
        return P.add(eng, lambda e: e.tensor_tensor(out=out, in0=in0, in1=in1, op=op), reads, writes)

    def cp(eng, out, in_, reads, writes):
        if eng == "act":
            return act(out, in_, AF.Copy, reads, writes)
        return P.add(eng, lambda e: e.tensor_copy(out=out, in_=in_), reads, writes)

    wb_ctr = [0]

    def wbank():
        b = wb_ctr[0] % 4
        wb_ctr[0] += 1
        return b

    ab_ctr = [0]

    def abank():
        b = 4 + ab_ctr[0] % 4
        ab_ctr[0] += 1
        return b

    alt = [0]

    def evac_eng():
        alt[0] += 1
        return "act" if alt[0] % 2 else "dve"

    dma("pool", CB[:], cstbd, "c_cb", writes=[b_cb])
    dma("pool", WKV[:], wkvd, "c_wkv", writes=[b_wkv])
    dma("sp", CF[:], cstfd, "c_cf", writes=[b_cf])
    dma("sp", FNG[:], fngd, "c_fng", writes=[b_fng])
    P.add("dve", lambda e: e.memset(VX[:, :, 64:128], 1.0), writes=[b_vones])
    cast_state = {"n": 0, "last": None}

    NCA = 8

    def cast_some(cnt, limit):
        for _ in range(cnt):
            k = cast_state["n"]
            if k >= limit:
                return
            dma("pool", wsc[k], wts[k], f"cast{k % 4}", writes=[b_wsc[k]])
            cast_state["n"] = k + 1

    ring_state = {"issued": 0, "got": 0, "released": 0}
    ring_total = ntiles_total * NCHUNK

    def ring_issue():
        n = ring_state["issued"]
        if n >= ring_total:
            return
        slot = n % NRING
        k = n % NCHUNK
        if k >= NCA and cast_state["n"] < NCHUNK:
            cast_some(NCHUNK, NCHUNK)
        assert cast_state["n"] > k
        dma("sp", RING[:, slot, :], wsc[k], f"r{slot}", reads=[b_wsc[k]], writes=[b_ring[slot]])
        ring_state["issued"] = n + 1

    def ring_get():
        n = ring_state["got"]
        assert n < ring_state["issued"], "ring underflow"
        ring_state["got"] = n + 1
        slot = n % NRING
        return RING[:, slot, :], b_ring[slot]

    def ring_release(cnt=1):
        for _ in range(cnt):
            ring_state["released"] += 1
            ring_issue()

    GM, GL, GP, QG, KG, PSC = 0, 8, 16, 24, 25, 26

    def n_stats(srcs, s):
        n = len(srcs)
        o = 24 * s
        for i, (ap, bufs) in enumerate(srcs):
            act(JUNKS[i], ap, AF.Square, bufs, b_junks[i] + [b_st[s]], accum=ST[:, o + i:o + i + 1])
        act(ST[:, o + 8:o + 8 + n], ST[:, o:o + n], AF.Ln, [b_st[s]], [b_st[s]], scale=1.0 / 1024, bias=EPS)
        act(ST[:, o + 16:o + 16 + n], ST[:, o + 8:o + 8 + n], AF.Exp, [b_st[s]], [b_st[s]], scale=-0.5)

    def n_apr(srcs, s, hbase):
        o = 24 * s + 16
        for i, (ap, bufs) in enumerate(srcs):
            ts("dve", hview(hbase, i), ap, ST[:, o + i:o + i + 1], ALU.mult, list(bufs) + [b_st[s]], b_hc(hbase, i))

    def n_tr(n, hbase, gcol, act_kcs=(0, 2, 4, 6)):
        for kc in range(8):
            bk = wbank()
            pbf = PS[bk][:].bitcast(BF16)
            for i in range(n):
                tr(pbf[:, i * 128:(i + 1) * 128], hview(hbase, i)[:, kc * 128:(kc + 1) * 128], b_hc(hbase, i),
                   [b_ps[bk]])
            eng = "act" if kc in act_kcs else "dve"
            g = CF[:, gcol + kc:gcol + kc + 1]
            if eng == "act":
                act(AT[:, kc, 0:n * 128], pbf[:, 0:n * 128], AF.Copy, [b_ps[bk], b_cf], [b_at[kc]], scale=g)
            else:
                ts("dve", AT[:, kc, 0:n * 128], pbf[:, 0:n * 128], g, ALU.mult, [b_ps[bk], b_cf], [b_at[kc]])

    def rope_pre(zb, gcol):
        Z = PS[zb][:]
        act(SQ, Z, AF.Square, [b_ps[zb]], [b_sq])
        act(Y32, Z, AF.Copy, [b_ps[zb], b_cf], [b_y32], scale=CF[:, gcol:gcol + 1])
        cp("dve", YB, Y32, [b_y32], [b_yb])

    def rope_mm(banks):
        sb_, rb = banks
        mm(PS[sb_][:], BLK, SQ, True, True, [b_sq, b_cb], [b_ps[sb_]])
        mm(PS[rb][:], RT, YB, True, True, [b_yb, b_cb], [b_ps[rb]])

    def rope_post(banks, dst, dst_bufs, add_eng="dve"):
        sb_, rb = banks
        act(RSTD, PS[sb_][:], AF.Ln, [b_ps[sb_]], [b_rstd], scale=1.0 / 64, bias=EPS)
        act(RSTD, RSTD, AF.Exp, [b_rstd], [b_rstd], scale=-0.5)
        tt("dve", T1, Y32, CS[:, 0:512], ALU.mult, [b_y32, b_cs[0]], [b_t1])
        tt("dve", T2, PS[rb][:], CS[:, 512:1024], ALU.mult, [b_ps[rb], b_cs[1]], [b_t2])
        tt(add_eng, T1, T1, T2, ALU.add, [b_t1, b_t2], [b_t1])
        tt("pool", dst, T1, RSTD, ALU.mult, [b_t1, b_rstd], dst_bufs)

    xctr = [0]
    last_store = []
    for j, (T, OWN) in enumerate(jobs):
        NKT = T // 512
        NKC = T // 128
        NT = OWN // 512
        NOC = OWN // 128
        dma("pool", BAND[:, 0:1280], bandd[j][:, 0:1280], "c_band0", writes=[b_band])
        dma("pool", BAND[:, 1280:2560], bandd[j][:, 1280:2560], "c_band1", writes=[b_band])
        cast_per = -(-NCA // NKT)

        def a_load(kt):
            par = xctr[0] % 2
            xctr[0] += 1
            src = xkeys[j][kt * 512:(kt + 1) * 512, :].rearrange("(c p) d -> p c d", p=128)
            dma("sp", XS[par][:], src, f"x{par}", writes=b_xs[par])
            return par

        def a_cs(kt):
            dma("sp", CS[:, 0:512], cosd[j][:, kt * 512:(kt + 1) * 512], "cs0", writes=[b_cs[0]])
            dma("sp", CS[:, 512:1024], sind[j][:, kt * 512:(kt + 1) * 512], "cs1", writes=[b_cs[1]])

        def a_srcs(par):
            return [(XS[par][:, c, :], [b_xs[par][c]]) for c in range(4)]

        par = a_load(0)
        a_cs(0)
        n_stats(a_srcs(par), 1)
        for kt in range(NKT):
            n_apr(a_srcs(par), 1, 0)
            n_tr(4, 0, GM, act_kcs=(7,))
            if j == 0:
                cast_some(cast_per, NCA)
            npar = a_load(kt + 1) if kt + 1 < NKT else None
            zb = wbank()
            for kc in range(8):
                mm(PS[zb][:], WKV[:, kc * 128:(kc + 1) * 128], AT[:, kc, 0:512], kc == 0, kc == 7,
                   [b_wkv, b_at[kc]], [b_ps[zb]])
            rope_pre(zb, KG)
            vb = abank()
            for c in range(4):
                for kc in range(8):
                    mm(PS[vb][:, c * 128:(c + 1) * 128], AT[:, kc, c * 128:(c + 1) * 128],
                       WKV[:, 1024 + kc * 128:1024 + (kc + 1) * 128], kc == 0, kc == 7,
                       [b_wkv, b_at[kc]], [b_ps[vb]])
            rb = (abank(), abank())
            rope_mm(rb)
            if npar is not None:
                n_stats(a_srcs(npar), 1)
            rope_post(rb, KT[:, kt * 512:(kt + 1) * 512], [b_kt[kt]], add_eng="pool")
            if kt + 1 < NKT:
                a_cs(kt + 1)
            pv = PS[vb][:].rearrange("p (c n) -> p c n", c=4)
            cp("dve", VX[:, kt * 4:(kt + 1) * 4, 0:64], pv[:, :, 0:64], [b_ps[vb]], [b_vx[kt]])
            cp("act", VX[:, kt * 4:(kt + 1) * 4, 128:192], pv[:, :, 64:128], [b_ps[vb]], [b_vx[kt]])
            par = npar
        if j == 0:
            cast_some(NCA, NCA)
            for _ in range(NRING):
                ring_issue()

        tile_par = {}

        def b_load(it):
            par = xctr[0] % 2
            xctr[0] += 1
            tile_par[it] = par
            r0 = it * 512
            dma("sp", XH[:, 0, :], xpad[j][r0:r0 + 128, :], "xh0", writes=[b_xh[0]])
            src = xpad[j][r0 + 128:r0 + 640, :].rearrange("(c p) d -> p c d", p=128)
            dma("sp", XS[par][:], src, f"x{par}", writes=b_xs[par])
            dma("sp", XH[:, 1, :], xpad[j][r0 + 640:r0 + 768, :], "xh1", writes=[b_xh[1]])

        def ld_cs(it):
            dma("sp", CS[:, 0:512], cosd[j][:, it * 512:(it + 1) * 512], "cs0", writes=[b_cs[0]])
            dma("sp", CS[:, 512:1024], sind[j][:, it * 512:(it + 1) * 512], "cs1", writes=[b_cs[1]])

        def b_srcs6(it):
            par = tile_par[it]
            return ([(XH[:, 0, :], [b_xh[0]])] + [(XS[par][:, c, :], [b_xs[par][c]]) for c in range(4)]
                    + [(XH[:, 1, :], [b_xh[1]])])

        def b_srcs4(it):
            par = tile_par[it]
            return [(XS[par][:, c, :], [b_xs[par][c]]) for c in range(4)]

        def head(it):
            psrc = pin[j][it * 512:(it + 1) * 512, :].rearrange("(c p) d -> p c d", p=128)
            dma("pool", PBF[:], psrc, "p", writes=[b_pbf])
            n_tr(6, 0, GM)
            wq = [ring_get(), ring_get()]
            wu = [ring_get(), ring_get()]

            def qproj(jq):
                wv_, wb_ = wq[jq // 2]
                wq_v = wv_.rearrange("p (f k m) -> p f k m", f=2, k=8)
                for kc in range(8):
                    mm(PS[jq][:], wq_v[:, jq % 2, kc, :], AT[:, kc, 128:640], kc == 0, kc == 7,
                       [wb_, b_at[kc]], [b_ps[jq]])

            qproj(0)
            qproj(1)
            for jq in range(4):
                rope_pre(jq, QG)
                rb = (abank(), abank())
                rope_mm(rb)
                if jq + 2 < 4:
                    qproj(jq + 2)
                rope_post(rb, QT[:, jq, :], [b_qt[jq]])
            ring_release(2)

            def deferred():
                dctr = [0]

                def dbank():
                    b = 6 + dctr[0] % 2
                    dctr[0] += 1
                    return b

                for c in range(6):
                    ub = dbank()
                    for kc in range(8):
                        wv_, wb_ = wu[kc // 4]
                        wu_v = wv_.rearrange("p (k n) -> p k n", k=4)
                        mm(PS[ub][:], AT[:, kc, c * 128:(c + 1) * 128], wu_v[:, kc % 4, :], kc == 0, kc == 7,
                           [wb_, b_at[kc]], [b_ps[ub]])
                        if kc == 7:
                            cp("dve", U6[:, c, :], PS[ub][:], [b_ps[ub]], [b_u[c]])
                        yield
                ring_release(2)
                bandv = BAND[:].rearrange("p (t g n) -> p t g n", t=5, g=4)
                for g in range(4):
                    db = dbank()
                    for ci in range(1, 5):
                        gi = it * 4 + (ci - 1)
                        curt = 2 if gi == 0 else (4 if gi == NOC - 1 else 3)
                        o = PS[db][:, (ci - 1) * 128:ci * 128]
                        mm(o, U6[:, ci - 1, g * 128:(g + 1) * 128], bandv[:, 0, g, :], True, False,
                           [b_u[ci - 1], b_band], [b_ps[db]])
                        yield
                        mm(o, U6[:, ci, g * 128:(g + 1) * 128], bandv[:, curt, g, :], False, False,
                           [b_u[ci], b_band], [b_ps[db]])
                        yield
                        mm(o, U6[:, ci + 1, g * 128:(g + 1) * 128], bandv[:, 1, g, :], False, True,
                           [b_u[ci + 1], b_band], [b_ps[db]])
                        if ci == 4:
                            cp("dve", DT[:, g, :], PS[db][:], [b_ps[db]], b_dt(g))
                        yield
                    ob = dbank()
                    mm(PS[ob][:], WPOOL[:, g, :], DT[:, g, :], True, True, [b_cb] + b_dt(g), [b_ps[ob]])
                    ts("dve", MIX[:, 4 + g, :], PS[ob][:], CF[:, PSC + g:PSC + g + 1], ALU.mult,
                       [b_ps[ob], b_cf], [b_mix[4 + g]])
                    yield

            return deferred()

        N_DEFERRED = 48 + 48 + 4

        def attention(it, dgen):
            nsteps = 4 * NKC
            acc = 0.0
            rate = N_DEFERRED / float(nsteps - NKC // 2)
            for c in range(4):
                oa, ob_ = 4, 5

                def qk(kc):
                    sa, sbk = 2 * (kc % 2), 2 * (kc % 2) + 1
                    mm(PS[sa][:], KT[0:64, kc * 128:(kc + 1) * 128], QT[0:64, c, :], True, True,
                       [b_kt[kc // 4], b_qt[c]], [b_ps[sa]])
                    mm(PS[sbk][:], KT[64:128, kc * 128:(kc + 1) * 128], QT[64:128, c, :], True, True,
                       [b_kt[kc // 4], b_qt[c]], [b_ps[sbk]])

                qk(0)
                for kc in range(NKC):
                    if kc + 1 < NKC:
                        qk(kc + 1)
                    sa, sbk = 2 * (kc % 2), 2 * (kc % 2) + 1
                    act(PTB[:, sa, :], PS[sa][:], AF.Exp, [b_ps[sa]], [b_pt[sa]], scale=0.125)
                    act(PTB[:, sbk, :], PS[sbk][:], AF.Exp, [b_ps[sbk]], [b_pt[sbk]], scale=0.125)
                    mm(PS[oa][:], VX[:, kc, 0:128], PTB[:, sa, :], kc == 0, kc == NKC - 1,
                       [b_vx[kc // 4], b_vones, b_pt[sa]], [b_ps[oa]])
                    mm(PS[ob_][:], VX[:, kc, 64:192], PTB[:, sbk, :], kc == 0, kc == NKC - 1,
                       [b_vx[kc // 4], b_vones, b_pt[sbk]], [b_ps[ob_]])
                    acc += rate
                    while acc >= 1.0:
                        acc -= 1.0
                        next(dgen, None)
                cp("dve", OST[:, 0:512], PS[oa][:], [b_ps[oa]], b_gt)
                cp("dve", OST[:, 512:1024], PS[ob_][:], [b_ps[ob_]], b_gt)
                P.add("dve", lambda e: e.reciprocal(out=REC[0:64, :], in_=OST[64:128, 0:512]), b_gt, [b_rec])
                tt("dve", MIX[0:64, c, :], OST[0:64, 0:512], REC[0:64, :], ALU.mult, b_gt + [b_rec], [b_mix[c]])
                P.add("dve", lambda e: e.reciprocal(out=REC[64:128, :], in_=OST[0:64, 512:1024]), b_gt, [b_rec])
                tt("dve", MIX[64:128, c, :], OST[64:128, 512:1024], REC[64:128, :], ALU.mult,
                   b_gt + [b_rec], [b_mix[c]])
            for _ in dgen:
                pass

        def mid(it, nxt):
            par = tile_par[it]
            X = XS[par]
            bx = b_xs[par]
            wo = [ring_get() for _ in range(4)]
            for t in range(4):
                for hf in range(2):
                    ab = 4 + (2 * t + hf) % 4
                    for kc in range(8):
                        wv_, wb_ = wo[kc // 2]
                        wo_v = wv_.rearrange("p (k n) -> p k n", k=2)
                        mm(PS[ab][:], MIX[:, kc, t * 128:(t + 1) * 128], wo_v[:, kc % 2, hf * 512:(hf + 1) * 512],
                           kc == 0, kc == 7, [wb_, b_mix[kc]], [b_ps[ab]])
                    xs = X[:, t, hf * 512:(hf + 1) * 512]
                    tt("dve", xs, xs, PS[ab][:], ALU.add, [bx[t], b_ps[ab]], [bx[t]])
            ring_release(4)
            n_stats(b_srcs4(it), 0)
            n_apr(b_srcs4(it), 0, 0)
            n_tr(4, 0, GL)
            for r in range(16):
                wv_, wb_ = ring_get()
                wu_v = wv_.rearrange("p (f k m) -> p f k m", f=2, k=8)
                for ff in range(2):
                    fc = 2 * r + ff
                    ub = wbank()
                    for kc in range(8):
                        mm(PS[ub][:], wu_v[:, ff, kc, :], AT[:, kc, 0:512], kc == 0, kc == 7,
                           [wb_, b_at[kc]], [b_ps[ub]])
                    rs = fc % 2
                    act(RELU[:, rs, :], PS[ub][:], AF.Relu, [b_ps[ub]], [b_relu[rs]])
                    tt("dve" if fc % 4 else "pool", H[:, fc, :], RELU[:, rs, :], RELU[:, rs, :], ALU.mult,
                       [b_relu[rs]], [b_h[fc]])
                ring_release(1)
            if nxt is not None:
                n_stats(b_srcs6(nxt), 1)
            for hf in range(2):
                for r in range(8):
                    wv_, wb_ = ring_get()
                    wd_v = wv_.rearrange("p (f n) -> p f n", f=4)
                    for ff in range(4):
                        fc = 4 * r + ff
                        for t in range(4):
                            mm(PS[4 + t][:], H[:, fc, t * 128:(t + 1) * 128], wd_v[:, ff, :], fc == 0, fc == 31,
                               [wb_, b_h[fc]], [b_ps[4 + t]])
                    ring_release(1)
                for t in range(4):
                    xs = X[:, t, hf * 512:(hf + 1) * 512]
                    tt("dve", xs, xs, PS[4 + t][:], ALU.add, [bx[t], b_ps[4 + t]], [bx[t]])
            n_stats(b_srcs4(it), 0)
            n_apr(b_srcs4(it), 0, 12)
            if nxt is not None:
                n_apr(b_srcs6(nxt), 1, 0)
            n_tr(4, 12, GP)
            for kc in range(2):
                bk = wbank()
                pbf = PS[bk][:].bitcast(BF16)
                for t in range(4):
                    tr(pbf[:, t * 128:(t + 1) * 128], PBF[:, t, kc * 128:(kc + 1) * 128], [b_pbf], [b_ps[bk]])
                cp(evac_eng(), PTT[:, kc, :], pbf[:, 0:512], [b_ps[bk]], [b_ptt])
            wg = [ring_get() for _ in range(4)]
            wp_v, wp_b = ring_get()
            wp_v = wp_v.rearrange("p (k n) -> p k n", k=2)
            for t in range(4):
                for hf in range(2):
                    gb = 4 + (2 * t + hf) % 2
                    pb = 6 + (2 * t + hf) % 2
                    for kc in range(8):
                        wv_, wb_ = wg[kc // 2]
                        wg_v = wv_.rearrange("p (k n) -> p k n", k=2)
                        mm(PS[gb][:], AT[:, kc, t * 128:(t + 1) * 128], wg_v[:, kc % 2, hf * 512:(hf + 1) * 512],
                           kc == 0, kc == 7, [wb_, b_at[kc]], [b_ps[gb]])
                    for kc in range(2):
                        mm(PS[pb][:], PTT[:, kc, t * 128:(t + 1) * 128], wp_v[:, kc, hf * 512:(hf + 1) * 512],
                           kc == 0, kc == 1, [wp_b, b_ptt], [b_ps[pb]])
                    k2 = (2 * t + hf) % 2
                    G = GT[:, k2 * 512:(k2 + 1) * 512]
                    gbufs = b_gt[2 * k2:2 * k2 + 2]
                    act(G, PS[gb][:], AF.Sigmoid, [b_ps[gb]], gbufs)
                    tt("dve", G, G, PS[pb][:], ALU.mult, gbufs + [b_ps[pb]], gbufs)
                    xs = X[:, t, hf * 512:(hf + 1) * 512]
                    tt("dve", xs, xs, G, ALU.add, [bx[t]] + gbufs, [bx[t]])
            ring_release(5)

        def final(it):
            par = tile_par[it]
            n_stats(b_srcs4(it), 0)
            for t in range(4):
                xs = XS[par][:, t, :]
                ys = YS[:, t, :]
                P.add("dve", lambda e, xs=xs, ys=ys, t=t: e.scalar_tensor_tensor(
                    out=ys, in0=xs, scalar=ST[:, 16 + t:17 + t], in1=FNG[:], op0=ALU.mult, op1=ALU.mult),
                    [b_xs[par][t], b_st[0], b_fng], b_h[16 + 4 * t:20 + 4 * t])
            dst = yout[j][it * 512:(it + 1) * 512, :].rearrange("(c p) d -> p c d", p=128)
            st = dma("sp", dst, YS, "y", reads=b_h[16:32], writes=[b_y[j]])
            last_store.append(st)

        b_load(0)
        ld_cs(0)
        n_stats(b_srcs6(0), 1)
        n_apr(b_srcs6(0), 1, 0)
        dgen = head(0)
        for it in range(NT):
            nxt = it + 1 if it + 1 < NT else None
            if nxt is not None:
                b_load(nxt)
                ld_cs(nxt)
            attention(it, dgen)
            mid(it, nxt)
            if nxt is not None:
                dgen = head(nxt)
            final(it)

    assert ring_state["got"] == ring_total and ring_state["issued"] == ring_total, ring_state
    P.emit(final_waits=last_store[-1:])
    return nc


def _rope_tables(T):
    t = np.arange(T)
    row = (t // 64).astype(np.float32)
    col = (t % 64).astype(np.float32)
    inv = (np.float32(10000.0) ** (-np.arange(16, dtype=np.float32) / np.float32(16))).astype(np.float32)
    ar = (row[:, None] * inv[None, :]).astype(np.float32)
    ac = (col[:, None] * inv[None, :]).astype(np.float32)
    ang = np.concatenate([ar, ar, ac, ac], axis=1)
    cos = np.cos(ang).astype(np.float32).T
    sin = np.sin(ang).astype(np.float32).T
    return np.ascontiguousarray(np.tile(cos, (2, 1))), np.ascontiguousarray(np.tile(sin, (2, 1)))


def _band_block(T, w, src0, dst0):
    half = w // 2
    out = np.zeros((128, 128), np.float32)
    for jj in range(128):
        tp = dst0 + jj
        if tp < 0 or tp >= T:
            continue
        lo = max(tp - half, 0)
        hi = min(tp + half, T)
        cnt = hi - lo
        for t in range(lo, hi):
            i = t - src0
            if 0 <= i < 128:
                out[i, jj] += 1.0 / cnt
        i = tp - src0
        if 0 <= i < 128:
            out[i, jj] -= 1.0
    return out


def _band_mats(T, OWN, half):
    wins = (2, 4, 8, 16)
    g0 = half * OWN
    BIG = 1 << 20
    mid0 = 4096
    res = np.zeros((128, 5, 4, 128), np.float32)
    for g, w in enumerate(wins):
        res[:, 0, g] = _band_block(BIG, w, mid0 - 128, mid0)
        res[:, 1, g] = _band_block(BIG, w, mid0 + 128, mid0)
        res[:, 2, g] = _band_block(T, w, g0, g0)
        res[:, 3, g] = _band_block(BIG, w, mid0, mid0)
        l0 = g0 + OWN - 128
        res[:, 4, g] = _band_block(T, w, l0, l0)
    return np.ascontiguousarray(res.reshape(128, 2560))


def _weight_chunks(w_in, w_out, w_up, w_down, w_gate, w_proj):
    ch = np.zeros((NCHUNK, 128, 2048), np.float32)
    qcols = np.zeros((4, 128), np.int64)
    for jq in range(4):
        qcols[jq, :64] = jq * 64 + np.arange(64)
        qcols[jq, 64:] = (jq + 4) * 64 + np.arange(64)
    wr = w_in.reshape(8, 128, 1280)
    n = 0
    for r in range(2):
        blk = np.stack([wr[:, :, qcols[2 * r + f]] for f in range(2)], axis=0)
        ch[n] = blk.transpose(2, 0, 1, 3).reshape(128, 2048)
        n += 1
    for r in range(2):
        blk = wr[4 * r:4 * r + 4, :, 768:1280]
        ch[n] = blk.transpose(1, 0, 2).reshape(128, 2048)
        n += 1
    rows = np.zeros(1024, np.int64)
    for c in range(4):
        rows[c * 128:c * 128 + 64] = c * 64 + np.arange(64)
        rows[c * 128 + 64:(c + 1) * 128] = (c + 4) * 64 + np.arange(64)
    rows[512:] = np.arange(512, 1024)
    wo = w_out[rows, :].reshape(8, 128, 1024)
    for r in range(4):
        ch[n] = wo[2 * r:2 * r + 2].transpose(1, 0, 2).reshape(128, 2048)
        n += 1
    wu = w_up.reshape(8, 128, 32, 128)
    for r in range(16):
        blk = wu[:, :, 2 * r:2 * r + 2, :]
        ch[n] = blk.transpose(1, 2, 0, 3).reshape(128, 2048)
        n += 1
    wd = w_down.reshape(32, 128, 2, 512)
    for hf in range(2):
        for r in range(8):
            blk = wd[4 * r:4 * r + 4, :, hf, :]
            ch[n] = blk.transpose(1, 0, 2).reshape(128, 2048)
            n += 1
    wg = w_gate.reshape(8, 128, 1024)
    for r in range(4):
        ch[n] = wg[2 * r:2 * r + 2].transpose(1, 0, 2).reshape(128, 2048)
        n += 1
    ch[n] = w_proj.reshape(2, 128, 1024).transpose(1, 0, 2).reshape(128, 2048)
    n += 1
    assert n == NCHUNK
    wkv = np.zeros((128, 2048), np.float32)
    wkv[:, 0:1024] = wr[:, :, 512:640].transpose(1, 0, 2).reshape(128, 1024)
    wkv[:, 1024:2048] = wr[:, :, 640:768].transpose(1, 0, 2).reshape(128, 1024)
    return ch, wkv


def _consts(norm_mix_g, norm_mlp_g, norm_ple_g, q_norm_g, k_norm_g, pool_scale, w_pool, final_norm_g):
    cb = np.zeros((128, 896), np.float32)
    cb[:, 0:128] = np.eye(128, dtype=np.float32)
    rt = np.zeros((128, 128), np.float32)
    for m in range(128):
        if m % 32 < 16:
            rt[m + 16, m] = -1.0
        else:
            rt[m - 16, m] = 1.0
    cb[:, 128:256] = rt
    blk = np.zeros((128, 128), np.float32)
    blk[:64, :64] = 1.0
    blk[64:, 64:] = 1.0
    cb[:, 256:384] = blk
    cb[:, 384:896] = w_pool.transpose(1, 0, 2).reshape(128, 512)
    cf = np.zeros((128, 32), np.float32)
    cf[:, 0:8] = norm_mix_g.reshape(8, 128).T
    cf[:, 8:16] = norm_mlp_g.reshape(8, 128).T
    cf[:, 16:24] = norm_ple_g.reshape(8, 128).T
    cf[:, 24] = np.tile(q_norm_g, 2)
    cf[:, 25] = np.tile(k_norm_g, 2)
    cf[:, 26:30] = pool_scale.reshape(4, 128).T
    fng = np.ascontiguousarray(np.broadcast_to(final_norm_g[None, :], (128, 1024))).astype(np.float32)
    return cb, cf, fng


def make_in_maps(jobs_data, weights):
    (norm_mix_g, w_in, q_norm_g, k_norm_g, w_pool, pool_scale, w_out, norm_mlp_g, w_up, w_down,
     norm_ple_g, w_ple_gate, w_ple_proj, final_norm_g) = weights
    ch, wkv = _weight_chunks(w_in, w_out, w_up, w_down, w_ple_gate, w_ple_proj)
    cb, cf, fng = _consts(norm_mix_g, norm_mlp_g, norm_ple_g, q_norm_g, k_norm_g, pool_scale, w_pool, final_norm_g)
    in_maps = []
    tabs = {}
    for core_jobs in jobs_data:
        m = {"wts": ch, "wkv": wkv, "cstb": cb, "cstf": cf, "fng": fng}
        for j, jd in enumerate(core_jobs):
            x, p, half = jd["x"], jd["p"], jd["half"]
            T = x.shape[0]
            OWN = T // 2
            o0 = half * OWN
            own = slice(o0, o0 + OWN)
            oth = slice((1 - half) * OWN, (1 - half) * OWN + OWN)
            m[f"xkeys{j}"] = np.ascontiguousarray(np.concatenate([x[own], x[oth]], axis=0))
            xp = np.zeros((OWN + 256, 1024), np.float32)
            xp[128:128 + OWN] = x[own]
            if o0 >= 128:
                xp[0:128] = x[o0 - 128:o0]
            if o0 + OWN + 128 <= T:
                xp[128 + OWN:] = x[o0 + OWN:o0 + OWN + 128]
            m[f"xpad{j}"] = xp
            m[f"p{j}"] = np.ascontiguousarray(p[own])
            if T not in tabs:
                tabs[T] = _rope_tables(T)
            cos, sin = tabs[T]
            m[f"cos{j}"] = np.ascontiguousarray(np.concatenate([cos[:, own], cos[:, oth]], axis=1))
            m[f"sin{j}"] = np.ascontiguousarray(np.concatenate([sin[:, own], sin[:, oth]], axis=1))
            m[f"band{j}"] = _band_mats(T, OWN, half)
        in_maps.append(m)
    return in_maps


_NC_CACHE = {}


def run(x_prompt, x_sample, p_prompt, p_sample, weights, n_cores=8):
    Ts, Tp = x_sample.shape[1], x_prompt.shape[1]
    jobs = [(Ts, Ts // 2), (Tp, Tp // 2)]
    key = tuple(jobs)
    if key not in _NC_CACHE:
        _NC_CACHE[key] = build_program(jobs)
    nc = _NC_CACHE[key]
    jobs_data = []
    for c in range(n_cores):
        s, half = c // 2, c % 2
        jobs_data.append([
            {"x": x_sample[s], "p": p_sample[s], "half": half},
            {"x": x_prompt[s], "p": p_prompt[s], "half": half},
        ])
    in_maps = make_in_maps(jobs_data, weights)
    res = run_bass_kernel_spmd(nc, in_maps, core_ids=list(range(n_cores)))
    y_s = np.zeros_like(x_sample)
    y_p = np.zeros_like(x_prompt)
    for c in range(n_cores):
        s, half = c // 2, c % 2
        r = res.results[c]
        y_s[s, half * (Ts // 2):(half + 1) * (Ts // 2)] = r["y0"]
        y_p[s, half * (Tp // 2):(half + 1) * (Tp // 2)] = r["y1"]
    return y_p, y_s


def kernel(x_prompt, x_sample, p_prompt, p_sample, norm_mix_g, w_in, q_norm_g, k_norm_g, w_pool, pool_scale,
           w_out, norm_mlp_g, w_up, w_down, norm_ple_g, w_ple_gate, w_ple_proj, final_norm_g):
    f = lambda a: np.asarray(a, dtype=np.float32)
    weights = (f(norm_mix_g)[0], f(w_in)[0], f(q_norm_g)[0], f(k_norm_g)[0], f(w_pool)[0], f(pool_scale)[0],
               f(w_out)[0], f(norm_mlp_g)[0], f(w_up)[0], f(w_down)[0], f(norm_ple_g)[0], f(w_ple_gate)[0],
               f(w_ple_proj)[0], f(final_norm_g))
    y_p, y_s = run(f(x_prompt), f(x_sample), f(p_prompt)[0], f(p_sample)[0], weights)
    return (y_p, y_s)
```

```python
import numpy as np
import concourse.bass as bass
import concourse.mybir as mybir
from concourse.bass_utils import run_bass_kernel_spmd

F32 = mybir.dt.float32
BF16 = mybir.dt.bfloat16
ALU = mybir.AluOpType
AF = mybir.ActivationFunctionType

ENGS = ("pe", "act", "dve", "pool", "sp")
EPS = 1e-6
NRING = 6
NCHUNK = 45


class Buf:
    __slots__ = ("name", "writer", "readers")

    def __init__(self, name):
        self.name = name
        self.writer = None
        self.readers = []


class Op:
    __slots__ = ("eng", "fn", "deps", "signal", "count", "dma_key")

    def __init__(self, eng, fn, dma_key=None):
        self.eng = eng
        self.fn = fn
        self.deps = []
        self.signal = False
        self.count = None
        self.dma_key = dma_key


class Prog:
    def __init__(self, nc):
        self.nc = nc
        self.streams = {e: [] for e in ENGS}
        self.dma_keys = {}

    def add(self, eng, fn, reads=(), writes=(), dma_key=None, extra_deps=()):
        op = Op(eng, fn, dma_key)
        deps = []
        for b in reads:
            if b.writer is not None:
                deps.append((b.writer, "raw"))
        for b in writes:
            if b.writer is not None:
                deps.append((b.writer, "waw"))
            for r in b.readers:
                deps.append((r, "war"))
        for d in extra_deps:
            deps.append((d, "raw"))
        if dma_key is not None:
            prev = self.dma_keys.get(dma_key)
            if prev is not None:
                deps.append((prev, "raw"))
        seen = set()
        for d, kind in deps:
            if d is op or id(d) in seen:
                continue
            if d.dma_key is None and op.dma_key is None and d.eng == eng:
                if eng == "pe" or kind != "raw":
                    continue
            seen.add(id(d))
            op.deps.append(d)
            d.signal = True
        for b in reads:
            b.readers.append(op)
        for b in writes:
            b.writer = op
            b.readers = []
        self.streams[eng].append(op)
        if dma_key is not None:
            self.dma_keys[dma_key] = op
        return op

    def emit(self, final_waits=()):
        nc = self.nc
        sems = {e: nc.alloc_semaphore(name=f"s_{e}") for e in ENGS}
        ksems = {k: nc.alloc_semaphore(name=f"k_{k}") for k in self.dma_keys}
        kcnt = {k: 0 for k in self.dma_keys}
        for e in ENGS:
            c = 0
            for op in self.streams[e]:
                if op.dma_key is not None:
                    kcnt[op.dma_key] += 16
                    op.count = kcnt[op.dma_key]
                elif op.signal:
                    c += 1
                    op.count = c
        engobj = {"pe": "tensor", "act": "scalar", "dve": "vector", "pool": "gpsimd", "sp": "sync"}
        with nc.Block() as block:
            for e in ENGS:
                ops = self.streams[e]
                fw = list(final_waits) if e == "sp" else []

                def body(eng, ops=ops, e=e, fw=fw):
                    waited = {}

                    def do_wait(d):
                        key = ("k", d.dma_key) if d.dma_key is not None else ("e", d.eng)
                        if waited.get(key, 0) >= d.count:
                            return
                        waited[key] = d.count
                        sem = ksems[d.dma_key] if d.dma_key is not None else sems[d.eng]
                        eng.wait_ge(sem, d.count)

                    for op in ops:
                        for d in op.deps:
                            do_wait(d)
                        inst = op.fn(eng)
                        if op.dma_key is not None:
                            inst.then_inc(ksems[op.dma_key], 16)
                        elif op.signal:
                            inst.then_inc(sems[e], 1)
                    for d in fw:
                        do_wait(d)

                getattr(block, engobj[e])(body)


def build_program(jobs):
    nc = bass.Bass("TRN2", target_bir_lowering=False)
    P = Prog(nc)
    TMAX = max(T for T, _ in jobs)
    ntiles_total = sum(O // 512 for _, O in jobs)

    def din(name, shape):
        return nc.dram_tensor(name, list(shape), F32, kind="ExternalInput").ap()

    xkeys = [din(f"xkeys{j}", (T, 1024)) for j, (T, O) in enumerate(jobs)]
    xpad = [din(f"xpad{j}", (O + 256, 1024)) for j, (T, O) in enumerate(jobs)]
    pin = [din(f"p{j}", (O, 256)) for j, (T, O) in enumerate(jobs)]
    cosd = [din(f"cos{j}", (128, T)) for j, (T, O) in enumerate(jobs)]
    sind = [din(f"sin{j}", (128, T)) for j, (T, O) in enumerate(jobs)]
    bandd = [din(f"band{j}", (128, 2560)) for j, (T, O) in enumerate(jobs)]
    yout = [nc.dram_tensor(f"y{j}", [O, 1024], F32, kind="ExternalOutput").ap() for j, (T, O) in enumerate(jobs)]
    wts = din("wts", (NCHUNK, 128, 2048))
    wkvd = din("wkv", (128, 2048))
    cstbd = din("cstb", (128, 896))
    cstfd = din("cstf", (128, 32))
    fngd = din("fng", (128, 1024))
    wsc = nc.dram_tensor("wsc", [NCHUNK, 128, 2048], BF16, kind="Internal").ap()

    sb = nc.alloc_sbuf_tensor
    KT = sb("KT", [128, TMAX], BF16)
    VX = sb("VX", [128, TMAX // 128, 192], BF16)
    XS = [sb("XA", [128, 4, 1024], F32), sb("XB", [128, 4, 1024], F32)]
    XH = sb("XH", [128, 2, 1024], F32)
    AT = sb("AT", [128, 8, 768], BF16)
    QT = sb("QT", [128, 4, 512], BF16)
    U6 = sb("U6", [128, 6, 512], BF16)
    MIX = sb("MIX", [128, 8, 512], BF16)
    H = sb("H", [128, 32, 512], BF16)
    RING = sb("RING", [128, NRING, 2048], BF16)
    WKV = sb("WKV", [128, 2048], BF16)
    PTB = sb("PTB", [128, 4, 512], BF16)
    CB = sb("CB", [128, 896], BF16)
    BAND = sb("BAND", [128, 2560], BF16)
    FNG = sb("FNG", [128, 1024], F32)
    CF = sb("CF", [128, 32], F32)
    PBF = sb("PBF", [128, 4, 256], BF16)
    PTT = sb("PTT", [128, 2, 512], BF16)
    TMPALL = sb("TMPALL", [128, 5, 512], F32)
    Y32 = TMPALL[:, 0, :]
    RSTD = TMPALL[:, 1, :]
    T12 = TMPALL[:, 2:4, :]
    T1 = TMPALL[:, 2, :]
    T2 = TMPALL[:, 3, :]
    RELU = T12
    REC = TMPALL[:, 4, :]
    SQYB = sb("SQYB", [128, 2, 512], BF16)
    SQ = SQYB[:, 0, :]
    YB = SQYB[:, 1, :]
    JUNKS = [TMPALL[:, i, :].bitcast(BF16) for i in (1, 2, 3, 4, 0)] + [SQYB[:].rearrange("p a n -> p (a n)")]
    ST = sb("ST", [128, 48], F32)
    CS = sb("CSB", [128, 1024], F32)[:]
    PSALL = nc.alloc_psum_tensor("PSALL", [128, 4096], F32)
    PS = [PSALL[:, b * 512:(b + 1) * 512] for b in range(8)]

    def hview(hbase, i):
        return H[:, hbase + 2 * i:hbase + 2 * i + 2, :].rearrange("p a n -> p (a n)")

    DT = H[:, 12:16, :]
    YS = H[:, 16:32, :].rearrange("p a n -> p (a n)").bitcast(F32).rearrange("p (c d) -> p c d", c=4)
    GT = H[:, 20:28, :].rearrange("p a n -> p (a n)").bitcast(F32)
    IDENT = CB[:, 0:128]
    RT = CB[:, 128:256]
    BLK = CB[:, 256:384]
    WPOOL = CB[:, 384:896].rearrange("p (g d) -> p g d", g=4)

    b_kt = [Buf(f"kt{i}") for i in range(TMAX // 512)]
    b_vx = [Buf(f"vx{i}") for i in range(TMAX // 512)]
    b_xs = [[Buf(f"xa{i}") for i in range(4)], [Buf(f"xb{i}") for i in range(4)]]
    b_xh = [Buf("xh0"), Buf("xh1")]
    b_at = [Buf(f"at{i}") for i in range(8)]
    b_qt = [Buf(f"qt{i}") for i in range(4)]
    b_u = [Buf(f"u{i}") for i in range(6)]
    b_mix = [Buf(f"mix{i}") for i in range(8)]
    b_h = [Buf(f"h{i}") for i in range(32)]
    b_ring = [Buf(f"ring{i}") for i in range(NRING)]
    b_wsc = [Buf(f"wsc{i}") for i in range(NCHUNK)]
    b_wkv, b_cb, b_band, b_fng, b_cf = Buf("wkv"), Buf("cb"), Buf("band"), Buf("fng"), Buf("cf")
    b_pt = [Buf(f"pt{i}") for i in range(4)]
    b_pbf, b_ptt = Buf("pbf"), Buf("ptt")
    b_y32, b_sq, b_yb, b_rstd, b_t1, b_t2 = (Buf(n) for n in ("y32", "sq", "yb", "rstd", "t1", "t2"))
    b_rec, b_vones = Buf("rec"), Buf("vones")
    b_junks = [[b_rstd], [b_t1], [b_t2], [b_rec], [b_y32], [b_sq, b_yb]]
    b_st = [Buf("st0"), Buf("st1")]
    b_relu = [b_t1, b_t2]
    b_ps = [Buf(f"ps{i}") for i in range(8)]
    b_y = [Buf(f"yout{j}") for j in range(len(jobs))]
    b_hc = lambda hbase, i: [b_h[hbase + 2 * i], b_h[hbase + 2 * i + 1]]
    b_dt = lambda g: [b_h[12 + g]]
    b_cs = [Buf("cs0"), Buf("cs1")]
    b_gt = b_h[20:28]

    def dma(q, out, in_, key, reads=(), writes=(), extra=()):
        return P.add(q, lambda e: e.dma_start(out=out, in_=in_), reads, writes, dma_key=key, extra_deps=extra)

    def mm(out, lhsT, rhs, start, stop, reads, writes):
        return P.add("pe", lambda e: e.matmul(out, lhsT=lhsT, rhs=rhs, start=start, stop=stop), reads, writes)

    def tr(out, in_, reads, writes):
        return P.add("pe", lambda e: e.transpose(out=out, in_=in_, identity=IDENT), list(reads) + [b_cb], writes)

    def act(out, in_, func, reads, writes, scale=1.0, bias=None, accum=None):
        def f(e):
            kw = {}
            if bias is not None:
                kw["bias"] = bias
            if accum is not None:
                kw["accum_out"] = accum
            return e.activation(out=out, in_=in_, func=func, scale=scale, **kw)
        return P.add("act", f, reads, writes)

    def ts(eng, out, in0, s1, op0, reads, writes):
        return P.add(eng, lambda e: e.tensor_scalar(out=out, in0=in0, scalar1=s1, scalar2=None, op0=op0),
                     reads, writes)

    def tt(eng, out, in0, in1, op, reads, writes):
        return P.add(eng, lambda e: e.tensor_tensor(out=out, in0=in0, in1=in1, op=op), reads, writes)

    def cp(eng, out, in_, reads, writes):
        if eng == "act":
            return act(out, in_, AF.Copy, reads, writes)
        return P.add(eng, lambda e: e.tensor_copy(out=out, in_=in_), reads, writes)

    wb_ctr = [0]

    def wbank():
        b = wb_ctr[0] % 4
        wb_ctr[0] += 1
        return b

    ab_ctr = [0]

    def abank():
        b = 4 + ab_ctr[0] % 4
        ab_ctr[0] += 1
        return b

    alt = [0]

    def evac_eng():
        alt[0] += 1
        return "act" if alt[0] % 2 else "dve"

    dma("pool", CB[:], cstbd, "c_cb", writes=[b_cb])
    dma("pool", WKV[:], wkvd, "c_wkv", writes=[b_wkv])
    dma("sp", CF[:], cstfd, "c_cf", writes=[b_cf])
    dma("sp", FNG[:], fngd, "c_fng", writes=[b_fng])
    P.add("dve", lambda e: e.memset(VX[:, :, 64:128], 1.0), writes=[b_vones])
    cast_state = {"n": 0, "last": None}

    NCA = 8

    def cast_some(cnt, limit):
        for _ in range(cnt):
            k = cast_state["n"]
            if k >= limit:
                return
            dma("pool", wsc[k], wts[k], f"cast{k % 4}", writes=[b_wsc[k]])
            cast_state["n"] = k + 1

    ring_state = {"issued": 0, "got": 0, "released": 0}
    ring_total = ntiles_total * NCHUNK

    def ring_issue():
        n = ring_state["issued"]
        if n >= ring_total:
            return
        slot = n % NRING
        k = n % NCHUNK
        if k >= NCA and cast_state["n"] < NCHUNK:
            cast_some(NCHUNK, NCHUNK)
        assert cast_state["n"] > k
        dma("sp", RING[:, slot, :], wsc[k], f"r{slot}", reads=[b_wsc[k]], writes=[b_ring[slot]])
        ring_state["issued"] = n + 1

    def ring_get():
        n = ring_state["got"]
        assert n < ring_state["issued"], "ring underflow"
        ring_state["got"] = n + 1
        slot = n % NRING
        return RING[:, slot, :], b_ring[slot]

    def ring_release(cnt=1):
        for _ in range(cnt):
            ring_state["released"] += 1
            ring_issue()

    GM, GL, GP, QG, KG, PSC = 0, 8, 16, 24, 25, 26

    _recb = REC.bitcast(BF16)
    TS = [
        dict(y32=Y32, rstd=RSTD, t1=T1, t2=T2, sq=SQ, yb=YB,
             b_y32=[b_y32], b_rstd=[b_rstd], b_t1=[b_t1], b_t2=[b_t2], b_sq=[b_sq], b_yb=[b_yb]),
        dict(y32=PTB[:, 0:2, :].rearrange("p a n -> p (a n)").bitcast(F32),
             rstd=PTB[:, 2:4, :].rearrange("p a n -> p (a n)").bitcast(F32),
             t1=MIX[:, 0:2, :].rearrange("p a n -> p (a n)").bitcast(F32),
             t2=MIX[:, 2:4, :].rearrange("p a n -> p (a n)").bitcast(F32),
             sq=_recb[:, 0:512], yb=_recb[:, 512:1024],
             b_y32=[b_pt[0], b_pt[1]], b_rstd=[b_pt[2], b_pt[3]], b_t1=[b_mix[0], b_mix[1]],
             b_t2=[b_mix[2], b_mix[3]], b_sq=[b_rec], b_yb=[b_rec]),
    ]

    def n_stats(srcs, s):
        n = len(srcs)
        o = 24 * s
        for i, (ap, bufs) in enumerate(srcs):
            act(JUNKS[i], ap, AF.Square, bufs, b_junks[i] + [b_st[s]], accum=ST[:, o + i:o + i + 1])
        act(ST[:, o + 8:o + 8 + n], ST[:, o:o + n], AF.Ln, [b_st[s]], [b_st[s]], scale=1.0 / 1024, bias=EPS)
        act(ST[:, o + 16:o + 16 + n], ST[:, o + 8:o + 8 + n], AF.Exp, [b_st[s]], [b_st[s]], scale=-0.5)

    def n_apr(srcs, s, hbase):
        o = 24 * s + 16
        for i, (ap, bufs) in enumerate(srcs):
            ts("dve", hview(hbase, i), ap, ST[:, o + i:o + i + 1], ALU.mult, list(bufs) + [b_st[s]], b_hc(hbase, i))

    def n_tr(n, hbase, gcol, act_kcs=(0, 2, 4, 6)):
        for kc in range(8):
            bk = wbank()
            pbf = PS[bk][:].bitcast(BF16)
            for i in range(n):
                tr(pbf[:, i * 128:(i + 1) * 128], hview(hbase, i)[:, kc * 128:(kc + 1) * 128], b_hc(hbase, i),
                   [b_ps[bk]])
            eng = "act" if kc in act_kcs else "dve"
            g = CF[:, gcol + kc:gcol + kc + 1]
            if eng == "act":
                act(AT[:, kc, 0:n * 128], pbf[:, 0:n * 128], AF.Copy, [b_ps[bk], b_cf], [b_at[kc]], scale=g)
            else:
                ts("dve", AT[:, kc, 0:n * 128], pbf[:, 0:n * 128], g, ALU.mult, [b_ps[bk], b_cf], [b_at[kc]])

    def rope_pre(zb, gcol, k=0):
        t = TS[k]
        Z = PS[zb][:]
        act(t["sq"], Z, AF.Square, [b_ps[zb]], t["b_sq"])
        act(t["y32"], Z, AF.Copy, [b_ps[zb], b_cf], t["b_y32"], scale=CF[:, gcol:gcol + 1])
        cp("dve", t["yb"], t["y32"], t["b_y32"], t["b_yb"])

    def rope_mm(banks, k=0):
        t = TS[k]
        sb_, rb = banks
        mm(PS[sb_][:], BLK, t["sq"], True, True, t["b_sq"] + [b_cb], [b_ps[sb_]])
        mm(PS[rb][:], RT, t["yb"], True, True, t["b_yb"] + [b_cb], [b_ps[rb]])

    def rope_post(banks, dst, dst_bufs, add_eng="dve", k=0):
        t = TS[k]
        sb_, rb = banks
        act(t["rstd"], PS[sb_][:], AF.Ln, [b_ps[sb_]], t["b_rstd"], scale=1.0 / 64, bias=EPS)
        act(t["rstd"], t["rstd"], AF.Exp, t["b_rstd"], t["b_rstd"], scale=-0.5)
        tt("dve", t["t1"], t["y32"], CS[:, 0:512], ALU.mult, t["b_y32"] + [b_cs[0]], t["b_t1"])
        tt("dve", t["t2"], PS[rb][:], CS[:, 512:1024], ALU.mult, [b_ps[rb], b_cs[1]], t["b_t2"])
        tt(add_eng, t["t1"], t["t1"], t["t2"], ALU.add, t["b_t1"] + t["b_t2"], t["b_t1"])
        tt("pool", dst, t["t1"], t["rstd"], ALU.mult, t["b_t1"] + t["b_rstd"], dst_bufs)

    xctr = [0]
    last_store = []
    for j, (T, OWN) in enumerate(jobs):
        NKT = T // 512
        NKC = T // 128
        NT = OWN // 512
        NOC = OWN // 128
        dma("pool", BAND[:, 0:1280], bandd[j][:, 0:1280], "c_band0", writes=[b_band])
        dma("pool", BAND[:, 1280:2560], bandd[j][:, 1280:2560], "c_band1", writes=[b_band])
        cast_per = -(-NCA // NKT)

        def a_load(kt):
            par = xctr[0] % 2
            xctr[0] += 1
            src = xkeys[j][kt * 512:(kt + 1) * 512, :].rearrange("(c p) d -> p c d", p=128)
            dma("sp", XS[par][:], src, f"x{par}", writes=b_xs[par])
            return par

        def a_cs(kt):
            dma("sp", CS[:, 0:512], cosd[j][:, kt * 512:(kt + 1) * 512], "cs0", writes=[b_cs[0]])
            dma("sp", CS[:, 512:1024], sind[j][:, kt * 512:(kt + 1) * 512], "cs1", writes=[b_cs[1]])

        def a_srcs(par):
            return [(XS[par][:, c, :], [b_xs[par][c]]) for c in range(4)]

        par = a_load(0)
        a_cs(0)
        n_stats(a_srcs(par), 1)
        for kt in range(NKT):
            n_apr(a_srcs(par), 1, 0)
            n_tr(4, 0, GM, act_kcs=(7,))
            if j == 0:
                cast_some(cast_per, NCA)
            npar = a_load(kt + 1) if kt + 1 < NKT else None
            zb = wbank()
            for kc in range(8):
                mm(PS[zb][:], WKV[:, kc * 128:(kc + 1) * 128], AT[:, kc, 0:512], kc == 0, kc == 7,
                   [b_wkv, b_at[kc]], [b_ps[zb]])
            rope_pre(zb, KG)
            vb = abank()
            for c in range(4):
                for kc in range(8):
                    mm(PS[vb][:, c * 128:(c + 1) * 128], AT[:, kc, c * 128:(c + 1) * 128],
                       WKV[:, 1024 + kc * 128:1024 + (kc + 1) * 128], kc == 0, kc == 7,
                       [b_wkv, b_at[kc]], [b_ps[vb]])
            rb = (abank(), abank())
            rope_mm(rb)
            if npar is not None:
                n_stats(a_srcs(npar), 1)
            rope_post(rb, KT[:, kt * 512:(kt + 1) * 512], [b_kt[kt]], add_eng="pool")
            if kt + 1 < NKT:
                a_cs(kt + 1)
            pv = PS[vb][:].rearrange("p (c n) -> p c n", c=4)
            cp("dve", VX[:, kt * 4:(kt + 1) * 4, 0:64], pv[:, :, 0:64], [b_ps[vb]], [b_vx[kt]])
            cp("act", VX[:, kt * 4:(kt + 1) * 4, 128:192], pv[:, :, 64:128], [b_ps[vb]], [b_vx[kt]])
            par = npar
        if j == 0:
            cast_some(NCA, NCA)
            for _ in range(NRING):
                ring_issue()

        tile_par = {}

        def b_load(it):
            par = xctr[0] % 2
            xctr[0] += 1
            tile_par[it] = par
            r0 = it * 512
            dma("sp", XH[:, 0, :], xpad[j][r0:r0 + 128, :], "xh0", writes=[b_xh[0]])
            src = xpad[j][r0 + 128:r0 + 640, :].rearrange("(c p) d -> p c d", p=128)
            dma("sp", XS[par][:], src, f"x{par}", writes=b_xs[par])
            dma("sp", XH[:, 1, :], xpad[j][r0 + 640:r0 + 768, :], "xh1", writes=[b_xh[1]])

        def ld_cs(it):
            dma("sp", CS[:, 0:512], cosd[j][:, it * 512:(it + 1) * 512], "cs0", writes=[b_cs[0]])
            dma("sp", CS[:, 512:1024], sind[j][:, it * 512:(it + 1) * 512], "cs1", writes=[b_cs[1]])

        def b_srcs6(it):
            par = tile_par[it]
            return ([(XH[:, 0, :], [b_xh[0]])] + [(XS[par][:, c, :], [b_xs[par][c]]) for c in range(4)]
                    + [(XH[:, 1, :], [b_xh[1]])])

        def b_srcs4(it):
            par = tile_par[it]
            return [(XS[par][:, c, :], [b_xs[par][c]]) for c in range(4)]

        def head(it):
            psrc = pin[j][it * 512:(it + 1) * 512, :].rearrange("(c p) d -> p c d", p=128)
            dma("pool", PBF[:], psrc, "p", writes=[b_pbf])
            n_tr(6, 0, GM)
            wq = [ring_get(), ring_get()]
            wu = [ring_get(), ring_get()]

            def qproj(jq):
                wv_, wb_ = wq[jq // 2]
                wq_v = wv_.rearrange("p (f k m) -> p f k m", f=2, k=8)
                for kc in range(8):
                    mm(PS[jq][:], wq_v[:, jq % 2, kc, :], AT[:, kc, 128:640], kc == 0, kc == 7,
                       [wb_, b_at[kc]], [b_ps[jq]])

            def uproj(c):
                ub = abank()
                for kc in range(8):
                    wv_, wb_ = wu[kc // 4]
                    wu_v = wv_.rearrange("p (k n) -> p k n", k=4)
                    mm(PS[ub][:], AT[:, kc, c * 128:(c + 1) * 128], wu_v[:, kc % 4, :], kc == 0, kc == 7,
                       [wb_, b_at[kc]], [b_ps[ub]])
                cp(evac_eng(), U6[:, c, :], PS[ub][:], [b_ps[ub]], [b_u[c]])

            qproj(0)
            qproj(1)
            rope_pre(0, QG, 0)
            uproj(0)
            for jq in range(4):
                if jq + 1 < 4:
                    rope_pre(jq + 1, QG, (jq + 1) % 2)
                rb = (abank(), abank())
                rope_mm(rb, jq % 2)
                if jq + 1 < 4:
                    uproj(jq + 1)
                if jq + 2 < 4:
                    qproj(jq + 2)
                rope_post(rb, QT[:, jq, :], [b_qt[jq]], k=jq % 2)
            uproj(4)
            uproj(5)
            ring_release(4)
            bandv = BAND[:].rearrange("p (t g n) -> p t g n", t=5, g=4)
            for g in range(4):
                db = wbank()
                for ci in range(1, 5):
                    gi = it * 4 + (ci - 1)
                    curt = 2 if gi == 0 else (4 if gi == NOC - 1 else 3)
                    o = PS[db][:, (ci - 1) * 128:ci * 128]
                    mm(o, U6[:, ci - 1, g * 128:(g + 1) * 128], bandv[:, 0, g, :], True, False,
                       [b_u[ci - 1], b_band], [b_ps[db]])
                    mm(o, U6[:, ci, g * 128:(g + 1) * 128], bandv[:, curt, g, :], False, False,
                       [b_u[ci], b_band], [b_ps[db]])
                    mm(o, U6[:, ci + 1, g * 128:(g + 1) * 128], bandv[:, 1, g, :], False, True,
                       [b_u[ci + 1], b_band], [b_ps[db]])
                cp(evac_eng(), DT[:, g, :], PS[db][:], [b_ps[db]], b_dt(g))
                ob = wbank()
                mm(PS[ob][:], WPOOL[:, g, :], DT[:, g, :], True, True, [b_cb] + b_dt(g), [b_ps[ob]])
                ts("dve", MIX[:, 4 + g, :], PS[ob][:], CF[:, PSC + g:PSC + g + 1], ALU.mult,
                   [b_ps[ob], b_cf], [b_mix[4 + g]])

        def attention(it):
            for c in range(4):
                oa, ob_ = 4 + 2 * (c % 2), 5 + 2 * (c % 2)

                def qk(kc):
                    sa, sbk = 2 * (kc % 2), 2 * (kc % 2) + 1
                    mm(PS[sa][:], KT[0:64, kc * 128:(kc + 1) * 128], QT[0:64, c, :], True, True,
                       [b_kt[kc // 4], b_qt[c]], [b_ps[sa]])
                    mm(PS[sbk][:], KT[64:128, kc * 128:(kc + 1) * 128], QT[64:128, c, :], True, True,
                       [b_kt[kc // 4], b_qt[c]], [b_ps[sbk]])

                qk(0)
                for kc in range(NKC):
                    if kc + 1 < NKC:
                        qk(kc + 1)
                    sa, sbk = 2 * (kc % 2), 2 * (kc % 2) + 1
                    act(PTB[:, sa, :], PS[sa][:], AF.Exp, [b_ps[sa]], [b_pt[sa]], scale=0.125)
                    act(PTB[:, sbk, :], PS[sbk][:], AF.Exp, [b_ps[sbk]], [b_pt[sbk]], scale=0.125)
                    mm(PS[oa][:], VX[:, kc, 0:128], PTB[:, sa, :], kc == 0, kc == NKC - 1,
                       [b_vx[kc // 4], b_vones, b_pt[sa]], [b_ps[oa]])
                    mm(PS[ob_][:], VX[:, kc, 64:192], PTB[:, sbk, :], kc == 0, kc == NKC - 1,
                       [b_vx[kc // 4], b_vones, b_pt[sbk]], [b_ps[ob_]])
                P.add("dve", lambda e, oa=oa: e.reciprocal(out=REC[0:64, :], in_=PS[oa][64:128, :]),
                      [b_ps[oa]], [b_rec])
                tt("dve", MIX[0:64, c, :], PS[oa][0:64, :], REC[0:64, :], ALU.mult, [b_ps[oa], b_rec], [b_mix[c]])
                P.add("dve", lambda e, ob_=ob_: e.reciprocal(out=REC[64:128, :], in_=PS[ob_][0:64, :]),
                      [b_ps[ob_]], [b_rec])
                tt("dve", MIX[64:128, c, :], PS[ob_][64:128, :], REC[64:128, :], ALU.mult,
                   [b_ps[ob_], b_rec], [b_mix[c]])

        def mid(it, nxt):
            par = tile_par[it]
            X = XS[par]
            bx = b_xs[par]
            wo = [ring_get() for _ in range(4)]
            for t in range(4):
                for hf in range(2):
                    ab = 4 + (2 * t + hf) % 4
                    for kc in range(8):
                        wv_, wb_ = wo[kc // 2]
                        wo_v = wv_.rearrange("p (k n) -> p k n", k=2)
                        mm(PS[ab][:], MIX[:, kc, t * 128:(t + 1) * 128], wo_v[:, kc % 2, hf * 512:(hf + 1) * 512],
                           kc == 0, kc == 7, [wb_, b_mix[kc]], [b_ps[ab]])
                    xs = X[:, t, hf * 512:(hf + 1) * 512]
                    tt("dve", xs, xs, PS[ab][:], ALU.add, [bx[t], b_ps[ab]], [bx[t]])
            ring_release(4)
            n_stats(b_srcs4(it), 0)
            n_apr(b_srcs4(it), 0, 0)
            n_tr(4, 0, GL)
            for r in range(16):
                wv_, wb_ = ring_get()
                wu_v = wv_.rearrange("p (f k m) -> p f k m", f=2, k=8)
                for ff in range(2):
                    fc = 2 * r + ff
                    ub = wbank()
                    for kc in range(8):
                        mm(PS[ub][:], wu_v[:, ff, kc, :], AT[:, kc, 0:512], kc == 0, kc == 7,
                           [wb_, b_at[kc]], [b_ps[ub]])
                    rs = fc % 2
                    act(RELU[:, rs, :], PS[ub][:], AF.Relu, [b_ps[ub]], [b_relu[rs]])
                    tt("dve" if fc % 4 else "pool", H[:, fc, :], RELU[:, rs, :], RELU[:, rs, :], ALU.mult,
                       [b_relu[rs]], [b_h[fc]])
                ring_release(1)
            if nxt is not None:
                n_stats(b_srcs6(nxt), 1)
            for hf in range(2):
                for r in range(8):
                    wv_, wb_ = ring_get()
                    wd_v = wv_.rearrange("p (f n) -> p f n", f=4)
                    for ff in range(4):
                        fc = 4 * r + ff
                        for t in range(4):
                            mm(PS[4 + t][:], H[:, fc, t * 128:(t + 1) * 128], wd_v[:, ff, :], fc == 0, fc == 31,
                               [wb_, b_h[fc]], [b_ps[4 + t]])
                    ring_release(1)
                for t in range(4):
                    xs = X[:, t, hf * 512:(hf + 1) * 512]
                    tt("dve", xs, xs, PS[4 + t][:], ALU.add, [bx[t], b_ps[4 + t]], [bx[t]])
            n_stats(b_srcs4(it), 0)
            n_apr(b_srcs4(it), 0, 12)
            if nxt is not None:
                n_apr(b_srcs6(nxt), 1, 0)
            n_tr(4, 12, GP)
            for kc in range(2):
                bk = wbank()
                pbf = PS[bk][:].bitcast(BF16)
                for t in range(4):
                    tr(pbf[:, t * 128:(t + 1) * 128], PBF[:, t, kc * 128:(kc + 1) * 128], [b_pbf], [b_ps[bk]])
                cp(evac_eng(), PTT[:, kc, :], pbf[:, 0:512], [b_ps[bk]], [b_ptt])
            wg = [ring_get() for _ in range(4)]
            wp_v, wp_b = ring_get()
            wp_v = wp_v.rearrange("p (k n) -> p k n", k=2)
            for t in range(4):
                for hf in range(2):
                    gb = 4 + (2 * t + hf) % 2
                    pb = 6 + (2 * t + hf) % 2
                    for kc in range(8):
                        wv_, wb_ = wg[kc // 2]
                        wg_v = wv_.rearrange("p (k n) -> p k n", k=2)
                        mm(PS[gb][:], AT[:, kc, t * 128:(t + 1) * 128], wg_v[:, kc % 2, hf * 512:(hf + 1) * 512],
                           kc == 0, kc == 7, [wb_, b_at[kc]], [b_ps[gb]])
                    for kc in range(2):
                        mm(PS[pb][:], PTT[:, kc, t * 128:(t + 1) * 128], wp_v[:, kc, hf * 512:(hf + 1) * 512],
                           kc == 0, kc == 1, [wp_b, b_ptt], [b_ps[pb]])
                    k2 = (2 * t + hf) % 2
                    G = GT[:, k2 * 512:(k2 + 1) * 512]
                    gbufs = b_gt[2 * k2:2 * k2 + 2]
                    act(G, PS[gb][:], AF.Sigmoid, [b_ps[gb]], gbufs)
                    tt("dve", G, G, PS[pb][:], ALU.mult, gbufs + [b_ps[pb]], gbufs)
                    xs = X[:, t, hf * 512:(hf + 1) * 512]
                    tt("dve", xs, xs, G, ALU.add, [bx[t]] + gbufs, [bx[t]])
            ring_release(5)

        def final(it):
            par = tile_par[it]
            n_stats(b_srcs4(it), 0)
            for t in range(4):
                xs = XS[par][:, t, :]
                ys = YS[:, t, :]
                P.add("dve", lambda e, xs=xs, ys=ys, t=t: e.scalar_tensor_tensor(
                    out=ys, in0=xs, scalar=ST[:, 16 + t:17 + t], in1=FNG[:], op0=ALU.mult, op1=ALU.mult),
                    [b_xs[par][t], b_st[0], b_fng], b_h[16 + 4 * t:20 + 4 * t])
            dst = yout[j][it * 512:(it + 1) * 512, :].rearrange("(c p) d -> p c d", p=128)
            st = dma("sp", dst, YS, "y", reads=b_h[16:32], writes=[b_y[j]])
            last_store.append(st)

        b_load(0)
        ld_cs(0)
        n_stats(b_srcs6(0), 1)
        n_apr(b_srcs6(0), 1, 0)
        head(0)
        for it in range(NT):
            nxt = it + 1 if it + 1 < NT else None
            if nxt is not None:
                b_load(nxt)
                ld_cs(nxt)
            attention(it)
            mid(it, nxt)
            if nxt is not None:
                head(nxt)
            final(it)

    assert ring_state["got"] == ring_total and ring_state["issued"] == ring_total, ring_state
    P.emit(final_waits=last_store[-1:])
    return nc


def _rope_tables(T):
    t = np.arange(T)
    row = (t // 64).astype(np.float32)
    col = (t % 64).astype(np.float32)
    inv = (np.float32(10000.0) ** (-np.arange(16, dtype=np.float32) / np.float32(16))).astype(np.float32)
    ar = (row[:, None] * inv[None, :]).astype(np.float32)
    ac = (col[:, None] * inv[None, :]).astype(np.float32)
    ang = np.concatenate([ar, ar, ac, ac], axis=1)
    cos = np.cos(ang).astype(np.float32).T
    sin = np.sin(ang).astype(np.float32).T
    return np.ascontiguousarray(np.tile(cos, (2, 1))), np.ascontiguousarray(np.tile(sin, (2, 1)))


def _band_block(T, w, src0, dst0):
    half = w // 2
    out = np.zeros((128, 128), np.float32)
    for jj in range(128):
        tp = dst0 + jj
        if tp < 0 or tp >= T:
            continue
        lo = max(tp - half, 0)
        hi = min(tp + half, T)
        cnt = hi - lo
        for t in range(lo, hi):
            i = t - src0
            if 0 <= i < 128:
                out[i, jj] += 1.0 / cnt
        i = tp - src0
        if 0 <= i < 128:
            out[i, jj] -= 1.0
    return out


def _band_mats(T, OWN, half):
    wins = (2, 4, 8, 16)
    g0 = half * OWN
    BIG = 1 << 20
    mid0 = 4096
    res = np.zeros((128, 5, 4, 128), np.float32)
    for g, w in enumerate(wins):
        res[:, 0, g] = _band_block(BIG, w, mid0 - 128, mid0)
        res[:, 1, g] = _band_block(BIG, w, mid0 + 128, mid0)
        res[:, 2, g] = _band_block(T, w, g0, g0)
        res[:, 3, g] = _band_block(BIG, w, mid0, mid0)
        l0 = g0 + OWN - 128
        res[:, 4, g] = _band_block(T, w, l0, l0)
    return np.ascontiguousarray(res.reshape(128, 2560))


def _weight_chunks(w_in, w_out, w_up, w_down, w_gate, w_proj):
    ch = np.zeros((NCHUNK, 128, 2048), np.float32)
    qcols = np.zeros((4, 128), np.int64)
    for jq in range(4):
        qcols[jq, :64] = jq * 64 + np.arange(64)
        qcols[jq, 64:] = (jq + 4) * 64 + np.arange(64)
    wr = w_in.reshape(8, 128, 1280)
    n = 0
    for r in range(2):
        blk = np.stack([wr[:, :, qcols[2 * r + f]] for f in range(2)], axis=0)
        ch[n] = blk.transpose(2, 0, 1, 3).reshape(128, 2048)
        n += 1
    for r in range(2):
        blk = wr[4 * r:4 * r + 4, :, 768:1280]
        ch[n] = blk.transpose(1, 0, 2).reshape(128, 2048)
        n += 1
    rows = np.zeros(1024, np.int64)
    for c in range(4):
        rows[c * 128:c * 128 + 64] = c * 64 + np.arange(64)
        rows[c * 128 + 64:(c + 1) * 128] = (c + 4) * 64 + np.arange(64)
    rows[512:] = np.arange(512, 1024)
    wo = w_out[rows, :].reshape(8, 128, 1024)
    for r in range(4):
        ch[n] = wo[2 * r:2 * r + 2].transpose(1, 0, 2).reshape(128, 2048)
        n += 1
    wu = w_up.reshape(8, 128, 32, 128)
    for r in range(16):
        blk = wu[:, :, 2 * r:2 * r + 2, :]
        ch[n] = blk.transpose(1, 2, 0, 3).reshape(128, 2048)
        n += 1
    wd = w_down.reshape(32, 128, 2, 512)
    for hf in range(2):
        for r in range(8):
            blk = wd[4 * r:4 * r + 4, :, hf, :]
            ch[n] = blk.transpose(1, 0, 2).reshape(128, 2048)
            n += 1
    wg = w_gate.reshape(8, 128, 1024)
    for r in range(4):
        ch[n] = wg[2 * r:2 * r + 2].transpose(1, 0, 2).reshape(128, 2048)
        n += 1
    ch[n] = w_proj.reshape(2, 128, 1024).transpose(1, 0, 2).reshape(128, 2048)
    n += 1
    assert n == NCHUNK
    wkv = np.zeros((128, 2048), np.float32)
    wkv[:, 0:1024] = wr[:, :, 512:640].transpose(1, 0, 2).reshape(128, 1024)
    wkv[:, 1024:2048] = wr[:, :, 640:768].transpose(1, 0, 2).reshape(128, 1024)
    return ch, wkv


def _consts(norm_mix_g, norm_mlp_g, norm_ple_g, q_norm_g, k_norm_g, pool_scale, w_pool, final_norm_g):
    cb = np.zeros((128, 896), np.float32)
    cb[:, 0:128] = np.eye(128, dtype=np.float32)
    rt = np.zeros((128, 128), np.float32)
    for m in range(128):
        if m % 32 < 16:
            rt[m + 16, m] = -1.0
        else:
            rt[m - 16, m] = 1.0
    cb[:, 128:256] = rt
    blk = np.zeros((128, 128), np.float32)
    blk[:64, :64] = 1.0
    blk[64:, 64:] = 1.0
    cb[:, 256:384] = blk
    cb[:, 384:896] = w_pool.transpose(1, 0, 2).reshape(128, 512)
    cf = np.zeros((128, 32), np.float32)
    cf[:, 0:8] = norm_mix_g.reshape(8, 128).T
    cf[:, 8:16] = norm_mlp_g.reshape(8, 128).T
    cf[:, 16:24] = norm_ple_g.reshape(8, 128).T
    cf[:, 24] = np.tile(q_norm_g, 2)
    cf[:, 25] = np.tile(k_norm_g, 2)
    cf[:, 26:30] = pool_scale.reshape(4, 128).T
    fng = np.ascontiguousarray(np.broadcast_to(final_norm_g[None, :], (128, 1024))).astype(np.float32)
    return cb, cf, fng


def make_in_maps(jobs_data, weights):
    (norm_mix_g, w_in, q_norm_g, k_norm_g, w_pool, pool_scale, w_out, norm_mlp_g, w_up, w_down,
     norm_ple_g, w_ple_gate, w_ple_proj, final_norm_g) = weights
    ch, wkv = _weight_chunks(w_in, w_out, w_up, w_down, w_ple_gate, w_ple_proj)
    cb, cf, fng = _consts(norm_mix_g, norm_mlp_g, norm_ple_g, q_norm_g, k_norm_g, pool_scale, w_pool, final_norm_g)
    in_maps = []
    tabs = {}
    for core_jobs in jobs_data:
        m = {"wts": ch, "wkv": wkv, "cstb": cb, "cstf": cf, "fng": fng}
        for j, jd in enumerate(core_jobs):
            x, p, half = jd["x"], jd["p"], jd["half"]
            T = x.shape[0]
            OWN = T // 2
            o0 = half * OWN
            own = slice(o0, o0 + OWN)
            oth = slice((1 - half) * OWN, (1 - half) * OWN + OWN)
            m[f"xkeys{j}"] = np.ascontiguousarray(np.concatenate([x[own], x[oth]], axis=0))
            xp = np.zeros((OWN + 256, 1024), np.float32)
            xp[128:128 + OWN] = x[own]
            if o0 >= 128:
                xp[0:128] = x[o0 - 128:o0]
            if o0 + OWN + 128 <= T:
                xp[128 + OWN:] = x[o0 + OWN:o0 + OWN + 128]
            m[f"xpad{j}"] = xp
            m[f"p{j}"] = np.ascontiguousarray(p[own])
            if T not in tabs:
                tabs[T] = _rope_tables(T)
            cos, sin = tabs[T]
            m[f"cos{j}"] = np.ascontiguousarray(np.concatenate([cos[:, own], cos[:, oth]], axis=1))
            m[f"sin{j}"] = np.ascontiguousarray(np.concatenate([sin[:, own], sin[:, oth]], axis=1))
            m[f"band{j}"] = _band_mats(T, OWN, half)
        in_maps.append(m)
    return in_maps


_NC_CACHE = {}


def run(x_prompt, x_sample, p_prompt, p_sample, weights, n_cores=8):
    Ts, Tp = x_sample.shape[1], x_prompt.shape[1]
    jobs = [(Ts, Ts // 2), (Tp, Tp // 2)]
    key = tuple(jobs)
    if key not in _NC_CACHE:
        _NC_CACHE[key] = build_program(jobs)
    nc = _NC_CACHE[key]
    jobs_data = []
    for c in range(n_cores):
        s, half = c // 2, c % 2
        jobs_data.append([
            {"x": x_sample[s], "p": p_sample[s], "half": half},
            {"x": x_prompt[s], "p": p_prompt[s], "half": half},
        ])
    in_maps = make_in_maps(jobs_data, weights)
    res = run_bass_kernel_spmd(nc, in_maps, core_ids=list(range(n_cores)))
    y_s = np.zeros_like(x_sample)
    y_p = np.zeros_like(x_prompt)
    for c in range(n_cores):
        s, half = c // 2, c % 2
        r = res.results[c]
        y_s[s, half * (Ts // 2):(half + 1) * (Ts // 2)] = r["y0"]
        y_p[s, half * (Tp // 2):(half + 1) * (Tp // 2)] = r["y1"]
    return y_p, y_s


def kernel(x_prompt, x_sample, p_prompt, p_sample, norm_mix_g, w_in, q_norm_g, k_norm_g, w_pool, pool_scale,
           w_out, norm_mlp_g, w_up, w_down, norm_ple_g, w_ple_gate, w_ple_proj, final_norm_g):
    f = lambda a: np.asarray(a, dtype=np.float32)
    weights = (f(norm_mix_g)[0], f(w_in)[0], f(q_norm_g)[0], f(k_norm_g)[0], f(w_pool)[0], f(pool_scale)[0],
               f(w_out)[0], f(norm_mlp_g)[0], f(w_up)[0], f(w_down)[0], f(norm_ple_g)[0], f(w_ple_gate)[0],
               f(w_ple_proj)[0], f(final_norm_g))
    y_p, y_s = run(f(x_prompt), f(x_sample), f(p_prompt)[0], f(p_sample)[0], weights)
    return (y_p, y_s)
```

```python
import numpy as np
import concourse.bass as bass
import concourse.mybir as mybir
from concourse.bass_utils import run_bass_kernel_spmd

F32 = mybir.dt.float32
BF16 = mybir.dt.bfloat16
ALU = mybir.AluOpType
AF = mybir.ActivationFunctionType

ENGS = ("pe", "act", "dve", "pool", "sp")
EPS = 1e-6
NRING = 6
NCHUNK = 45


class Buf:
    __slots__ = ("name", "writer", "readers")

    def __init__(self, name):
        self.name = name
        self.writer = None
        self.readers = []


class Op:
    __slots__ = ("eng", "fn", "deps", "signal", "count", "dma_key")

    def __init__(self, eng, fn, dma_key=None):
        self.eng = eng
        self.fn = fn
        self.deps = []
        self.signal = False
        self.count = None
        self.dma_key = dma_key


class Prog:
    def __init__(self, nc):
        self.nc = nc
        self.streams = {e: [] for e in ENGS}
        self.dma_keys = {}

    def add(self, eng, fn, reads=(), writes=(), dma_key=None, extra_deps=()):
        op = Op(eng, fn, dma_key)
        deps = []
        for b in reads:
            if b.writer is not None:
                deps.append((b.writer, "raw"))
        for b in writes:
            if b.writer is not None:
                deps.append((b.writer, "waw"))
            for r in b.readers:
                deps.append((r, "war"))
        for d in extra_deps:
            deps.append((d, "raw"))
        if dma_key is not None:
            prev = self.dma_keys.get(dma_key)
            if prev is not None:
                deps.append((prev, "raw"))
        seen = set()
        for d, kind in deps:
            if d is op or id(d) in seen:
                continue
            if d.dma_key is None and op.dma_key is None and d.eng == eng:
                if eng == "pe" or kind != "raw":
                    continue
            seen.add(id(d))
            op.deps.append(d)
            d.signal = True
        for b in reads:
            b.readers.append(op)
        for b in writes:
            b.writer = op
            b.readers = []
        self.streams[eng].append(op)
        if dma_key is not None:
            self.dma_keys[dma_key] = op
        return op

    def emit(self, final_waits=()):
        nc = self.nc
        sems = {e: nc.alloc_semaphore(name=f"s_{e}") for e in ENGS}
        ksems = {k: nc.alloc_semaphore(name=f"k_{k}") for k in self.dma_keys}
        kcnt = {k: 0 for k in self.dma_keys}
        for e in ENGS:
            c = 0
            for op in self.streams[e]:
                if op.dma_key is not None:
                    kcnt[op.dma_key] += 16
                    op.count = kcnt[op.dma_key]
                elif op.signal:
                    c += 1
                    op.count = c
        engobj = {"pe": "tensor", "act": "scalar", "dve": "vector", "pool": "gpsimd", "sp": "sync"}
        with nc.Block() as block:
            for e in ENGS:
                ops = self.streams[e]
                fw = list(final_waits) if e == "sp" else []

                def body(eng, ops=ops, e=e, fw=fw):
                    waited = {}

                    def do_wait(d):
                        key = ("k", d.dma_key) if d.dma_key is not None else ("e", d.eng)
                        if waited.get(key, 0) >= d.count:
                            return
                        waited[key] = d.count
                        sem = ksems[d.dma_key] if d.dma_key is not None else sems[d.eng]
                        eng.wait_ge(sem, d.count)

                    for op in ops:
                        for d in op.deps:
                            do_wait(d)
                        inst = op.fn(eng)
                        if op.dma_key is not None:
                            inst.then_inc(ksems[op.dma_key], 16)
                        elif op.signal:
                            inst.then_inc(sems[e], 1)
                    for d in fw:
                        do_wait(d)

                getattr(block, engobj[e])(body)


def build_program(jobs):
    nc = bass.Bass("TRN2", target_bir_lowering=False)
    P = Prog(nc)
    TMAX = max(T for T, _ in jobs)
    ntiles_total = sum(O // 512 for _, O in jobs)

    def din(name, shape):
        return nc.dram_tensor(name, list(shape), F32, kind="ExternalInput").ap()

    xkeys = [din(f"xkeys{j}", (T, 1024)) for j, (T, O) in enumerate(jobs)]
    xpad = [din(f"xpad{j}", (O + 256, 1024)) for j, (T, O) in enumerate(jobs)]
    pin = [din(f"p{j}", (O, 256)) for j, (T, O) in enumerate(jobs)]
    cosd = [din(f"cos{j}", (128, T)) for j, (T, O) in enumerate(jobs)]
    sind = [din(f"sin{j}", (128, T)) for j, (T, O) in enumerate(jobs)]
    bandd = [din(f"band{j}", (128, 2560)) for j, (T, O) in enumerate(jobs)]
    yout = [nc.dram_tensor(f"y{j}", [O, 1024], F32, kind="ExternalOutput").ap() for j, (T, O) in enumerate(jobs)]
    wts = din("wts", (NCHUNK, 128, 2048))
    wkvd = din("wkv", (128, 2048))
    cstbd = din("cstb", (128, 896))
    cstfd = din("cstf", (128, 32))
    fngd = din("fng", (128, 1024))
    wsc = nc.dram_tensor("wsc", [NCHUNK, 128, 2048], BF16, kind="Internal").ap()

    sb = nc.alloc_sbuf_tensor
    KT = sb("KT", [128, TMAX], BF16)
    VX = sb("VX", [128, TMAX // 128, 192], BF16)
    XS = [sb("XA", [128, 4, 1024], F32), sb("XB", [128, 4, 1024], F32)]
    XH = sb("XH", [128, 2, 1024], F32)
    AT = sb("AT", [128, 8, 768], BF16)
    QT = sb("QT", [128, 4, 512], BF16)
    U6 = sb("U6", [128, 6, 512], BF16)
    MIX = sb("MIX", [128, 8, 512], BF16)
    H = sb("H", [128, 32, 512], BF16)
    RING = sb("RING", [128, NRING, 2048], BF16)
    WKV = sb("WKV", [128, 2048], BF16)
    PTB = sb("PTB", [128, 4, 512], BF16)
    CB = sb("CB", [128, 896], BF16)
    BAND = sb("BAND", [128, 2560], BF16)
    FNG = sb("FNG", [128, 1024], F32)
    CF = sb("CF", [128, 32], F32)
    PBF = sb("PBF", [128, 4, 256], BF16)
    PTT = sb("PTT", [128, 2, 512], BF16)
    TMPALL = sb("TMPALL", [128, 5, 512], F32)
    Y32 = TMPALL[:, 0, :]
    RSTD = TMPALL[:, 1, :]
    T12 = TMPALL[:, 2:4, :]
    T1 = TMPALL[:, 2, :]
    T2 = TMPALL[:, 3, :]
    RELU = T12
    REC = TMPALL[:, 4, :]
    SQYB = sb("SQYB", [128, 2, 512], BF16)
    SQ = SQYB[:, 0, :]
    YB = SQYB[:, 1, :]
    JUNKS = [TMPALL[:, i, :].bitcast(BF16) for i in (1, 2, 3, 4, 0)] + [SQYB[:].rearrange("p a n -> p (a n)")]
    ST = sb("ST", [128, 48], F32)
    CS = sb("CSB", [128, 1024], F32)[:]
    PSALL = nc.alloc_psum_tensor("PSALL", [128, 4096], F32)
    PS = [PSALL[:, b * 512:(b + 1) * 512] for b in range(8)]

    def hview(hbase, i):
        return H[:, hbase + 2 * i:hbase + 2 * i + 2, :].rearrange("p a n -> p (a n)")

    DT = H[:, 12:16, :]
    YS = H[:, 16:32, :].rearrange("p a n -> p (a n)").bitcast(F32).rearrange("p (c d) -> p c d", c=4)
    GT = H[:, 20:28, :].rearrange("p a n -> p (a n)").bitcast(F32)
    IDENT = CB[:, 0:128]
    RT = CB[:, 128:256]
    BLK = CB[:, 256:384]
    WPOOL = CB[:, 384:896].rearrange("p (g d) -> p g d", g=4)

    b_kt = [Buf(f"kt{i}") for i in range(TMAX // 512)]
    b_vx = [Buf(f"vx{i}") for i in range(TMAX // 512)]
    b_xs = [[Buf(f"xa{i}") for i in range(4)], [Buf(f"xb{i}") for i in range(4)]]
    b_xh = [Buf("xh0"), Buf("xh1")]
    b_at = [Buf(f"at{i}") for i in range(8)]
    b_qt = [Buf(f"qt{i}") for i in range(4)]
    b_u = [Buf(f"u{i}") for i in range(6)]
    b_mix = [Buf(f"mix{i}") for i in range(8)]
    b_h = [Buf(f"h{i}") for i in range(32)]
    b_ring = [Buf(f"ring{i}") for i in range(NRING)]
    b_wsc = [Buf(f"wsc{i}") for i in range(NCHUNK)]
    b_wkv, b_cb, b_band, b_fng, b_cf = Buf("wkv"), Buf("cb"), Buf("band"), Buf("fng"), Buf("cf")
    b_pt = [Buf(f"pt{i}") for i in range(4)]
    b_pbf, b_ptt = Buf("pbf"), Buf("ptt")
    b_y32, b_sq, b_yb, b_rstd, b_t1, b_t2 = (Buf(n) for n in ("y32", "sq", "yb", "rstd", "t1", "t2"))
    b_rec, b_vones = Buf("rec"), Buf("vones")
    b_junks = [[b_rstd], [b_t1], [b_t2], [b_rec], [b_y32], [b_sq, b_yb]]
    b_st = [Buf("st0"), Buf("st1")]
    b_relu = [b_t1, b_t2]
    b_ps = [Buf(f"ps{i}") for i in range(8)]
    b_y = [Buf(f"yout{j}") for j in range(len(jobs))]
    b_hc = lambda hbase, i: [b_h[hbase + 2 * i], b_h[hbase + 2 * i + 1]]
    b_dt = lambda g: [b_h[12 + g]]
    b_cs = [Buf("cs0"), Buf("cs1")]
    b_gt = b_h[20:28]

    def dma(q, out, in_, key, reads=(), writes=(), extra=()):
        return P.add(q, lambda e: e.dma_start(out=out, in_=in_), reads, writes, dma_key=key, extra_deps=extra)

    def mm(out, lhsT, rhs, start, stop, reads, writes):
        return P.add("pe", lambda e: e.matmul(out, lhsT=lhsT, rhs=rhs, start=start, stop=stop), reads, writes)

    def tr(out, in_, reads, writes):
        return P.add("pe", lambda e: e.transpose(out=out, in_=in_, identity=IDENT), list(reads) + [b_cb], writes)

    def act(out, in_, func, reads, writes, scale=1.0, bias=None, accum=None):
        def f(e):
            kw = {}
            if bias is not None:
                kw["bias"] = bias
            if accum is not None:
                kw["accum_out"] = accum
            return e.activation(out=out, in_=in_, func=func, scale=scale, **kw)
        return P.add("act", f, reads, writes)

    def ts(eng, out, in0, s1, op0, reads, writes):
        return P.add(eng, lambda e: e.tensor_scalar(out=out, in0=in0, scalar1=s1, scalar2=None, op0=op0),
                     reads, writes)

    def tt(eng, out, in0, in1, op, reads, writes):
        return P.add(eng, lambda e: e.tensor_tensor(out=out, in0=in0, in1=in1, op=op), reads, writes)

    def cp(eng, out, in_, reads, writes):
        if eng == "act":
            return act(out, in_, AF.Copy, reads, writes)
        return P.add(eng, lambda e: e.tensor_copy(out=out, in_=in_), reads, writes)

    wb_ctr = [0]

    def wbank():
        b = wb_ctr[0] % 4
        wb_ctr[0] += 1
        return b

    ab_ctr = [0]

    def abank():
        b = 4 + ab_ctr[0] % 4
        ab_ctr[0] += 1
        return b

    alt = [0]

    def evac_eng():
        alt[0] += 1
        return "act" if alt[0] % 2 else "dve"

    dma("pool", CB[:], cstbd, "c_cb", writes=[b_cb])
    dma("pool", WKV[:], wkvd, "c_wkv", writes=[b_wkv])
    dma("sp", CF[:], cstfd, "c_cf", writes=[b_cf])
    dma("sp", FNG[:], fngd, "c_fng", writes=[b_fng])
    P.add("dve", lambda e: e.memset(VX[:, :, 64:128], 1.0), writes=[b_vones])
    cast_state = {"n": 0, "last": None}

    NCA = 8

    def cast_some(cnt, limit):
        for _ in range(cnt):
            k = cast_state["n"]
            if k >= limit:
                return
            dma("pool", wsc[k], wts[k], f"cast{k % 4}", writes=[b_wsc[k]])
            cast_state["n"] = k + 1

    ring_state = {"issued": 0, "got": 0, "released": 0}
    ring_total = ntiles_total * NCHUNK

    def ring_issue():
        n = ring_state["issued"]
        if n >= ring_total:
            return
        slot = n % NRING
        k = n % NCHUNK
        if k >= NCA and cast_state["n"] < NCHUNK:
            cast_some(NCHUNK, NCHUNK)
        assert cast_state["n"] > k
        dma("sp", RING[:, slot, :], wsc[k], f"r{slot}", reads=[b_wsc[k]], writes=[b_ring[slot]])
        ring_state["issued"] = n + 1

    def ring_get():
        n = ring_state["got"]
        assert n < ring_state["issued"], "ring underflow"
        ring_state["got"] = n + 1
        slot = n % NRING
        return RING[:, slot, :], b_ring[slot]

    def ring_release(cnt=1):
        for _ in range(cnt):
            ring_state["released"] += 1
            ring_issue()

    GM, GL, GP, QG, KG, PSC = 0, 8, 16, 24, 25, 26

    _recb = REC.bitcast(BF16)
    TS = [
        dict(y32=Y32, rstd=RSTD, t1=T1, t2=T2, sq=SQ, yb=YB,
             b_y32=[b_y32], b_rstd=[b_rstd], b_t1=[b_t1], b_t2=[b_t2], b_sq=[b_sq], b_yb=[b_yb]),
        dict(y32=PTB[:, 0:2, :].rearrange("p a n -> p (a n)").bitcast(F32),
             rstd=PTB[:, 2:4, :].rearrange("p a n -> p (a n)").bitcast(F32),
             t1=MIX[:, 0:2, :].rearrange("p a n -> p (a n)").bitcast(F32),
             t2=MIX[:, 2:4, :].rearrange("p a n -> p (a n)").bitcast(F32),
             sq=_recb[:, 0:512], yb=_recb[:, 512:1024],
             b_y32=[b_pt[0], b_pt[1]], b_rstd=[b_pt[2], b_pt[3]], b_t1=[b_mix[0], b_mix[1]],
             b_t2=[b_mix[2], b_mix[3]], b_sq=[b_rec], b_yb=[b_rec]),
    ]

    def n_stats(srcs, s):
        n = len(srcs)
        o = 24 * s
        for i, (ap, bufs) in enumerate(srcs):
            act(JUNKS[i], ap, AF.Square, bufs, b_junks[i] + [b_st[s]], accum=ST[:, o + i:o + i + 1])
        act(ST[:, o + 8:o + 8 + n], ST[:, o:o + n], AF.Ln, [b_st[s]], [b_st[s]], scale=1.0 / 1024, bias=EPS)
        act(ST[:, o + 16:o + 16 + n], ST[:, o + 8:o + 8 + n], AF.Exp, [b_st[s]], [b_st[s]], scale=-0.5)

    ALT_V = [U6[:, 0:2, :], U6[:, 2:4, :], U6[:, 4:6, :], QT[:, 0:2, :], QT[:, 2:4, :], PTB[:, 0:2, :]]
    ALT_B = [[b_u[0], b_u[1]], [b_u[2], b_u[3]], [b_u[4], b_u[5]], [b_qt[0], b_qt[1]], [b_qt[2], b_qt[3]],
             [b_pt[0], b_pt[1]]]

    def aview(loc, i):
        if loc == "alt":
            return ALT_V[i].rearrange("p a n -> p (a n)")
        return hview(loc, i)

    def abufs(loc, i):
        return ALT_B[i] if loc == "alt" else b_hc(loc, i)

    def n_apr(srcs, s, hbase):
        o = 24 * s + 16
        for i, (ap, bufs) in enumerate(srcs):
            ts("dve", aview(hbase, i), ap, ST[:, o + i:o + i + 1], ALU.mult, list(bufs) + [b_st[s]], abufs(hbase, i))

    def n_tr(n, hbase, gcol, act_kcs=(0, 2, 4, 6)):
        for kc in range(8):
            bk = wbank()
            pbf = PS[bk][:].bitcast(BF16)
            for i in range(n):
                tr(pbf[:, i * 128:(i + 1) * 128], aview(hbase, i)[:, kc * 128:(kc + 1) * 128], abufs(hbase, i),
                   [b_ps[bk]])
            eng = "act" if kc in act_kcs else "dve"
            g = CF[:, gcol + kc:gcol + kc + 1]
            if eng == "act":
                act(AT[:, kc, 0:n * 128], pbf[:, 0:n * 128], AF.Copy, [b_ps[bk], b_cf], [b_at[kc]], scale=g)
            else:
                ts("dve", AT[:, kc, 0:n * 128], pbf[:, 0:n * 128], g, ALU.mult, [b_ps[bk], b_cf], [b_at[kc]])

    def rope_pre(zb, gcol, k=0):
        t = TS[k]
        Z = PS[zb][:]
        act(t["sq"], Z, AF.Square, [b_ps[zb]], t["b_sq"])
        act(t["y32"], Z, AF.Copy, [b_ps[zb], b_cf], t["b_y32"], scale=CF[:, gcol:gcol + 1])
        cp("dve", t["yb"], t["y32"], t["b_y32"], t["b_yb"])

    def rope_mm(banks, k=0):
        t = TS[k]
        sb_, rb = banks
        mm(PS[sb_][:], BLK, t["sq"], True, True, t["b_sq"] + [b_cb], [b_ps[sb_]])
        mm(PS[rb][:], RT, t["yb"], True, True, t["b_yb"] + [b_cb], [b_ps[rb]])

    def rope_post(banks, dst, dst_bufs, add_eng="dve", k=0):
        t = TS[k]
        sb_, rb = banks
        act(t["rstd"], PS[sb_][:], AF.Ln, [b_ps[sb_]], t["b_rstd"], scale=1.0 / 64, bias=EPS)
        act(t["rstd"], t["rstd"], AF.Exp, t["b_rstd"], t["b_rstd"], scale=-0.5)
        tt(add_eng, t["t1"], t["y32"], CS[:, 0:512], ALU.mult, t["b_y32"] + [b_cs[0]], t["b_t1"])
        tt("dve", t["t2"], PS[rb][:], CS[:, 512:1024], ALU.mult, [b_ps[rb], b_cs[1]], t["b_t2"])
        tt(add_eng, t["t1"], t["t1"], t["t2"], ALU.add, t["b_t1"] + t["b_t2"], t["b_t1"])
        tt("pool", dst, t["t1"], t["rstd"], ALU.mult, t["b_t1"] + t["b_rstd"], dst_bufs)

    xctr = [0]
    last_store = []
    for j, (T, OWN) in enumerate(jobs):
        NKT = T // 512
        NKC = T // 128
        NT = OWN // 512
        NOC = OWN // 128
        dma("pool", BAND[:, 0:1280], bandd[j][:, 0:1280], "c_band0", writes=[b_band])
        dma("pool", BAND[:, 1280:2560], bandd[j][:, 1280:2560], "c_band1", writes=[b_band])
        cast_per = -(-NCA // NKT)

        XA3 = [XS[0][:], XS[1][:], YS]
        BA3 = [[[bb] for bb in b_xs[0]], [[bb] for bb in b_xs[1]], [b_h[16 + 4 * c:20 + 4 * c] for c in range(4)]]

        def a_load(kt):
            r = kt % 3
            src = xkeys[j][kt * 512:(kt + 1) * 512, :].rearrange("(c p) d -> p c d", p=128)
            dma("sp", XA3[r], src, f"x{r}", writes=[bb for grp in BA3[r] for bb in grp])

        def a_cs(kt):
            dma("sp", CS[:, 0:512], cosd[j][:, kt * 512:(kt + 1) * 512], "cs0", writes=[b_cs[0]])
            dma("sp", CS[:, 512:1024], sind[j][:, kt * 512:(kt + 1) * 512], "cs1", writes=[b_cs[1]])

        def a_srcs(kt):
            r = kt % 3
            return [(XA3[r][:, c, :], BA3[r][c]) for c in range(4)]

        a_load(0)
        a_cs(0)
        if NKT > 1:
            a_load(1)
        n_stats(a_srcs(0), 1)
        n_apr(a_srcs(0), 1, 0)
        for kt in range(NKT):
            sset = (kt + 1) % 2
            n_tr(4, 0, GM, act_kcs=(7,))
            if j == 0:
                cast_some(cast_per, NCA)
            if kt + 2 < NKT:
                a_load(kt + 2)
            if kt + 1 < NKT:
                n_stats(a_srcs(kt + 1), 1 - sset)
                n_apr(a_srcs(kt + 1), 1 - sset, 0)
            zb = wbank()
            for kc in range(8):
                mm(PS[zb][:], WKV[:, kc * 128:(kc + 1) * 128], AT[:, kc, 0:512], kc == 0, kc == 7,
                   [b_wkv, b_at[kc]], [b_ps[zb]])
            rope_pre(zb, KG)
            vb = abank()
            for c in range(4):
                for kc in range(8):
                    mm(PS[vb][:, c * 128:(c + 1) * 128], AT[:, kc, c * 128:(c + 1) * 128],
                       WKV[:, 1024 + kc * 128:1024 + (kc + 1) * 128], kc == 0, kc == 7,
                       [b_wkv, b_at[kc]], [b_ps[vb]])
            rb = (abank(), abank())
            rope_mm(rb)
            rope_post(rb, KT[:, kt * 512:(kt + 1) * 512], [b_kt[kt]], add_eng="pool")
            if kt + 1 < NKT:
                a_cs(kt + 1)
            pv = PS[vb][:].rearrange("p (c n) -> p c n", c=4)
            cp("dve", VX[:, kt * 4:(kt + 1) * 4, 0:64], pv[:, :, 0:64], [b_ps[vb]], [b_vx[kt]])
            cp("act", VX[:, kt * 4:(kt + 1) * 4, 128:192], pv[:, :, 64:128], [b_ps[vb]], [b_vx[kt]])
        if j == 0:
            cast_some(NCA, NCA)
            for _ in range(NRING):
                ring_issue()

        tile_par = {}

        def b_load(it):
            par = xctr[0] % 2
            xctr[0] += 1
            tile_par[it] = par
            r0 = it * 512
            dma("sp", XH[:, 0, :], xpad[j][r0:r0 + 128, :], "xh0", writes=[b_xh[0]])
            src = xpad[j][r0 + 128:r0 + 640, :].rearrange("(c p) d -> p c d", p=128)
            dma("sp", XS[par][:], src, f"x{par}", writes=b_xs[par])
            dma("sp", XH[:, 1, :], xpad[j][r0 + 640:r0 + 768, :], "xh1", writes=[b_xh[1]])

        def ld_cs(it):
            dma("sp", CS[:, 0:512], cosd[j][:, it * 512:(it + 1) * 512], "cs0", writes=[b_cs[0]])
            dma("sp", CS[:, 512:1024], sind[j][:, it * 512:(it + 1) * 512], "cs1", writes=[b_cs[1]])

        def b_srcs6(it):
            par = tile_par[it]
            return ([(XH[:, 0, :], [b_xh[0]])] + [(XS[par][:, c, :], [b_xs[par][c]]) for c in range(4)]
                    + [(XH[:, 1, :], [b_xh[1]])])

        def b_srcs4(it):
            par = tile_par[it]
            return [(XS[par][:, c, :], [b_xs[par][c]]) for c in range(4)]

        def head(it, aloc=0):
            psrc = pin[j][it * 512:(it + 1) * 512, :].rearrange("(c p) d -> p c d", p=128)
            dma("pool", PBF[:], psrc, "p", writes=[b_pbf])
            n_tr(6, aloc, GM)
            wq = [ring_get(), ring_get()]
            wu = [ring_get(), ring_get()]

            def qproj(jq):
                wv_, wb_ = wq[jq // 2]
                wq_v = wv_.rearrange("p (f k m) -> p f k m", f=2, k=8)
                for kc in range(8):
                    mm(PS[jq][:], wq_v[:, jq % 2, kc, :], AT[:, kc, 128:640], kc == 0, kc == 7,
                       [wb_, b_at[kc]], [b_ps[jq]])

            def uproj(c):
                ub = abank()
                for kc in range(8):
                    wv_, wb_ = wu[kc // 4]
                    wu_v = wv_.rearrange("p (k n) -> p k n", k=4)
                    mm(PS[ub][:], AT[:, kc, c * 128:(c + 1) * 128], wu_v[:, kc % 4, :], kc == 0, kc == 7,
                       [wb_, b_at[kc]], [b_ps[ub]])
                cp(evac_eng(), U6[:, c, :], PS[ub][:], [b_ps[ub]], [b_u[c]])

            qproj(0)
            qproj(1)
            rope_pre(0, QG, 0)
            uproj(0)
            for jq in range(4):
                if jq + 1 < 4:
                    rope_pre(jq + 1, QG, (jq + 1) % 2)
                rb = (abank(), abank())
                rope_mm(rb, jq % 2)
                if jq + 1 < 4:
                    uproj(jq + 1)
                if jq + 2 < 4:
                    qproj(jq + 2)
                rope_post(rb, QT[:, jq, :], [b_qt[jq]], k=jq % 2)
            uproj(4)
            uproj(5)
            ring_release(4)
            bandv = BAND[:].rearrange("p (t g n) -> p t g n", t=5, g=4)
            for g in range(4):
                db = wbank()
                for ci in range(1, 5):
                    gi = it * 4 + (ci - 1)
                    curt = 2 if gi == 0 else (4 if gi == NOC - 1 else 3)
                    o = PS[db][:, (ci - 1) * 128:ci * 128]
                    mm(o, U6[:, ci - 1, g * 128:(g + 1) * 128], bandv[:, 0, g, :], True, False,
                       [b_u[ci - 1], b_band], [b_ps[db]])
                    mm(o, U6[:, ci, g * 128:(g + 1) * 128], bandv[:, curt, g, :], False, False,
                       [b_u[ci], b_band], [b_ps[db]])
                    mm(o, U6[:, ci + 1, g * 128:(g + 1) * 128], bandv[:, 1, g, :], False, True,
                       [b_u[ci + 1], b_band], [b_ps[db]])
                cp(evac_eng(), DT[:, g, :], PS[db][:], [b_ps[db]], b_dt(g))
                ob = wbank()
                mm(PS[ob][:], WPOOL[:, g, :], DT[:, g, :], True, True, [b_cb] + b_dt(g), [b_ps[ob]])
                ts("dve", MIX[:, 4 + g, :], PS[ob][:], CF[:, PSC + g:PSC + g + 1], ALU.mult,
                   [b_ps[ob], b_cf], [b_mix[4 + g]])

        def attention(it):
            for c in range(4):
                oa, ob_ = 4 + 2 * (c % 2), 5 + 2 * (c % 2)

                def qk(kc):
                    sa, sbk = 2 * (kc % 2), 2 * (kc % 2) + 1
                    mm(PS[sa][:], KT[0:64, kc * 128:(kc + 1) * 128], QT[0:64, c, :], True, True,
                       [b_kt[kc // 4], b_qt[c]], [b_ps[sa]])
                    mm(PS[sbk][:], KT[64:128, kc * 128:(kc + 1) * 128], QT[64:128, c, :], True, True,
                       [b_kt[kc // 4], b_qt[c]], [b_ps[sbk]])

                qk(0)
                for kc in range(NKC):
                    if kc + 1 < NKC:
                        qk(kc + 1)
                    sa, sbk = 2 * (kc % 2), 2 * (kc % 2) + 1
                    act(PTB[:, sa, :], PS[sa][:], AF.Exp, [b_ps[sa]], [b_pt[sa]], scale=0.125)
                    act(PTB[:, sbk, :], PS[sbk][:], AF.Exp, [b_ps[sbk]], [b_pt[sbk]], scale=0.125)
                    mm(PS[oa][:], VX[:, kc, 0:128], PTB[:, sa, :], kc == 0, kc == NKC - 1,
                       [b_vx[kc // 4], b_vones, b_pt[sa]], [b_ps[oa]])
                    mm(PS[ob_][:], VX[:, kc, 64:192], PTB[:, sbk, :], kc == 0, kc == NKC - 1,
                       [b_vx[kc // 4], b_vones, b_pt[sbk]], [b_ps[ob_]])
                P.add("dve", lambda e, oa=oa: e.reciprocal(out=REC[0:64, :], in_=PS[oa][64:128, :]),
                      [b_ps[oa]], [b_rec])
                tt("dve", MIX[0:64, c, :], PS[oa][0:64, :], REC[0:64, :], ALU.mult, [b_ps[oa], b_rec], [b_mix[c]])
                P.add("dve", lambda e, ob_=ob_: e.reciprocal(out=REC[64:128, :], in_=PS[ob_][0:64, :]),
                      [b_ps[ob_]], [b_rec])
                tt("dve", MIX[64:128, c, :], PS[ob_][64:128, :], REC[64:128, :], ALU.mult,
                   [b_ps[ob_], b_rec], [b_mix[c]])

        def mid(it, nxt):
            par = tile_par[it]
            X = XS[par]
            bx = b_xs[par]
            wo = [ring_get() for _ in range(4)]
            for t in range(4):
                for hf in range(2):
                    ab = 4 + (2 * t + hf) % 4
                    for kc in range(8):
                        wv_, wb_ = wo[kc // 2]
                        wo_v = wv_.rearrange("p (k n) -> p k n", k=2)
                        mm(PS[ab][:], MIX[:, kc, t * 128:(t + 1) * 128], wo_v[:, kc % 2, hf * 512:(hf + 1) * 512],
                           kc == 0, kc == 7, [wb_, b_mix[kc]], [b_ps[ab]])
                    xs = X[:, t, hf * 512:(hf + 1) * 512]
                    tt("dve", xs, xs, PS[ab][:], ALU.add, [bx[t], b_ps[ab]], [bx[t]])
            ring_release(4)
            n_stats(b_srcs4(it), 0)
            n_apr(b_srcs4(it), 0, 0)
            n_tr(4, 0, GL)
            for r in range(16):
                wv_, wb_ = ring_get()
                wu_v = wv_.rearrange("p (f k m) -> p f k m", f=2, k=8)
                for ff in range(2):
                    fc = 2 * r + ff
                    ub = wbank()
                    for kc in range(8):
                        mm(PS[ub][:], wu_v[:, ff, kc, :], AT[:, kc, 0:512], kc == 0, kc == 7,
                           [wb_, b_at[kc]], [b_ps[ub]])
                    rs = fc % 2
                    act(RELU[:, rs, :], PS[ub][:], AF.Relu, [b_ps[ub]], [b_relu[rs]])
                    tt("dve" if fc % 4 else "pool", H[:, fc, :], RELU[:, rs, :], RELU[:, rs, :], ALU.mult,
                       [b_relu[rs]], [b_h[fc]])
                ring_release(1)
            if nxt is not None:
                n_stats(b_srcs6(nxt), 1)
                n_apr(b_srcs6(nxt), 1, "alt")
            for hf in range(2):
                for r in range(8):
                    wv_, wb_ = ring_get()
                    wd_v = wv_.rearrange("p (f n) -> p f n", f=4)
                    for ff in range(4):
                        fc = 4 * r + ff
                        for t in range(4):
                            mm(PS[4 + t][:], H[:, fc, t * 128:(t + 1) * 128], wd_v[:, ff, :], fc == 0, fc == 31,
                               [wb_, b_h[fc]], [b_ps[4 + t]])
                    ring_release(1)
                for t in range(4):
                    xs = X[:, t, hf * 512:(hf + 1) * 512]
                    tt("dve", xs, xs, PS[4 + t][:], ALU.add, [bx[t], b_ps[4 + t]], [bx[t]])
            n_stats(b_srcs4(it), 0)
            n_apr(b_srcs4(it), 0, 12)
            n_tr(4, 12, GP)
            for kc in range(2):
                bk = wbank()
                pbf = PS[bk][:].bitcast(BF16)
                for t in range(4):
                    tr(pbf[:, t * 128:(t + 1) * 128], PBF[:, t, kc * 128:(kc + 1) * 128], [b_pbf], [b_ps[bk]])
                cp(evac_eng(), PTT[:, kc, :], pbf[:, 0:512], [b_ps[bk]], [b_ptt])
            wg = [ring_get() for _ in range(4)]
            wp_v, wp_b = ring_get()
            wp_v = wp_v.rearrange("p (k n) -> p k n", k=2)
            for t in range(4):
                for hf in range(2):
                    gb = 4 + (2 * t + hf) % 2
                    pb = 6 + (2 * t + hf) % 2
                    for kc in range(8):
                        wv_, wb_ = wg[kc // 2]
                        wg_v = wv_.rearrange("p (k n) -> p k n", k=2)
                        mm(PS[gb][:], AT[:, kc, t * 128:(t + 1) * 128], wg_v[:, kc % 2, hf * 512:(hf + 1) * 512],
                           kc == 0, kc == 7, [wb_, b_at[kc]], [b_ps[gb]])
                    for kc in range(2):
                        mm(PS[pb][:], PTT[:, kc, t * 128:(t + 1) * 128], wp_v[:, kc, hf * 512:(hf + 1) * 512],
                           kc == 0, kc == 1, [wp_b, b_ptt], [b_ps[pb]])
                    k2 = (2 * t + hf) % 2
                    G = GT[:, k2 * 512:(k2 + 1) * 512]
                    gbufs = b_gt[2 * k2:2 * k2 + 2]
                    act(G, PS[gb][:], AF.Sigmoid, [b_ps[gb]], gbufs)
                    tt("dve", G, G, PS[pb][:], ALU.mult, gbufs + [b_ps[pb]], gbufs)
                    xs = X[:, t, hf * 512:(hf + 1) * 512]
                    tt("dve", xs, xs, G, ALU.add, [bx[t]] + gbufs, [bx[t]])
            ring_release(5)

        def final(it):
            par = tile_par[it]
            n_stats(b_srcs4(it), 0)
            for t in range(4):
                xs = XS[par][:, t, :]
                ys = YS[:, t, :]
                P.add("dve", lambda e, xs=xs, ys=ys, t=t: e.scalar_tensor_tensor(
                    out=ys, in0=xs, scalar=ST[:, 16 + t:17 + t], in1=FNG[:], op0=ALU.mult, op1=ALU.mult),
                    [b_xs[par][t], b_st[0], b_fng], b_h[16 + 4 * t:20 + 4 * t])
            dst = yout[j][it * 512:(it + 1) * 512, :].rearrange("(c p) d -> p c d", p=128)
            st = dma("sp", dst, YS, "y", reads=b_h[16:32], writes=[b_y[j]])
            last_store.append(st)

        b_load(0)
        ld_cs(0)
        n_stats(b_srcs6(0), 1)
        n_apr(b_srcs6(0), 1, 0)
        head(0)
        for it in range(NT):
            nxt = it + 1 if it + 1 < NT else None
            if nxt is not None:
                b_load(nxt)
                ld_cs(nxt)
            attention(it)
            mid(it, nxt)
            if nxt is not None:
                head(nxt, "alt")
            final(it)

    assert ring_state["got"] == ring_total and ring_state["issued"] == ring_total, ring_state
    P.emit(final_waits=last_store[-1:])
    return nc


def _rope_tables(T):
    t = np.arange(T)
    row = (t // 64).astype(np.float32)
    col = (t % 64).astype(np.float32)
    inv = (np.float32(10000.0) ** (-np.arange(16, dtype=np.float32) / np.float32(16))).astype(np.float32)
    ar = (row[:, None] * inv[None, :]).astype(np.float32)
    ac = (col[:, None] * inv[None, :]).astype(np.float32)
    ang = np.concatenate([ar, ar, ac, ac], axis=1)
    cos = np.cos(ang).astype(np.float32).T
    sin = np.sin(ang).astype(np.float32).T
    return np.ascontiguousarray(np.tile(cos, (2, 1))), np.ascontiguousarray(np.tile(sin, (2, 1)))


def _band_block(T, w, src0, dst0):
    half = w // 2
    out = np.zeros((128, 128), np.float32)
    for jj in range(128):
        tp = dst0 + jj
        if tp < 0 or tp >= T:
            continue
        lo = max(tp - half, 0)
        hi = min(tp + half, T)
        cnt = hi - lo
        for t in range(lo, hi):
            i = t - src0
            if 0 <= i < 128:
                out[i, jj] += 1.0 / cnt
        i = tp - src0
        if 0 <= i < 128:
            out[i, jj] -= 1.0
    return out


def _band_mats(T, OWN, half):
    wins = (2, 4, 8, 16)
    g0 = half * OWN
    BIG = 1 << 20
    mid0 = 4096
    res = np.zeros((128, 5, 4, 128), np.float32)
    for g, w in enumerate(wins):
        res[:, 0, g] = _band_block(BIG, w, mid0 - 128, mid0)
        res[:, 1, g] = _band_block(BIG, w, mid0 + 128, mid0)
        res[:, 2, g] = _band_block(T, w, g0, g0)
        res[:, 3, g] = _band_block(BIG, w, mid0, mid0)
        l0 = g0 + OWN - 128
        res[:, 4, g] = _band_block(T, w, l0, l0)
    return np.ascontiguousarray(res.reshape(128, 2560))


def _weight_chunks(w_in, w_out, w_up, w_down, w_gate, w_proj):
    ch = np.zeros((NCHUNK, 128, 2048), np.float32)
    qcols = np.zeros((4, 128), np.int64)
    for jq in range(4):
        qcols[jq, :64] = jq * 64 + np.arange(64)
        qcols[jq, 64:] = (jq + 4) * 64 + np.arange(64)
    wr = w_in.reshape(8, 128, 1280)
    n = 0
    for r in range(2):
        blk = np.stack([wr[:, :, qcols[2 * r + f]] for f in range(2)], axis=0)
        ch[n] = blk.transpose(2, 0, 1, 3).reshape(128, 2048)
        n += 1
    for r in range(2):
        blk = wr[4 * r:4 * r + 4, :, 768:1280]
        ch[n] = blk.transpose(1, 0, 2).reshape(128, 2048)
        n += 1
    rows = np.zeros(1024, np.int64)
    for c in range(4):
        rows[c * 128:c * 128 + 64] = c * 64 + np.arange(64)
        rows[c * 128 + 64:(c + 1) * 128] = (c + 4) * 64 + np.arange(64)
    rows[512:] = np.arange(512, 1024)
    wo = w_out[rows, :].reshape(8, 128, 1024)
    for r in range(4):
        ch[n] = wo[2 * r:2 * r + 2].transpose(1, 0, 2).reshape(128, 2048)
        n += 1
    wu = w_up.reshape(8, 128, 32, 128)
    for r in range(16):
        blk = wu[:, :, 2 * r:2 * r + 2, :]
        ch[n] = blk.transpose(1, 2, 0, 3).reshape(128, 2048)
        n += 1
    wd = w_down.reshape(32, 128, 2, 512)
    for hf in range(2):
        for r in range(8):
            blk = wd[4 * r:4 * r + 4, :, hf, :]
            ch[n] = blk.transpose(1, 0, 2).reshape(128, 2048)
            n += 1
    wg = w_gate.reshape(8, 128, 1024)
    for r in range(4):
        ch[n] = wg[2 * r:2 * r + 2].transpose(1, 0, 2).reshape(128, 2048)
        n += 1
    ch[n] = w_proj.reshape(2, 128, 1024).transpose(1, 0, 2).reshape(128, 2048)
    n += 1
    assert n == NCHUNK
    wkv = np.zeros((128, 2048), np.float32)
    wkv[:, 0:1024] = wr[:, :, 512:640].transpose(1, 0, 2).reshape(128, 1024)
    wkv[:, 1024:2048] = wr[:, :, 640:768].transpose(1, 0, 2).reshape(128, 1024)
    return ch, wkv


def _consts(norm_mix_g, norm_mlp_g, norm_ple_g, q_norm_g, k_norm_g, pool_scale, w_pool, final_norm_g):
    cb = np.zeros((128, 896), np.float32)
    cb[:, 0:128] = np.eye(128, dtype=np.float32)
    rt = np.zeros((128, 128), np.float32)
    for m in range(128):
        if m % 32 < 16:
            rt[m + 16, m] = -1.0
        else:
            rt[m - 16, m] = 1.0
    cb[:, 128:256] = rt
    blk = np.zeros((128, 128), np.float32)
    blk[:64, :64] = 1.0
    blk[64:, 64:] = 1.0
    cb[:, 256:384] = blk
    cb[:, 384:896] = w_pool.transpose(1, 0, 2).reshape(128, 512)
    cf = np.zeros((128, 32), np.float32)
    cf[:, 0:8] = norm_mix_g.reshape(8, 128).T
    cf[:, 8:16] = norm_mlp_g.reshape(8, 128).T
    cf[:, 16:24] = norm_ple_g.reshape(8, 128).T
    cf[:, 24] = np.tile(q_norm_g, 2)
    cf[:, 25] = np.tile(k_norm_g, 2)
    cf[:, 26:30] = pool_scale.reshape(4, 128).T
    fng = np.ascontiguousarray(np.broadcast_to(final_norm_g[None, :], (128, 1024))).astype(np.float32)
    return cb, cf, fng


def make_in_maps(jobs_data, weights):
    (norm_mix_g, w_in, q_norm_g, k_norm_g, w_pool, pool_scale, w_out, norm_mlp_g, w_up, w_down,
     norm_ple_g, w_ple_gate, w_ple_proj, final_norm_g) = weights
    ch, wkv = _weight_chunks(w_in, w_out, w_up, w_down, w_ple_gate, w_ple_proj)
    cb, cf, fng = _consts(norm_mix_g, norm_mlp_g, norm_ple_g, q_norm_g, k_norm_g, pool_scale, w_pool, final_norm_g)
    in_maps = []
    tabs = {}
    for core_jobs in jobs_data:
        m = {"wts": ch, "wkv": wkv, "cstb": cb, "cstf": cf, "fng": fng}
        for j, jd in enumerate(core_jobs):
            x, p, half = jd["x"], jd["p"], jd["half"]
            T = x.shape[0]
            OWN = T // 2
            o0 = half * OWN
            own = slice(o0, o0 + OWN)
            oth = slice((1 - half) * OWN, (1 - half) * OWN + OWN)
            m[f"xkeys{j}"] = np.ascontiguousarray(np.concatenate([x[own], x[oth]], axis=0))
            xp = np.zeros((OWN + 256, 1024), np.float32)
            xp[128:128 + OWN] = x[own]
            if o0 >= 128:
                xp[0:128] = x[o0 - 128:o0]
            if o0 + OWN + 128 <= T:
                xp[128 + OWN:] = x[o0 + OWN:o0 + OWN + 128]
            m[f"xpad{j}"] = xp
            m[f"p{j}"] = np.ascontiguousarray(p[own])
            if T not in tabs:
                tabs[T] = _rope_tables(T)
            cos, sin = tabs[T]
            m[f"cos{j}"] = np.ascontiguousarray(np.concatenate([cos[:, own], cos[:, oth]], axis=1))
            m[f"sin{j}"] = np.ascontiguousarray(np.concatenate([sin[:, own], sin[:, oth]], axis=1))
            m[f"band{j}"] = _band_mats(T, OWN, half)
        in_maps.append(m)
    return in_maps


_NC_CACHE = {}


def run(x_prompt, x_sample, p_prompt, p_sample, weights, n_cores=8):
    Ts, Tp = x_sample.shape[1], x_prompt.shape[1]
    jobs = [(Ts, Ts // 2), (Tp, Tp // 2)]
    key = tuple(jobs)
    if key not in _NC_CACHE:
        _NC_CACHE[key] = build_program(jobs)
    nc = _NC_CACHE[key]
    jobs_data = []
    for c in range(n_cores):
        s, half = c // 2, c % 2
        jobs_data.append([
            {"x": x_sample[s], "p": p_sample[s], "half": half},
            {"x": x_prompt[s], "p": p_prompt[s], "half": half},
        ])
    in_maps = make_in_maps(jobs_data, weights)
    res = run_bass_kernel_spmd(nc, in_maps, core_ids=list(range(n_cores)))
    y_s = np.zeros_like(x_sample)
    y_p = np.zeros_like(x_prompt)
    for c in range(n_cores):
        s, half = c // 2, c % 2
        r = res.results[c]
        y_s[s, half * (Ts // 2):(half + 1) * (Ts // 2)] = r["y0"]
        y_p[s, half * (Tp // 2):(half + 1) * (Tp // 2)] = r["y1"]
    return y_p, y_s


def kernel(x_prompt, x_sample, p_prompt, p_sample, norm_mix_g, w_in, q_norm_g, k_norm_g, w_pool, pool_scale,
           w_out, norm_mlp_g, w_up, w_down, norm_ple_g, w_ple_gate, w_ple_proj, final_norm_g):
    f = lambda a: np.asarray(a, dtype=np.float32)
    weights = (f(norm_mix_g)[0], f(w_in)[0], f(q_norm_g)[0], f(k_norm_g)[0], f(w_pool)[0], f(pool_scale)[0],
               f(w_out)[0], f(norm_mlp_g)[0], f(w_up)[0], f(w_down)[0], f(norm_ple_g)[0], f(w_ple_gate)[0],
               f(w_ple_proj)[0], f(final_norm_g))
    y_p, y_s = run(f(x_prompt), f(x_sample), f(p_prompt)[0], f(p_sample)[0], weights)
    return (y_p, y_s)
```

```python
import numpy as np
import concourse.bass as bass
import concourse.mybir as mybir
from concourse.bass_utils import run_bass_kernel_spmd

F32 = mybir.dt.float32
BF16 = mybir.dt.bfloat16
ALU = mybir.AluOpType
AF = mybir.ActivationFunctionType

ENGS = ("pe", "act", "dve", "pool", "sp")
EPS = 1e-6
NRING = 6
NCHUNK = 45


class Buf:
    __slots__ = ("name", "writer", "readers")

    def __init__(self, name):
        self.name = name
        self.writer = None
        self.readers = []


class Op:
    __slots__ = ("eng", "fn", "deps", "signal", "count", "dma_key")

    def __init__(self, eng, fn, dma_key=None):
        self.eng = eng
        self.fn = fn
        self.deps = []
        self.signal = False
        self.count = None
        self.dma_key = dma_key


class Prog:
    def __init__(self, nc):
        self.nc = nc
        self.streams = {e: [] for e in ENGS}
        self.dma_keys = {}

    def add(self, eng, fn, reads=(), writes=(), dma_key=None, extra_deps=()):
        op = Op(eng, fn, dma_key)
        deps = []
        for b in reads:
            if b.writer is not None:
                deps.append((b.writer, "raw"))
        for b in writes:
            if b.writer is not None:
                deps.append((b.writer, "waw"))
            for r in b.readers:
                deps.append((r, "war"))
        for d in extra_deps:
            deps.append((d, "raw"))
        if dma_key is not None:
            prev = self.dma_keys.get(dma_key)
            if prev is not None:
                deps.append((prev, "raw"))
        seen = set()
        for d, kind in deps:
            if d is op or id(d) in seen:
                continue
            if d.dma_key is None and op.dma_key is None and d.eng == eng:
                if eng == "pe" or kind != "raw":
                    continue
            seen.add(id(d))
            op.deps.append(d)
            d.signal = True
        for b in reads:
            b.readers.append(op)
        for b in writes:
            b.writer = op
            b.readers = []
        self.streams[eng].append(op)
        if dma_key is not None:
            self.dma_keys[dma_key] = op
        return op

    def emit(self, final_waits=()):
        nc = self.nc
        sems = {e: nc.alloc_semaphore(name=f"s_{e}") for e in ENGS}
        ksems = {k: nc.alloc_semaphore(name=f"k_{k}") for k in self.dma_keys}
        kcnt = {k: 0 for k in self.dma_keys}
        for e in ENGS:
            c = 0
            for op in self.streams[e]:
                if op.dma_key is not None:
                    kcnt[op.dma_key] += 16
                    op.count = kcnt[op.dma_key]
                elif op.signal:
                    c += 1
                    op.count = c
        engobj = {"pe": "tensor", "act": "scalar", "dve": "vector", "pool": "gpsimd", "sp": "sync"}
        with nc.Block() as block:
            for e in ENGS:
                ops = self.streams[e]
                fw = list(final_waits) if e == "sp" else []

                def body(eng, ops=ops, e=e, fw=fw):
                    waited = {}

                    def do_wait(d):
                        key = ("k", d.dma_key) if d.dma_key is not None else ("e", d.eng)
                        if waited.get(key, 0) >= d.count:
                            return
                        waited[key] = d.count
                        sem = ksems[d.dma_key] if d.dma_key is not None else sems[d.eng]
                        eng.wait_ge(sem, d.count)

                    for op in ops:
                        for d in op.deps:
                            do_wait(d)
                        inst = op.fn(eng)
                        if op.dma_key is not None:
                            inst.then_inc(ksems[op.dma_key], 16)
                        elif op.signal:
                            inst.then_inc(sems[e], 1)
                    for d in fw:
                        do_wait(d)

                getattr(block, engobj[e])(body)


def build_program(jobs):
    nc = bass.Bass("TRN2", target_bir_lowering=False)
    P = Prog(nc)
    TMAX = max(T for T, _ in jobs)
    ntiles_total = sum(O // 512 for _, O in jobs)

    def din(name, shape):
        return nc.dram_tensor(name, list(shape), F32, kind="ExternalInput").ap()

    xkeys = [din(f"xkeys{j}", (T, 1024)) for j, (T, O) in enumerate(jobs)]
    xpad = [din(f"xpad{j}", (O + 256, 1024)) for j, (T, O) in enumerate(jobs)]
    pin = [din(f"p{j}", (O, 256)) for j, (T, O) in enumerate(jobs)]
    cosd = [din(f"cos{j}", (128, T)) for j, (T, O) in enumerate(jobs)]
    sind = [din(f"sin{j}", (128, T)) for j, (T, O) in enumerate(jobs)]
    bandd = [din(f"band{j}", (128, 2560)) for j, (T, O) in enumerate(jobs)]
    yout = [nc.dram_tensor(f"y{j}", [O, 1024], F32, kind="ExternalOutput").ap() for j, (T, O) in enumerate(jobs)]
    wts = din("wts", (NCHUNK, 128, 2048))
    wkvd = din("wkv", (128, 2048))
    cstbd = din("cstb", (128, 896))
    cstfd = din("cstf", (128, 32))
    fngd = din("fng", (128, 1024))
    wsc = nc.dram_tensor("wsc", [NCHUNK, 128, 2048], BF16, kind="Internal").ap()

    sb = nc.alloc_sbuf_tensor
    KT = sb("KT", [128, TMAX], BF16)
    VX = sb("VX", [128, TMAX // 128, 192], BF16)
    XS = [sb("XA", [128, 4, 1024], F32), sb("XB", [128, 4, 1024], F32)]
    XH = sb("XH", [128, 2, 1024], F32)
    AT = sb("AT", [128, 8, 768], BF16)
    QT = sb("QT", [128, 4, 512], BF16)
    U6 = sb("U6", [128, 6, 512], BF16)
    MIX = sb("MIX", [128, 8, 512], BF16)
    H = sb("H", [128, 32, 512], BF16)
    RING = sb("RING", [128, NRING, 2048], BF16)
    WKV = sb("WKV", [128, 2048], BF16)
    PTB = sb("PTB", [128, 4, 512], BF16)
    CB = sb("CB", [128, 896], BF16)
    BAND = sb("BAND", [128, 2560], BF16)
    FNG = sb("FNG", [128, 1024], F32)
    CF = sb("CF", [128, 32], F32)
    PBF = sb("PBF", [128, 4, 256], BF16)
    PTT = sb("PTT", [128, 2, 512], BF16)
    TMPALL = sb("TMPALL", [128, 5, 512], F32)
    Y32 = TMPALL[:, 0, :]
    RSTD = TMPALL[:, 1, :]
    T12 = TMPALL[:, 2:4, :]
    T1 = TMPALL[:, 2, :]
    T2 = TMPALL[:, 3, :]
    RELU = T12
    REC = TMPALL[:, 4, :]
    SQYB = sb("SQYB", [128, 2, 512], BF16)
    SQ = SQYB[:, 0, :]
    YB = SQYB[:, 1, :]
    JUNKS = [TMPALL[:, i, :].bitcast(BF16) for i in (1, 2, 3, 4, 0)] + [SQYB[:].rearrange("p a n -> p (a n)")]
    ST = sb("ST", [128, 48], F32)
    CS = sb("CSB", [128, 1024], F32)[:]
    PSALL = nc.alloc_psum_tensor("PSALL", [128, 4096], F32)
    PS = [PSALL[:, b * 512:(b + 1) * 512] for b in range(8)]

    def hview(hbase, i):
        return H[:, hbase + 2 * i:hbase + 2 * i + 2, :].rearrange("p a n -> p (a n)")

    DT = H[:, 12:16, :]
    YS = H[:, 16:32, :].rearrange("p a n -> p (a n)").bitcast(F32).rearrange("p (c d) -> p c d", c=4)
    GT = H[:, 20:28, :].rearrange("p a n -> p (a n)").bitcast(F32)
    IDENT = CB[:, 0:128]
    RT = CB[:, 128:256]
    BLK = CB[:, 256:384]
    WPOOL = CB[:, 384:896].rearrange("p (g d) -> p g d", g=4)

    b_kt = [Buf(f"kt{i}") for i in range(TMAX // 512)]
    b_vx = [Buf(f"vx{i}") for i in range(TMAX // 512)]
    b_xs = [[Buf(f"xa{i}") for i in range(4)], [Buf(f"xb{i}") for i in range(4)]]
    b_xh = [Buf("xh0"), Buf("xh1")]
    b_at = [Buf(f"at{i}") for i in range(8)]
    b_qt = [Buf(f"qt{i}") for i in range(4)]
    b_u = [Buf(f"u{i}") for i in range(6)]
    b_mix = [Buf(f"mix{i}") for i in range(8)]
    b_h = [Buf(f"h{i}") for i in range(32)]
    b_ring = [Buf(f"ring{i}") for i in range(NRING)]
    b_wsc = [Buf(f"wsc{i}") for i in range(NCHUNK)]
    b_wkv, b_cb, b_band, b_fng, b_cf = Buf("wkv"), Buf("cb"), Buf("band"), Buf("fng"), Buf("cf")
    b_pt = [Buf(f"pt{i}") for i in range(4)]
    b_pbf, b_ptt = Buf("pbf"), Buf("ptt")
    b_y32, b_sq, b_yb, b_rstd, b_t1, b_t2 = (Buf(n) for n in ("y32", "sq", "yb", "rstd", "t1", "t2"))
    b_rec, b_vones = Buf("rec"), Buf("vones")
    b_junks = [[b_rstd], [b_t1], [b_t2], [b_rec], [b_y32], [b_sq, b_yb]]
    b_st = [Buf("st0"), Buf("st1")]
    b_relu = [b_t1, b_t2]
    b_ps = [Buf(f"ps{i}") for i in range(8)]
    b_y = [Buf(f"yout{j}") for j in range(len(jobs))]
    b_hc = lambda hbase, i: [b_h[hbase + 2 * i], b_h[hbase + 2 * i + 1]]
    b_dt = lambda g: [b_h[12 + g]]
    b_cs = [Buf("cs0"), Buf("cs1")]
    b_gt = b_h[20:28]

    def dma(q, out, in_, key, reads=(), writes=(), extra=()):
        return P.add(q, lambda e: e.dma_start(out=out, in_=in_), reads, writes, dma_key=key, extra_deps=extra)

    def mm(out, lhsT, rhs, start, stop, reads, writes):
        return P.add("pe", lambda e: e.matmul(out, lhsT=lhsT, rhs=rhs, start=start, stop=stop), reads, writes)

    def tr(out, in_, reads, writes):
        return P.add("pe", lambda e: e.transpose(out=out, in_=in_, identity=IDENT), list(reads) + [b_cb], writes)

    def act(out, in_, func, reads, writes, scale=1.0, bias=None, accum=None):
        def f(e):
            kw = {}
            if bias is not None:
                kw["bias"] = bias
            if accum is not None:
                kw["accum_out"] = accum
            return e.activation(out=out, in_=in_, func=func, scale=scale, **kw)
        return P.add("act", f, reads, writes)

    def ts(eng, out, in0, s1, op0, reads, writes):
        return P.add(eng, lambda e: e.tensor_scalar(out=out, in0=in0, scalar1=s1, scalar2=None, op0=op0),
                     reads, writes)

    def tt(eng, out, in0, in1, op, reads, writes):
        return P.add(eng, lambda e: e.tensor_tensor(out=out, in0=in0, in1=in1, op=op), reads, writes)

    def cp(eng, out, in_, reads, writes):
        if eng == "act":
            return act(out, in_, AF.Copy, reads, writes)
        return P.add(eng, lambda e: e.tensor_copy(out=out, in_=in_), reads, writes)

    wb_ctr = [0]

    def wbank():
        b = wb_ctr[0] % 4
        wb_ctr[0] += 1
        return b

    ab_ctr = [0]

    def abank():
        b = 4 + ab_ctr[0] % 4
        ab_ctr[0] += 1
        return b

    alt = [0]

    def evac_eng():
        alt[0] += 1
        return "act" if alt[0] % 2 else "dve"

    dma("pool", CB[:], cstbd, "c_cb", writes=[b_cb])
    dma("pool", WKV[:], wkvd, "c_wkv", writes=[b_wkv])
    dma("sp", CF[:], cstfd, "c_cf", writes=[b_cf])
    dma("sp", FNG[:], fngd, "c_fng", writes=[b_fng])
    P.add("dve", lambda e: e.memset(VX[:, :, 64:128], 1.0), writes=[b_vones])
    cast_state = {"n": 0, "last": None}

    NCA = 8

    def cast_some(cnt, limit):
        for _ in range(cnt):
            k = cast_state["n"]
            if k >= limit:
                return
            dma("pool", wsc[k], wts[k], f"cast{k % 4}", writes=[b_wsc[k]])
            cast_state["n"] = k + 1

    ring_state = {"issued": 0, "got": 0, "released": 0}
    ring_total = ntiles_total * NCHUNK

    def ring_issue():
        n = ring_state["issued"]
        if n >= ring_total:
            return
        slot = n % NRING
        k = n % NCHUNK
        if k >= NCA and cast_state["n"] < NCHUNK:
            cast_some(NCHUNK, NCHUNK)
        assert cast_state["n"] > k
        dma("sp", RING[:, slot, :], wsc[k], f"r{slot}", reads=[b_wsc[k]], writes=[b_ring[slot]])
        ring_state["issued"] = n + 1

    def ring_get():
        n = ring_state["got"]
        assert n < ring_state["issued"], "ring underflow"
        ring_state["got"] = n + 1
        slot = n % NRING
        return RING[:, slot, :], b_ring[slot]

    def ring_release(cnt=1):
        for _ in range(cnt):
            ring_state["released"] += 1
            ring_issue()

    GM, GL, GP, QG, KG, PSC = 0, 8, 16, 24, 25, 26

    _recb = REC.bitcast(BF16)
    TS = [
        dict(y32=Y32, rstd=RSTD, t1=T1, t2=T2, sq=SQ, yb=YB,
             b_y32=[b_y32], b_rstd=[b_rstd], b_t1=[b_t1], b_t2=[b_t2], b_sq=[b_sq], b_yb=[b_yb]),
        dict(y32=PTB[:, 0:2, :].rearrange("p a n -> p (a n)").bitcast(F32),
             rstd=PTB[:, 2:4, :].rearrange("p a n -> p (a n)").bitcast(F32),
             t1=MIX[:, 0:2, :].rearrange("p a n -> p (a n)").bitcast(F32),
             t2=MIX[:, 2:4, :].rearrange("p a n -> p (a n)").bitcast(F32),
             sq=_recb[:, 0:512], yb=_recb[:, 512:1024],
             b_y32=[b_pt[0], b_pt[1]], b_rstd=[b_pt[2], b_pt[3]], b_t1=[b_mix[0], b_mix[1]],
             b_t2=[b_mix[2], b_mix[3]], b_sq=[b_rec], b_yb=[b_rec]),
    ]

    def n_stats(srcs, s):
        n = len(srcs)
        o = 24 * s
        for i, (ap, bufs) in enumerate(srcs):
            act(JUNKS[i], ap, AF.Square, bufs, b_junks[i] + [b_st[s]], accum=ST[:, o + i:o + i + 1])
        act(ST[:, o + 8:o + 8 + n], ST[:, o:o + n], AF.Ln, [b_st[s]], [b_st[s]], scale=1.0 / 1024, bias=EPS)
        act(ST[:, o + 16:o + 16 + n], ST[:, o + 8:o + 8 + n], AF.Exp, [b_st[s]], [b_st[s]], scale=-0.5)

    ALT_V = [U6[:, 0:2, :], U6[:, 2:4, :], U6[:, 4:6, :], QT[:, 0:2, :], QT[:, 2:4, :], PTB[:, 0:2, :]]
    ALT_B = [[b_u[0], b_u[1]], [b_u[2], b_u[3]], [b_u[4], b_u[5]], [b_qt[0], b_qt[1]], [b_qt[2], b_qt[3]],
             [b_pt[0], b_pt[1]]]

    def aview(loc, i):
        if loc == "alt":
            return ALT_V[i].rearrange("p a n -> p (a n)")
        return hview(loc, i)

    def abufs(loc, i):
        return ALT_B[i] if loc == "alt" else b_hc(loc, i)

    def n_apr(srcs, s, hbase):
        o = 24 * s + 16
        for i, (ap, bufs) in enumerate(srcs):
            ts("dve", aview(hbase, i), ap, ST[:, o + i:o + i + 1], ALU.mult, list(bufs) + [b_st[s]], abufs(hbase, i))

    def n_tr(n, hbase, gcol, act_kcs=(0, 2, 4, 6)):
        for kc in range(8):
            bk = wbank()
            pbf = PS[bk][:].bitcast(BF16)
            for i in range(n):
                tr(pbf[:, i * 128:(i + 1) * 128], aview(hbase, i)[:, kc * 128:(kc + 1) * 128], abufs(hbase, i),
                   [b_ps[bk]])
            eng = "act" if kc in act_kcs else "dve"
            g = CF[:, gcol + kc:gcol + kc + 1]
            if eng == "act":
                act(AT[:, kc, 0:n * 128], pbf[:, 0:n * 128], AF.Copy, [b_ps[bk], b_cf], [b_at[kc]], scale=g)
            else:
                ts("dve", AT[:, kc, 0:n * 128], pbf[:, 0:n * 128], g, ALU.mult, [b_ps[bk], b_cf], [b_at[kc]])

    def rope_pre(zb, gcol, k=0):
        t = TS[k]
        Z = PS[zb][:]
        act(t["sq"], Z, AF.Square, [b_ps[zb]], t["b_sq"])
        act(t["y32"], Z, AF.Copy, [b_ps[zb], b_cf], t["b_y32"], scale=CF[:, gcol:gcol + 1])
        cp("dve", t["yb"], t["y32"], t["b_y32"], t["b_yb"])

    def rope_mm(banks, k=0):
        t = TS[k]
        sb_, rb = banks
        mm(PS[sb_][:], BLK, t["sq"], True, True, t["b_sq"] + [b_cb], [b_ps[sb_]])
        mm(PS[rb][:], RT, t["yb"], True, True, t["b_yb"] + [b_cb], [b_ps[rb]])

    def rope_post(banks, dst, dst_bufs, add_eng="dve", k=0):
        t = TS[k]
        sb_, rb = banks
        act(t["rstd"], PS[sb_][:], AF.Ln, [b_ps[sb_]], t["b_rstd"], scale=1.0 / 64, bias=EPS)
        act(t["rstd"], t["rstd"], AF.Exp, t["b_rstd"], t["b_rstd"], scale=-0.5)
        tt("dve", t["t1"], t["y32"], CS[:, 0:512], ALU.mult, t["b_y32"] + [b_cs[0]], t["b_t1"])
        tt("dve", t["t2"], PS[rb][:], CS[:, 512:1024], ALU.mult, [b_ps[rb], b_cs[1]], t["b_t2"])
        tt(add_eng, t["t1"], t["t1"], t["t2"], ALU.add, t["b_t1"] + t["b_t2"], t["b_t1"])
        tt("pool", dst, t["t1"], t["rstd"], ALU.mult, t["b_t1"] + t["b_rstd"], dst_bufs)

    xctr = [0]
    last_store = []
    for j, (T, OWN) in enumerate(jobs):
        NKT = T // 512
        NKC = T // 128
        NT = OWN // 512
        NOC = OWN // 128
        dma("pool", BAND[:, 0:1280], bandd[j][:, 0:1280], "c_band0", writes=[b_band])
        dma("pool", BAND[:, 1280:2560], bandd[j][:, 1280:2560], "c_band1", writes=[b_band])
        cast_per = -(-NCA // NKT)

        XA3 = [XS[0][:], XS[1][:], YS]
        BA3 = [[[bb] for bb in b_xs[0]], [[bb] for bb in b_xs[1]], [b_h[16 + 4 * c:20 + 4 * c] for c in range(4)]]

        def a_load(kt):
            r = kt % 3
            src = xkeys[j][kt * 512:(kt + 1) * 512, :].rearrange("(c p) d -> p c d", p=128)
            dma("sp", XA3[r], src, f"x{r}", writes=[bb for grp in BA3[r] for bb in grp])

        def a_cs(kt):
            dma("sp", CS[:, 0:512], cosd[j][:, kt * 512:(kt + 1) * 512], "cs0", writes=[b_cs[0]])
            dma("sp", CS[:, 512:1024], sind[j][:, kt * 512:(kt + 1) * 512], "cs1", writes=[b_cs[1]])

        def a_srcs(kt):
            r = kt % 3
            return [(XA3[r][:, c, :], BA3[r][c]) for c in range(4)]

        a_load(0)
        a_cs(0)
        if NKT > 1:
            a_load(1)
        n_stats(a_srcs(0), 1)
        n_apr(a_srcs(0), 1, 0)
        for kt in range(NKT):
            sset = (kt + 1) % 2
            n_tr(4, 0, GM, act_kcs=(0, 7))
            if j == 0:
                cast_some(cast_per, NCA)
            if kt + 2 < NKT:
                a_load(kt + 2)
            if kt + 1 < NKT:
                n_stats(a_srcs(kt + 1), 1 - sset)
                n_apr(a_srcs(kt + 1), 1 - sset, 0)
            zb = wbank()
            for kc in range(8):
                mm(PS[zb][:], WKV[:, kc * 128:(kc + 1) * 128], AT[:, kc, 0:512], kc == 0, kc == 7,
                   [b_wkv, b_at[kc]], [b_ps[zb]])
            rope_pre(zb, KG)
            vb = abank()
            for c in range(4):
                for kc in range(8):
                    mm(PS[vb][:, c * 128:(c + 1) * 128], AT[:, kc, c * 128:(c + 1) * 128],
                       WKV[:, 1024 + kc * 128:1024 + (kc + 1) * 128], kc == 0, kc == 7,
                       [b_wkv, b_at[kc]], [b_ps[vb]])
            rb = (abank(), abank())
            rope_mm(rb)
            rope_post(rb, KT[:, kt * 512:(kt + 1) * 512], [b_kt[kt]], add_eng="pool")
            if kt + 1 < NKT:
                a_cs(kt + 1)
            pv = PS[vb][:].rearrange("p (c n) -> p c n", c=4)
            cp("dve", VX[:, kt * 4:(kt + 1) * 4, 0:64], pv[:, :, 0:64], [b_ps[vb]], [b_vx[kt]])
            cp("act", VX[:, kt * 4:(kt + 1) * 4, 128:192], pv[:, :, 64:128], [b_ps[vb]], [b_vx[kt]])
        if j == 0:
            cast_some(NCA, NCA)
            for _ in range(NRING):
                ring_issue()

        tile_par = {}

        def b_load(it):
            par = xctr[0] % 2
            xctr[0] += 1
            tile_par[it] = par
            r0 = it * 512
            dma("sp", XH[:, 0, :], xpad[j][r0:r0 + 128, :], "xh0", writes=[b_xh[0]])
            src = xpad[j][r0 + 128:r0 + 640, :].rearrange("(c p) d -> p c d", p=128)
            dma("sp", XS[par][:], src, f"x{par}", writes=b_xs[par])
            dma("sp", XH[:, 1, :], xpad[j][r0 + 640:r0 + 768, :], "xh1", writes=[b_xh[1]])

        def ld_cs(it):
            dma("sp", CS[:, 0:512], cosd[j][:, it * 512:(it + 1) * 512], "cs0", writes=[b_cs[0]])
            dma("sp", CS[:, 512:1024], sind[j][:, it * 512:(it + 1) * 512], "cs1", writes=[b_cs[1]])

        def b_srcs6(it):
            par = tile_par[it]
            return ([(XH[:, 0, :], [b_xh[0]])] + [(XS[par][:, c, :], [b_xs[par][c]]) for c in range(4)]
                    + [(XH[:, 1, :], [b_xh[1]])])

        def b_srcs4(it):
            par = tile_par[it]
            return [(XS[par][:, c, :], [b_xs[par][c]]) for c in range(4)]

        def head(it, aloc=0):
            psrc = pin[j][it * 512:(it + 1) * 512, :].rearrange("(c p) d -> p c d", p=128)
            dma("pool", PBF[:], psrc, "p", writes=[b_pbf])
            n_tr(6, aloc, GM)
            wq = [ring_get(), ring_get()]
            wu = [ring_get(), ring_get()]

            def qproj(jq):
                wv_, wb_ = wq[jq // 2]
                wq_v = wv_.rearrange("p (f k m) -> p f k m", f=2, k=8)
                for kc in range(8):
                    mm(PS[jq][:], wq_v[:, jq % 2, kc, :], AT[:, kc, 128:640], kc == 0, kc == 7,
                       [wb_, b_at[kc]], [b_ps[jq]])

            def uproj(c):
                ub = abank()
                for kc in range(8):
                    wv_, wb_ = wu[kc // 4]
                    wu_v = wv_.rearrange("p (k n) -> p k n", k=4)
                    mm(PS[ub][:], AT[:, kc, c * 128:(c + 1) * 128], wu_v[:, kc % 4, :], kc == 0, kc == 7,
                       [wb_, b_at[kc]], [b_ps[ub]])
                cp(evac_eng(), U6[:, c, :], PS[ub][:], [b_ps[ub]], [b_u[c]])

            qproj(0)
            qproj(1)
            rope_pre(0, QG, 0)
            uproj(0)
            for jq in range(4):
                if jq + 1 < 4:
                    rope_pre(jq + 1, QG, (jq + 1) % 2)
                rb = (abank(), abank())
                rope_mm(rb, jq % 2)
                if jq + 1 < 4:
                    uproj(jq + 1)
                if jq + 2 < 4:
                    qproj(jq + 2)
                rope_post(rb, QT[:, jq, :], [b_qt[jq]], k=jq % 2)
            uproj(4)
            uproj(5)
            ring_release(4)
            bandv = BAND[:].rearrange("p (t g n) -> p t g n", t=5, g=4)
            for g in range(4):
                db = wbank()
                for ci in range(1, 5):
                    gi = it * 4 + (ci - 1)
                    curt = 2 if gi == 0 else (4 if gi == NOC - 1 else 3)
                    o = PS[db][:, (ci - 1) * 128:ci * 128]
                    mm(o, U6[:, ci - 1, g * 128:(g + 1) * 128], bandv[:, 0, g, :], True, False,
                       [b_u[ci - 1], b_band], [b_ps[db]])
                    mm(o, U6[:, ci, g * 128:(g + 1) * 128], bandv[:, curt, g, :], False, False,
                       [b_u[ci], b_band], [b_ps[db]])
                    mm(o, U6[:, ci + 1, g * 128:(g + 1) * 128], bandv[:, 1, g, :], False, True,
                       [b_u[ci + 1], b_band], [b_ps[db]])
                cp(evac_eng(), DT[:, g, :], PS[db][:], [b_ps[db]], b_dt(g))
                ob = wbank()
                mm(PS[ob][:], WPOOL[:, g, :], DT[:, g, :], True, True, [b_cb] + b_dt(g), [b_ps[ob]])
                ts("dve", MIX[:, 4 + g, :], PS[ob][:], CF[:, PSC + g:PSC + g + 1], ALU.mult,
                   [b_ps[ob], b_cf], [b_mix[4 + g]])

        def attention(it):
            for c in range(4):
                oa, ob_ = 4 + 2 * (c % 2), 5 + 2 * (c % 2)

                def qk(kc):
                    sa, sbk = 2 * (kc % 2), 2 * (kc % 2) + 1
                    mm(PS[sa][:], KT[0:64, kc * 128:(kc + 1) * 128], QT[0:64, c, :], True, True,
                       [b_kt[kc // 4], b_qt[c]], [b_ps[sa]])
                    mm(PS[sbk][:], KT[64:128, kc * 128:(kc + 1) * 128], QT[64:128, c, :], True, True,
                       [b_kt[kc // 4], b_qt[c]], [b_ps[sbk]])

                qk(0)
                for kc in range(NKC):
                    if kc + 1 < NKC:
                        qk(kc + 1)
                    sa, sbk = 2 * (kc % 2), 2 * (kc % 2) + 1
                    act(PTB[:, sa, :], PS[sa][:], AF.Exp, [b_ps[sa]], [b_pt[sa]], scale=0.125)
                    act(PTB[:, sbk, :], PS[sbk][:], AF.Exp, [b_ps[sbk]], [b_pt[sbk]], scale=0.125)
                    mm(PS[oa][:], VX[:, kc, 0:128], PTB[:, sa, :], kc == 0, kc == NKC - 1,
                       [b_vx[kc // 4], b_vones, b_pt[sa]], [b_ps[oa]])
                    mm(PS[ob_][:], VX[:, kc, 64:192], PTB[:, sbk, :], kc == 0, kc == NKC - 1,
                       [b_vx[kc // 4], b_vones, b_pt[sbk]], [b_ps[ob_]])
                P.add("dve", lambda e, oa=oa: e.reciprocal(out=REC[0:64, :], in_=PS[oa][64:128, :]),
                      [b_ps[oa]], [b_rec])
                tt("dve", MIX[0:64, c, :], PS[oa][0:64, :], REC[0:64, :], ALU.mult, [b_ps[oa], b_rec], [b_mix[c]])
                P.add("dve", lambda e, ob_=ob_: e.reciprocal(out=REC[64:128, :], in_=PS[ob_][0:64, :]),
                      [b_ps[ob_]], [b_rec])
                tt("dve", MIX[64:128, c, :], PS[ob_][64:128, :], REC[64:128, :], ALU.mult,
                   [b_ps[ob_], b_rec], [b_mix[c]])

        def mid(it, nxt):
            par = tile_par[it]
            X = XS[par]
            bx = b_xs[par]
            wo = [ring_get() for _ in range(4)]
            for t in range(4):
                for hf in range(2):
                    ab = 4 + (2 * t + hf) % 4
                    for kc in range(8):
                        wv_, wb_ = wo[kc // 2]
                        wo_v = wv_.rearrange("p (k n) -> p k n", k=2)
                        mm(PS[ab][:], MIX[:, kc, t * 128:(t + 1) * 128], wo_v[:, kc % 2, hf * 512:(hf + 1) * 512],
                           kc == 0, kc == 7, [wb_, b_mix[kc]], [b_ps[ab]])
                    xs = X[:, t, hf * 512:(hf + 1) * 512]
                    tt("dve", xs, xs, PS[ab][:], ALU.add, [bx[t], b_ps[ab]], [bx[t]])
            ring_release(4)
            n_stats(b_srcs4(it), 0)
            n_apr(b_srcs4(it), 0, 0)
            n_tr(4, 0, GL)
            for r in range(16):
                wv_, wb_ = ring_get()
                wu_v = wv_.rearrange("p (f k m) -> p f k m", f=2, k=8)
                for ff in range(2):
                    fc = 2 * r + ff
                    ub = wbank()
                    for kc in range(8):
                        mm(PS[ub][:], wu_v[:, ff, kc, :], AT[:, kc, 0:512], kc == 0, kc == 7,
                           [wb_, b_at[kc]], [b_ps[ub]])
                    rs = fc % 2
                    act(RELU[:, rs, :], PS[ub][:], AF.Relu, [b_ps[ub]], [b_relu[rs]])
                    tt("dve" if fc % 4 else "pool", H[:, fc, :], RELU[:, rs, :], RELU[:, rs, :], ALU.mult,
                       [b_relu[rs]], [b_h[fc]])
                ring_release(1)
            if nxt is not None:
                n_stats(b_srcs6(nxt), 1)
                n_apr(b_srcs6(nxt), 1, "alt")
            for hf in range(2):
                for r in range(8):
                    wv_, wb_ = ring_get()
                    wd_v = wv_.rearrange("p (f n) -> p f n", f=4)
                    for ff in range(4):
                        fc = 4 * r + ff
                        for t in range(4):
                            mm(PS[4 + t][:], H[:, fc, t * 128:(t + 1) * 128], wd_v[:, ff, :], fc == 0, fc == 31,
                               [wb_, b_h[fc]], [b_ps[4 + t]])
                    ring_release(1)
                for t in range(4):
                    xs = X[:, t, hf * 512:(hf + 1) * 512]
                    tt("dve", xs, xs, PS[4 + t][:], ALU.add, [bx[t], b_ps[4 + t]], [bx[t]])
            n_stats(b_srcs4(it), 0)
            n_apr(b_srcs4(it), 0, 12)
            n_tr(4, 12, GP)
            for kc in range(2):
                bk = wbank()
                pbf = PS[bk][:].bitcast(BF16)
                for t in range(4):
                    tr(pbf[:, t * 128:(t + 1) * 128], PBF[:, t, kc * 128:(kc + 1) * 128], [b_pbf], [b_ps[bk]])
                cp(evac_eng(), PTT[:, kc, :], pbf[:, 0:512], [b_ps[bk]], [b_ptt])
            wg = [ring_get() for _ in range(4)]
            wp_v, wp_b = ring_get()
            wp_v = wp_v.rearrange("p (k n) -> p k n", k=2)
            for t in range(4):
                for hf in range(2):
                    gb = 4 + (2 * t + hf) % 2
                    pb = 6 + (2 * t + hf) % 2
                    for kc in range(8):
                        wv_, wb_ = wg[kc // 2]
                        wg_v = wv_.rearrange("p (k n) -> p k n", k=2)
                        mm(PS[gb][:], AT[:, kc, t * 128:(t + 1) * 128], wg_v[:, kc % 2, hf * 512:(hf + 1) * 512],
                           kc == 0, kc == 7, [wb_, b_at[kc]], [b_ps[gb]])
                    for kc in range(2):
                        mm(PS[pb][:], PTT[:, kc, t * 128:(t + 1) * 128], wp_v[:, kc, hf * 512:(hf + 1) * 512],
                           kc == 0, kc == 1, [wp_b, b_ptt], [b_ps[pb]])
                    k2 = (2 * t + hf) % 2
                    G = GT[:, k2 * 512:(k2 + 1) * 512]
                    gbufs = b_gt[2 * k2:2 * k2 + 2]
                    act(G, PS[gb][:], AF.Sigmoid, [b_ps[gb]], gbufs)
                    tt("dve", G, G, PS[pb][:], ALU.mult, gbufs + [b_ps[pb]], gbufs)
                    xs = X[:, t, hf * 512:(hf + 1) * 512]
                    tt("dve", xs, xs, G, ALU.add, [bx[t]] + gbufs, [bx[t]])
            ring_release(5)

        def final(it):
            par = tile_par[it]
            n_stats(b_srcs4(it), 0)
            for t in range(4):
                xs = XS[par][:, t, :]
                ys = YS[:, t, :]
                P.add("dve", lambda e, xs=xs, ys=ys, t=t: e.scalar_tensor_tensor(
                    out=ys, in0=xs, scalar=ST[:, 16 + t:17 + t], in1=FNG[:], op0=ALU.mult, op1=ALU.mult),
                    [b_xs[par][t], b_st[0], b_fng], b_h[16 + 4 * t:20 + 4 * t])
            dst = yout[j][it * 512:(it + 1) * 512, :].rearrange("(c p) d -> p c d", p=128)
            st = dma("sp", dst, YS, "y", reads=b_h[16:32], writes=[b_y[j]])
            last_store.append(st)

        b_load(0)
        ld_cs(0)
        n_stats(b_srcs6(0), 1)
        n_apr(b_srcs6(0), 1, 0)
        head(0)
        for it in range(NT):
            nxt = it + 1 if it + 1 < NT else None
            if nxt is not None:
                b_load(nxt)
                ld_cs(nxt)
            attention(it)
            mid(it, nxt)
            if nxt is not None:
                head(nxt, "alt")
            final(it)

    assert ring_state["got"] == ring_total and ring_state["issued"] == ring_total, ring_state
    P.emit(final_waits=last_store[-1:])
    return nc


def _rope_tables(T):
    t = np.arange(T)
    row = (t // 64).astype(np.float32)
    col = (t % 64).astype(np.float32)
    inv = (np.float32(10000.0) ** (-np.arange(16, dtype=np.float32) / np.float32(16))).astype(np.float32)
    ar = (row[:, None] * inv[None, :]).astype(np.float32)
    ac = (col[:, None] * inv[None, :]).astype(np.float32)
    ang = np.concatenate([ar, ar, ac, ac], axis=1)
    cos = np.cos(ang).astype(np.float32).T
    sin = np.sin(ang).astype(np.float32).T
    return np.ascontiguousarray(np.tile(cos, (2, 1))), np.ascontiguousarray(np.tile(sin, (2, 1)))


def _band_block(T, w, src0, dst0):
    half = w // 2
    out = np.zeros((128, 128), np.float32)
    for jj in range(128):
        tp = dst0 + jj
        if tp < 0 or tp >= T:
            continue
        lo = max(tp - half, 0)
        hi = min(tp + half, T)
        cnt = hi - lo
        for t in range(lo, hi):
            i = t - src0
            if 0 <= i < 128:
                out[i, jj] += 1.0 / cnt
        i = tp - src0
        if 0 <= i < 128:
            out[i, jj] -= 1.0
    return out


def _band_mats(T, OWN, half):
    wins = (2, 4, 8, 16)
    g0 = half * OWN
    BIG = 1 << 20
    mid0 = 4096
    res = np.zeros((128, 5, 4, 128), np.float32)
    for g, w in enumerate(wins):
        res[:, 0, g] = _band_block(BIG, w, mid0 - 128, mid0)
        res[:, 1, g] = _band_block(BIG, w, mid0 + 128, mid0)
        res[:, 2, g] = _band_block(T, w, g0, g0)
        res[:, 3, g] = _band_block(BIG, w, mid0, mid0)
        l0 = g0 + OWN - 128
        res[:, 4, g] = _band_block(T, w, l0, l0)
    return np.ascontiguousarray(res.reshape(128, 2560))


def _weight_chunks(w_in, w_out, w_up, w_down, w_gate, w_proj):
    ch = np.zeros((NCHUNK, 128, 2048), np.float32)
    qcols = np.zeros((4, 128), np.int64)
    for jq in range(4):
        qcols[jq, :64] = jq * 64 + np.arange(64)
        qcols[jq, 64:] = (jq + 4) * 64 + np.arange(64)
    wr = w_in.reshape(8, 128, 1280)
    n = 0
    for r in range(2):
        blk = np.stack([wr[:, :, qcols[2 * r + f]] for f in range(2)], axis=0)
        ch[n] = blk.transpose(2, 0, 1, 3).reshape(128, 2048)
        n += 1
    for r in range(2):
        blk = wr[4 * r:4 * r + 4, :, 768:1280]
        ch[n] = blk.transpose(1, 0, 2).reshape(128, 2048)
        n += 1
    rows = np.zeros(1024, np.int64)
    for c in range(4):
        rows[c * 128:c * 128 + 64] = c * 64 + np.arange(64)
        rows[c * 128 + 64:(c + 1) * 128] = (c + 4) * 64 + np.arange(64)
    rows[512:] = np.arange(512, 1024)
    wo = w_out[rows, :].reshape(8, 128, 1024)
    for r in range(4):
        ch[n] = wo[2 * r:2 * r + 2].transpose(1, 0, 2).reshape(128, 2048)
        n += 1
    wu = w_up.reshape(8, 128, 32, 128)
    for r in range(16):
        blk = wu[:, :, 2 * r:2 * r + 2, :]
        ch[n] = blk.transpose(1, 2, 0, 3).reshape(128, 2048)
        n += 1
    wd = w_down.reshape(32, 128, 2, 512)
    for hf in range(2):
        for r in range(8):
            blk = wd[4 * r:4 * r + 4, :, hf, :]
            ch[n] = blk.transpose(1, 0, 2).reshape(128, 2048)
            n += 1
    wg = w_gate.reshape(8, 128, 1024)
    for r in range(4):
        ch[n] = wg[2 * r:2 * r + 2].transpose(1, 0, 2).reshape(128, 2048)
        n += 1
    ch[n] = w_proj.reshape(2, 128, 1024).transpose(1, 0, 2).reshape(128, 2048)
    n += 1
    assert n == NCHUNK
    wkv = np.zeros((128, 2048), np.float32)
    wkv[:, 0:1024] = wr[:, :, 512:640].transpose(1, 0, 2).reshape(128, 1024)
    wkv[:, 1024:2048] = wr[:, :, 640:768].transpose(1, 0, 2).reshape(128, 1024)
    return ch, wkv


def _consts(norm_mix_g, norm_mlp_g, norm_ple_g, q_norm_g, k_norm_g, pool_scale, w_pool, final_norm_g):
    cb = np.zeros((128, 896), np.float32)
    cb[:, 0:128] = np.eye(128, dtype=np.float32)
    rt = np.zeros((128, 128), np.float32)
    for m in range(128):
        if m % 32 < 16:
            rt[m + 16, m] = -1.0
        else:
            rt[m - 16, m] = 1.0
    cb[:, 128:256] = rt
    blk = np.zeros((128, 128), np.float32)
    blk[:64, :64] = 1.0
    blk[64:, 64:] = 1.0
    cb[:, 256:384] = blk
    cb[:, 384:896] = w_pool.transpose(1, 0, 2).reshape(128, 512)
    cf = np.zeros((128, 32), np.float32)
    cf[:, 0:8] = norm_mix_g.reshape(8, 128).T
    cf[:, 8:16] = norm_mlp_g.reshape(8, 128).T
    cf[:, 16:24] = norm_ple_g.reshape(8, 128).T
    cf[:, 24] = np.tile(q_norm_g, 2)
    cf[:, 25] = np.tile(k_norm_g, 2)
    cf[:, 26:30] = pool_scale.reshape(4, 128).T
    fng = np.ascontiguousarray(np.broadcast_to(final_norm_g[None, :], (128, 1024))).astype(np.float32)
    return cb, cf, fng


def make_in_maps(jobs_data, weights):
    (norm_mix_g, w_in, q_norm_g, k_norm_g, w_pool, pool_scale, w_out, norm_mlp_g, w_up, w_down,
     norm_ple_g, w_ple_gate, w_ple_proj, final_norm_g) = weights
    ch, wkv = _weight_chunks(w_in, w_out, w_up, w_down, w_ple_gate, w_ple_proj)
    cb, cf, fng = _consts(norm_mix_g, norm_mlp_g, norm_ple_g, q_norm_g, k_norm_g, pool_scale, w_pool, final_norm_g)
    in_maps = []
    tabs = {}
    for core_jobs in jobs_data:
        m = {"wts": ch, "wkv": wkv, "cstb": cb, "cstf": cf, "fng": fng}
        for j, jd in enumerate(core_jobs):
            x, p, half = jd["x"], jd["p"], jd["half"]
            T = x.shape[0]
            OWN = T // 2
            o0 = half * OWN
            own = slice(o0, o0 + OWN)
            oth = slice((1 - half) * OWN, (1 - half) * OWN + OWN)
            m[f"xkeys{j}"] = np.ascontiguousarray(np.concatenate([x[own], x[oth]], axis=0))
            xp = np.zeros((OWN + 256, 1024), np.float32)
            xp[128:128 + OWN] = x[own]
            if o0 >= 128:
                xp[0:128] = x[o0 - 128:o0]
            if o0 + OWN + 128 <= T:
                xp[128 + OWN:] = x[o0 + OWN:o0 + OWN + 128]
            m[f"xpad{j}"] = xp
            m[f"p{j}"] = np.ascontiguousarray(p[own])
            if T not in tabs:
                tabs[T] = _rope_tables(T)
            cos, sin = tabs[T]
            m[f"cos{j}"] = np.ascontiguousarray(np.concatenate([cos[:, own], cos[:, oth]], axis=1))
            m[f"sin{j}"] = np.ascontiguousarray(np.concatenate([sin[:, own], sin[:, oth]], axis=1))
            m[f"band{j}"] = _band_mats(T, OWN, half)
        in_maps.append(m)
    return in_maps


_NC_CACHE = {}


def run(x_prompt, x_sample, p_prompt, p_sample, weights, n_cores=8):
    Ts, Tp = x_sample.shape[1], x_prompt.shape[1]
    jobs = [(Ts, Ts // 2), (Tp, Tp // 2)]
    key = tuple(jobs)
    if key not in _NC_CACHE:
        _NC_CACHE[key] = build_program(jobs)
    nc = _NC_CACHE[key]
    jobs_data = []
    for c in range(n_cores):
        s, half = c // 2, c % 2
        jobs_data.append([
            {"x": x_sample[s], "p": p_sample[s], "half": half},
            {"x": x_prompt[s], "p": p_prompt[s], "half": half},
        ])
    in_maps = make_in_maps(jobs_data, weights)
    res = run_bass_kernel_spmd(nc, in_maps, core_ids=list(range(n_cores)))
    y_s = np.zeros_like(x_sample)
    y_p = np.zeros_like(x_prompt)
    for c in range(n_cores):
        s, half = c // 2, c % 2
        r = res.results[c]
        y_s[s, half * (Ts // 2):(half + 1) * (Ts // 2)] = r["y0"]
        y_p[s, half * (Tp // 2):(half + 1) * (Tp // 2)] = r["y1"]
    return y_p, y_s


def kernel(x_prompt, x_sample, p_prompt, p_sample, norm_mix_g, w_in, q_norm_g, k_norm_g, w_pool, pool_scale,
           w_out, norm_mlp_g, w_up, w_down, norm_ple_g, w_ple_gate, w_ple_proj, final_norm_g):
    f = lambda a: np.asarray(a, dtype=np.float32)
    weights = (f(norm_mix_g)[0], f(w_in)[0], f(q_norm_g)[0], f(k_norm_g)[0], f(w_pool)[0], f(pool_scale)[0],
               f(w_out)[0], f(norm_mlp_g)[0], f(w_up)[0], f(w_down)[0], f(norm_ple_g)[0], f(w_ple_gate)[0],
               f(w_ple_proj)[0], f(final_norm_g))
    y_p, y_s = run(f(x_prompt), f(x_sample), f(p_prompt)[0], f(p_sample)[0], weights)
    return (y_p, y_s)
```

```python
import numpy as np
import concourse.bass as bass
import concourse.mybir as mybir
from concourse.bass_utils import run_bass_kernel_spmd

F32 = mybir.dt.float32
BF16 = mybir.dt.bfloat16
ALU = mybir.AluOpType
AF = mybir.ActivationFunctionType

ENGS = ("pe", "act", "dve", "pool", "sp")
EPS = 1e-6
NRING = 6
NCHUNK = 45


class Buf:
    __slots__ = ("name", "writer", "readers")

    def __init__(self, name):
        self.name = name
        self.writer = None
        self.readers = []


class Op:
    __slots__ = ("eng", "fn", "deps", "signal", "count", "dma_key")

    def __init__(self, eng, fn, dma_key=None):
        self.eng = eng
        self.fn = fn
        self.deps = []
        self.signal = False
        self.count = None
        self.dma_key = dma_key


class Prog:
    def __init__(self, nc):
        self.nc = nc
        self.streams = {e: [] for e in ENGS}
        self.dma_keys = {}

    def add(self, eng, fn, reads=(), writes=(), dma_key=None, extra_deps=()):
        op = Op(eng, fn, dma_key)
        deps = []
        for b in reads:
            if b.writer is not None:
                deps.append((b.writer, "raw"))
        for b in writes:
            if b.writer is not None:
                deps.append((b.writer, "waw"))
            for r in b.readers:
                deps.append((r, "war"))
        for d in extra_deps:
            deps.append((d, "raw"))
        if dma_key is not None:
            prev = self.dma_keys.get(dma_key)
            if prev is not None:
                deps.append((prev, "raw"))
        seen = set()
        for d, kind in deps:
            if d is op or id(d) in seen:
                continue
            if d.dma_key is None and op.dma_key is None and d.eng == eng:
                if eng == "pe" or kind != "raw":
                    continue
            seen.add(id(d))
            op.deps.append(d)
            d.signal = True
        for b in reads:
            b.readers.append(op)
        for b in writes:
            b.writer = op
            b.readers = []
        self.streams[eng].append(op)
        if dma_key is not None:
            self.dma_keys[dma_key] = op
        return op

    def emit(self, final_waits=()):
        nc = self.nc
        sems = {e: nc.alloc_semaphore(name=f"s_{e}") for e in ENGS}
        ksems = {k: nc.alloc_semaphore(name=f"k_{k}") for k in self.dma_keys}
        kcnt = {k: 0 for k in self.dma_keys}
        for e in ENGS:
            c = 0
            for op in self.streams[e]:
                if op.dma_key is not None:
                    kcnt[op.dma_key] += 16
                    op.count = kcnt[op.dma_key]
                elif op.signal:
                    c += 1
                    op.count = c
        engobj = {"pe": "tensor", "act": "scalar", "dve": "vector", "pool": "gpsimd", "sp": "sync"}
        with nc.Block() as block:
            for e in ENGS:
                ops = self.streams[e]
                fw = list(final_waits) if e == "sp" else []

                def body(eng, ops=ops, e=e, fw=fw):
                    waited = {}

                    def do_wait(d):
                        key = ("k", d.dma_key) if d.dma_key is not None else ("e", d.eng)
                        if waited.get(key, 0) >= d.count:
                            return
                        waited[key] = d.count
                        sem = ksems[d.dma_key] if d.dma_key is not None else sems[d.eng]
                        eng.wait_ge(sem, d.count)

                    for op in ops:
                        for d in op.deps:
                            do_wait(d)
                        inst = op.fn(eng)
                        if op.dma_key is not None:
                            inst.then_inc(ksems[op.dma_key], 16)
                        elif op.signal:
                            inst.then_inc(sems[e], 1)
                    for d in fw:
                        do_wait(d)

                getattr(block, engobj[e])(body)


def build_program(jobs):
    nc = bass.Bass("TRN2", target_bir_lowering=False)
    P = Prog(nc)
    TMAX = max(T for T, _ in jobs)
    ntiles_total = sum(O // 512 for _, O in jobs)

    def din(name, shape):
        return nc.dram_tensor(name, list(shape), F32, kind="ExternalInput").ap()

    xkeys = [din(f"xkeys{j}", (T, 1024)) for j, (T, O) in enumerate(jobs)]
    xpad = [din(f"xpad{j}", (O + 256, 1024)) for j, (T, O) in enumerate(jobs)]
    pin = [din(f"p{j}", (O, 256)) for j, (T, O) in enumerate(jobs)]
    cosd = [din(f"cos{j}", (128, T)) for j, (T, O) in enumerate(jobs)]
    sind = [din(f"sin{j}", (128, T)) for j, (T, O) in enumerate(jobs)]
    bandd = [din(f"band{j}", (128, 2560)) for j, (T, O) in enumerate(jobs)]
    yout = [nc.dram_tensor(f"y{j}", [O, 1024], F32, kind="ExternalOutput").ap() for j, (T, O) in enumerate(jobs)]
    wts = din("wts", (NCHUNK, 128, 2048))
    wkvd = din("wkv", (128, 2048))
    cstbd = din("cstb", (128, 896))
    cstfd = din("cstf", (128, 32))
    fngd = din("fng", (128, 1024))
    wsc = nc.dram_tensor("wsc", [NCHUNK, 128, 2048], BF16, kind="Internal").ap()

    sb = nc.alloc_sbuf_tensor
    KT = sb("KT", [128, TMAX], BF16)
    VX = sb("VX", [128, TMAX // 128, 192], BF16)
    XS = [sb("XA", [128, 4, 1024], F32), sb("XB", [128, 4, 1024], F32)]
    XH = sb("XH", [128, 2, 1024], F32)
    AT = sb("AT", [128, 8, 768], BF16)
    QT = sb("QT", [128, 4, 512], BF16)
    U6 = sb("U6", [128, 6, 512], BF16)
    MIX = sb("MIX", [128, 8, 512], BF16)
    H = sb("H", [128, 32, 512], BF16)
    RING = sb("RING", [128, NRING, 2048], BF16)
    WKV = sb("WKV", [128, 2048], BF16)
    PTB = sb("PTB", [128, 4, 512], BF16)
    CB = sb("CB", [128, 896], BF16)
    BAND = sb("BAND", [128, 2560], BF16)
    FNG = sb("FNG", [128, 1024], F32)
    CF = sb("CF", [128, 32], F32)
    PBF = sb("PBF", [128, 4, 256], BF16)
    PTT = sb("PTT", [128, 2, 512], BF16)
    TMPALL = sb("TMPALL", [128, 5, 512], F32)
    Y32 = TMPALL[:, 0, :]
    RSTD = TMPALL[:, 1, :]
    T12 = TMPALL[:, 2:4, :]
    T1 = TMPALL[:, 2, :]
    T2 = TMPALL[:, 3, :]
    RELU = T12
    REC = TMPALL[:, 4, :]
    SQYB = sb("SQYB", [128, 2, 512], BF16)
    SQ = SQYB[:, 0, :]
    YB = SQYB[:, 1, :]
    JUNKS = [TMPALL[:, i, :].bitcast(BF16) for i in (1, 2, 3, 4, 0)] + [SQYB[:].rearrange("p a n -> p (a n)")]
    ST = sb("ST", [128, 48], F32)
    CS = sb("CSB", [128, 1024], F32)[:]
    PSALL = nc.alloc_psum_tensor("PSALL", [128, 4096], F32)
    PS = [PSALL[:, b * 512:(b + 1) * 512] for b in range(8)]

    def hview(hbase, i):
        return H[:, hbase + 2 * i:hbase + 2 * i + 2, :].rearrange("p a n -> p (a n)")

    DT = H[:, 12:16, :]
    YS = H[:, 16:32, :].rearrange("p a n -> p (a n)").bitcast(F32).rearrange("p (c d) -> p c d", c=4)
    GT = H[:, 20:28, :].rearrange("p a n -> p (a n)").bitcast(F32)
    IDENT = CB[:, 0:128]
    RT = CB[:, 128:256]
    BLK = CB[:, 256:384]
    WPOOL = CB[:, 384:896].rearrange("p (g d) -> p g d", g=4)

    b_kt = [Buf(f"kt{i}") for i in range(TMAX // 512)]
    b_vx = [Buf(f"vx{i}") for i in range(TMAX // 512)]
    b_xs = [[Buf(f"xa{i}") for i in range(4)], [Buf(f"xb{i}") for i in range(4)]]
    b_xh = [Buf("xh0"), Buf("xh1")]
    b_at = [Buf(f"at{i}") for i in range(8)]
    b_qt = [Buf(f"qt{i}") for i in range(4)]
    b_u = [Buf(f"u{i}") for i in range(6)]
    b_mix = [Buf(f"mix{i}") for i in range(8)]
    b_h = [Buf(f"h{i}") for i in range(32)]
    b_ring = [Buf(f"ring{i}") for i in range(NRING)]
    b_wsc = [Buf(f"wsc{i}") for i in range(NCHUNK)]
    b_wkv, b_cb, b_band, b_fng, b_cf = Buf("wkv"), Buf("cb"), Buf("band"), Buf("fng"), Buf("cf")
    b_pt = [Buf(f"pt{i}") for i in range(4)]
    b_pbf, b_ptt = Buf("pbf"), Buf("ptt")
    b_y32, b_sq, b_yb, b_rstd, b_t1, b_t2 = (Buf(n) for n in ("y32", "sq", "yb", "rstd", "t1", "t2"))
    b_rec, b_vones = Buf("rec"), Buf("vones")
    b_junks = [[b_rstd], [b_t1], [b_t2], [b_rec], [b_y32], [b_sq, b_yb]]
    b_st = [Buf("st0"), Buf("st1")]
    b_relu = [b_t1, b_t2]
    b_ps = [Buf(f"ps{i}") for i in range(8)]
    b_y = [Buf(f"yout{j}") for j in range(len(jobs))]
    b_hc = lambda hbase, i: [b_h[hbase + 2 * i], b_h[hbase + 2 * i + 1]]
    b_dt = lambda g: [b_h[12 + g]]
    b_cs = [Buf("cs0"), Buf("cs1")]
    b_gt = b_h[20:28]

    def dma(q, out, in_, key, reads=(), writes=(), extra=()):
        return P.add(q, lambda e: e.dma_start(out=out, in_=in_), reads, writes, dma_key=key, extra_deps=extra)

    def mm(out, lhsT, rhs, start, stop, reads, writes):
        return P.add("pe", lambda e: e.matmul(out, lhsT=lhsT, rhs=rhs, start=start, stop=stop), reads, writes)

    def tr(out, in_, reads, writes):
        return P.add("pe", lambda e: e.transpose(out=out, in_=in_, identity=IDENT), list(reads) + [b_cb], writes)

    def act(out, in_, func, reads, writes, scale=1.0, bias=None, accum=None):
        def f(e):
            kw = {}
            if bias is not None:
                kw["bias"] = bias
            if accum is not None:
                kw["accum_out"] = accum
            return e.activation(out=out, in_=in_, func=func, scale=scale, **kw)
        return P.add("act", f, reads, writes)

    def ts(eng, out, in0, s1, op0, reads, writes):
        return P.add(eng, lambda e: e.tensor_scalar(out=out, in0=in0, scalar1=s1, scalar2=None, op0=op0),
                     reads, writes)

    def tt(eng, out, in0, in1, op, reads, writes):
        return P.add(eng, lambda e: e.tensor_tensor(out=out, in0=in0, in1=in1, op=op), reads, writes)

    def cp(eng, out, in_, reads, writes):
        if eng == "act":
            return act(out, in_, AF.Copy, reads, writes)
        return P.add(eng, lambda e: e.tensor_copy(out=out, in_=in_), reads, writes)

    wb_ctr = [0]

    def wbank():
        b = wb_ctr[0] % 4
        wb_ctr[0] += 1
        return b

    ab_ctr = [0]

    def abank():
        b = 4 + ab_ctr[0] % 4
        ab_ctr[0] += 1
        return b

    alt = [0]

    def evac_eng():
        alt[0] += 1
        return "act" if alt[0] % 2 else "dve"

    dma("pool", CB[:], cstbd, "c_cb", writes=[b_cb])
    dma("pool", WKV[:], wkvd, "c_wkv", writes=[b_wkv])
    dma("sp", CF[:], cstfd, "c_cf", writes=[b_cf])
    dma("sp", FNG[:], fngd, "c_fng", writes=[b_fng])
    P.add("dve", lambda e: e.memset(VX[:, :, 64:128], 1.0), writes=[b_vones])
    cast_state = {"n": 0, "last": None}

    NCA = 8

    def cast_some(cnt, limit):
        for _ in range(cnt):
            k = cast_state["n"]
            if k >= limit:
                return
            dma("pool", wsc[k], wts[k], f"cast{k % 4}", writes=[b_wsc[k]])
            cast_state["n"] = k + 1

    ring_state = {"issued": 0, "got": 0, "released": 0}
    ring_total = ntiles_total * NCHUNK

    def ring_issue():
        n = ring_state["issued"]
        if n >= ring_total:
            return
        slot = n % NRING
        k = n % NCHUNK
        if k >= NCA and cast_state["n"] < NCHUNK:
            cast_some(NCHUNK, NCHUNK)
        assert cast_state["n"] > k
        dma("sp", RING[:, slot, :], wsc[k], f"r{slot}", reads=[b_wsc[k]], writes=[b_ring[slot]])
        ring_state["issued"] = n + 1

    def ring_get():
        n = ring_state["got"]
        assert n < ring_state["issued"], "ring underflow"
        ring_state["got"] = n + 1
        slot = n % NRING
        return RING[:, slot, :], b_ring[slot]

    def ring_release(cnt=1):
        for _ in range(cnt):
            ring_state["released"] += 1
            ring_issue()

    GM, GL, GP, QG, KG, PSC = 0, 8, 16, 24, 25, 26

    _recb = REC.bitcast(BF16)
    TS = [
        dict(y32=Y32, rstd=RSTD, t1=T1, t2=T2, sq=SQ, yb=YB,
             b_y32=[b_y32], b_rstd=[b_rstd], b_t1=[b_t1], b_t2=[b_t2], b_sq=[b_sq], b_yb=[b_yb]),
        dict(y32=PTB[:, 0:2, :].rearrange("p a n -> p (a n)").bitcast(F32),
             rstd=PTB[:, 2:4, :].rearrange("p a n -> p (a n)").bitcast(F32),
             t1=MIX[:, 0:2, :].rearrange("p a n -> p (a n)").bitcast(F32),
             t2=MIX[:, 2:4, :].rearrange("p a n -> p (a n)").bitcast(F32),
             sq=_recb[:, 0:512], yb=_recb[:, 512:1024],
             b_y32=[b_pt[0], b_pt[1]], b_rstd=[b_pt[2], b_pt[3]], b_t1=[b_mix[0], b_mix[1]],
             b_t2=[b_mix[2], b_mix[3]], b_sq=[b_rec], b_yb=[b_rec]),
    ]

    def n_stats(srcs, s):
        n = len(srcs)
        o = 24 * s
        for i, (ap, bufs) in enumerate(srcs):
            act(JUNKS[i], ap, AF.Square, bufs, b_junks[i] + [b_st[s]], accum=ST[:, o + i:o + i + 1])
        act(ST[:, o + 8:o + 8 + n], ST[:, o:o + n], AF.Ln, [b_st[s]], [b_st[s]], scale=1.0 / 1024, bias=EPS)
        act(ST[:, o + 16:o + 16 + n], ST[:, o + 8:o + 8 + n], AF.Exp, [b_st[s]], [b_st[s]], scale=-0.5)

    ALT_V = [U6[:, 0:2, :], U6[:, 2:4, :], U6[:, 4:6, :], QT[:, 0:2, :], QT[:, 2:4, :], PTB[:, 0:2, :]]
    ALT_B = [[b_u[0], b_u[1]], [b_u[2], b_u[3]], [b_u[4], b_u[5]], [b_qt[0], b_qt[1]], [b_qt[2], b_qt[3]],
             [b_pt[0], b_pt[1]]]

    def aview(loc, i):
        if loc == "alt":
            return ALT_V[i].rearrange("p a n -> p (a n)")
        return hview(loc, i)

    def abufs(loc, i):
        return ALT_B[i] if loc == "alt" else b_hc(loc, i)

    def n_apr(srcs, s, hbase):
        o = 24 * s + 16
        for i, (ap, bufs) in enumerate(srcs):
            ts("dve", aview(hbase, i), ap, ST[:, o + i:o + i + 1], ALU.mult, list(bufs) + [b_st[s]], abufs(hbase, i))

    def n_tr(n, hbase, gcol, act_kcs=(0, 2, 4, 6), banks=None):
        for kc in range(8):
            bk = wbank() if banks is None else banks[kc]
            pbf = PS[bk][:].bitcast(BF16)
            for i in range(n):
                tr(pbf[:, i * 128:(i + 1) * 128], aview(hbase, i)[:, kc * 128:(kc + 1) * 128], abufs(hbase, i),
                   [b_ps[bk]])
            eng = "act" if kc in act_kcs else "dve"
            g = CF[:, gcol + kc:gcol + kc + 1]
            if eng == "act":
                act(AT[:, kc, 0:n * 128], pbf[:, 0:n * 128], AF.Copy, [b_ps[bk], b_cf], [b_at[kc]], scale=g)
            else:
                ts("dve", AT[:, kc, 0:n * 128], pbf[:, 0:n * 128], g, ALU.mult, [b_ps[bk], b_cf], [b_at[kc]])

    def rope_pre(zb, gcol, k=0):
        t = TS[k]
        Z = PS[zb][:]
        act(t["sq"], Z, AF.Square, [b_ps[zb]], t["b_sq"])
        act(t["y32"], Z, AF.Copy, [b_ps[zb], b_cf], t["b_y32"], scale=CF[:, gcol:gcol + 1])
        cp("dve", t["yb"], t["y32"], t["b_y32"], t["b_yb"])

    def rope_mm(banks, k=0):
        t = TS[k]
        sb_, rb = banks
        mm(PS[sb_][:], BLK, t["sq"], True, True, t["b_sq"] + [b_cb], [b_ps[sb_]])
        mm(PS[rb][:], RT, t["yb"], True, True, t["b_yb"] + [b_cb], [b_ps[rb]])

    def rope_post(banks, dst, dst_bufs, add_eng="dve", k=0):
        t = TS[k]
        sb_, rb = banks
        act(t["rstd"], PS[sb_][:], AF.Ln, [b_ps[sb_]], t["b_rstd"], scale=1.0 / 64, bias=EPS)
        act(t["rstd"], t["rstd"], AF.Exp, t["b_rstd"], t["b_rstd"], scale=-0.5)
        tt("dve", t["t1"], t["y32"], CS[:, 0:512], ALU.mult, t["b_y32"] + [b_cs[0]], t["b_t1"])
        tt("dve", t["t2"], PS[rb][:], CS[:, 512:1024], ALU.mult, [b_ps[rb], b_cs[1]], t["b_t2"])
        tt(add_eng, t["t1"], t["t1"], t["t2"], ALU.add, t["b_t1"] + t["b_t2"], t["b_t1"])
        tt("pool", dst, t["t1"], t["rstd"], ALU.mult, t["b_t1"] + t["b_rstd"], dst_bufs)

    xctr = [0]
    last_store = []
    for j, (T, OWN) in enumerate(jobs):
        NKT = T // 512
        NKC = T // 128
        NT = OWN // 512
        NOC = OWN // 128
        dma("pool", BAND[:, 0:1280], bandd[j][:, 0:1280], "c_band0", writes=[b_band])
        dma("pool", BAND[:, 1280:2560], bandd[j][:, 1280:2560], "c_band1", writes=[b_band])
        cast_per = -(-NCA // NKT)

        XA3 = [XS[0][:], XS[1][:], YS]
        BA3 = [[[bb] for bb in b_xs[0]], [[bb] for bb in b_xs[1]], [b_h[16 + 4 * c:20 + 4 * c] for c in range(4)]]

        def a_load(kt):
            r = kt % 3
            src = xkeys[j][kt * 512:(kt + 1) * 512, :].rearrange("(c p) d -> p c d", p=128)
            dma("sp", XA3[r], src, f"x{r}", writes=[bb for grp in BA3[r] for bb in grp])

        def a_cs(kt):
            dma("sp", CS[:, 0:512], cosd[j][:, kt * 512:(kt + 1) * 512], "cs0", writes=[b_cs[0]])
            dma("sp", CS[:, 512:1024], sind[j][:, kt * 512:(kt + 1) * 512], "cs1", writes=[b_cs[1]])

        def a_srcs(kt):
            r = kt % 3
            return [(XA3[r][:, c, :], BA3[r][c]) for c in range(4)]

        a_load(0)
        a_cs(0)
        if NKT > 1:
            a_load(1)
        n_stats(a_srcs(0), 1)
        n_apr(a_srcs(0), 1, 0)
        for kt in range(NKT):
            sset = (kt + 1) % 2
            n_tr(4, 0, GM, act_kcs=(7,), banks=list(range(8)))
            if j == 0:
                cast_some(cast_per, NCA)
            if kt + 2 < NKT:
                a_load(kt + 2)
            if kt + 1 < NKT:
                n_stats(a_srcs(kt + 1), 1 - sset)
                n_apr(a_srcs(kt + 1), 1 - sset, 0)
            zb = wbank()
            for kc in range(8):
                mm(PS[zb][:], WKV[:, kc * 128:(kc + 1) * 128], AT[:, kc, 0:512], kc == 0, kc == 7,
                   [b_wkv, b_at[kc]], [b_ps[zb]])
            rope_pre(zb, KG)
            vb = abank()
            for c in range(4):
                for kc in range(8):
                    mm(PS[vb][:, c * 128:(c + 1) * 128], AT[:, kc, c * 128:(c + 1) * 128],
                       WKV[:, 1024 + kc * 128:1024 + (kc + 1) * 128], kc == 0, kc == 7,
                       [b_wkv, b_at[kc]], [b_ps[vb]])
            rb = (abank(), abank())
            rope_mm(rb)
            rope_post(rb, KT[:, kt * 512:(kt + 1) * 512], [b_kt[kt]], add_eng="pool")
            if kt + 1 < NKT:
                a_cs(kt + 1)
            pv = PS[vb][:].rearrange("p (c n) -> p c n", c=4)
            cp("dve", VX[:, kt * 4:(kt + 1) * 4, 0:64], pv[:, :, 0:64], [b_ps[vb]], [b_vx[kt]])
            cp("act", VX[:, kt * 4:(kt + 1) * 4, 128:192], pv[:, :, 64:128], [b_ps[vb]], [b_vx[kt]])
        if j == 0:
            cast_some(NCA, NCA)
            for _ in range(NRING):
                ring_issue()

        tile_par = {}

        def b_load(it):
            par = xctr[0] % 2
            xctr[0] += 1
            tile_par[it] = par
            r0 = it * 512
            dma("sp", XH[:, 0, :], xpad[j][r0:r0 + 128, :], "xh0", writes=[b_xh[0]])
            src = xpad[j][r0 + 128:r0 + 640, :].rearrange("(c p) d -> p c d", p=128)
            dma("sp", XS[par][:], src, f"x{par}", writes=b_xs[par])
            dma("sp", XH[:, 1, :], xpad[j][r0 + 640:r0 + 768, :], "xh1", writes=[b_xh[1]])

        def ld_cs(it):
            dma("sp", CS[:, 0:512], cosd[j][:, it * 512:(it + 1) * 512], "cs0", writes=[b_cs[0]])
            dma("sp", CS[:, 512:1024], sind[j][:, it * 512:(it + 1) * 512], "cs1", writes=[b_cs[1]])

        def b_srcs6(it):
            par = tile_par[it]
            return ([(XH[:, 0, :], [b_xh[0]])] + [(XS[par][:, c, :], [b_xs[par][c]]) for c in range(4)]
                    + [(XH[:, 1, :], [b_xh[1]])])

        def b_srcs4(it):
            par = tile_par[it]
            return [(XS[par][:, c, :], [b_xs[par][c]]) for c in range(4)]

        def head(it, aloc=0):
            psrc = pin[j][it * 512:(it + 1) * 512, :].rearrange("(c p) d -> p c d", p=128)
            dma("pool", PBF[:], psrc, "p", writes=[b_pbf])
            n_tr(6, aloc, GM)
            wq = [ring_get(), ring_get()]
            wu = [ring_get(), ring_get()]

            def qproj(jq):
                wv_, wb_ = wq[jq // 2]
                wq_v = wv_.rearrange("p (f k m) -> p f k m", f=2, k=8)
                for kc in range(8):
                    mm(PS[jq][:], wq_v[:, jq % 2, kc, :], AT[:, kc, 128:640], kc == 0, kc == 7,
                       [wb_, b_at[kc]], [b_ps[jq]])

            def uproj(c):
                ub = abank()
                for kc in range(8):
                    wv_, wb_ = wu[kc // 4]
                    wu_v = wv_.rearrange("p (k n) -> p k n", k=4)
                    mm(PS[ub][:], AT[:, kc, c * 128:(c + 1) * 128], wu_v[:, kc % 4, :], kc == 0, kc == 7,
                       [wb_, b_at[kc]], [b_ps[ub]])
                cp(evac_eng(), U6[:, c, :], PS[ub][:], [b_ps[ub]], [b_u[c]])

            qproj(0)
            qproj(1)
            rope_pre(0, QG, 0)
            uproj(0)
            for jq in range(4):
                if jq + 1 < 4:
                    rope_pre(jq + 1, QG, (jq + 1) % 2)
                rb = (abank(), abank())
                rope_mm(rb, jq % 2)
                if jq + 1 < 4:
                    uproj(jq + 1)
                if jq + 2 < 4:
                    qproj(jq + 2)
                rope_post(rb, QT[:, jq, :], [b_qt[jq]], k=jq % 2)
            uproj(4)
            uproj(5)
            ring_release(4)
            bandv = BAND[:].rearrange("p (t g n) -> p t g n", t=5, g=4)
            for g in range(4):
                db = wbank()
                for ci in range(1, 5):
                    gi = it * 4 + (ci - 1)
                    curt = 2 if gi == 0 else (4 if gi == NOC - 1 else 3)
                    o = PS[db][:, (ci - 1) * 128:ci * 128]
                    mm(o, U6[:, ci - 1, g * 128:(g + 1) * 128], bandv[:, 0, g, :], True, False,
                       [b_u[ci - 1], b_band], [b_ps[db]])
                    mm(o, U6[:, ci, g * 128:(g + 1) * 128], bandv[:, curt, g, :], False, False,
                       [b_u[ci], b_band], [b_ps[db]])
                    mm(o, U6[:, ci + 1, g * 128:(g + 1) * 128], bandv[:, 1, g, :], False, True,
                       [b_u[ci + 1], b_band], [b_ps[db]])
                cp(evac_eng(), DT[:, g, :], PS[db][:], [b_ps[db]], b_dt(g))
                ob = wbank()
                mm(PS[ob][:], WPOOL[:, g, :], DT[:, g, :], True, True, [b_cb] + b_dt(g), [b_ps[ob]])
                ts("dve", MIX[:, 4 + g, :], PS[ob][:], CF[:, PSC + g:PSC + g + 1], ALU.mult,
                   [b_ps[ob], b_cf], [b_mix[4 + g]])

        def attention(it):
            for c in range(4):
                oa, ob_ = 4 + 2 * (c % 2), 5 + 2 * (c % 2)

                def qk(kc):
                    sa, sbk = 2 * (kc % 2), 2 * (kc % 2) + 1
                    mm(PS[sa][:], KT[0:64, kc * 128:(kc + 1) * 128], QT[0:64, c, :], True, True,
                       [b_kt[kc // 4], b_qt[c]], [b_ps[sa]])
                    mm(PS[sbk][:], KT[64:128, kc * 128:(kc + 1) * 128], QT[64:128, c, :], True, True,
                       [b_kt[kc // 4], b_qt[c]], [b_ps[sbk]])

                qk(0)
                for kc in range(NKC):
                    if kc + 1 < NKC:
                        qk(kc + 1)
                    sa, sbk = 2 * (kc % 2), 2 * (kc % 2) + 1
                    act(PTB[:, sa, :], PS[sa][:], AF.Exp, [b_ps[sa]], [b_pt[sa]], scale=0.125)
                    act(PTB[:, sbk, :], PS[sbk][:], AF.Exp, [b_ps[sbk]], [b_pt[sbk]], scale=0.125)
                    mm(PS[oa][:], VX[:, kc, 0:128], PTB[:, sa, :], kc == 0, kc == NKC - 1,
                       [b_vx[kc // 4], b_vones, b_pt[sa]], [b_ps[oa]])
                    mm(PS[ob_][:], VX[:, kc, 64:192], PTB[:, sbk, :], kc == 0, kc == NKC - 1,
                       [b_vx[kc // 4], b_vones, b_pt[sbk]], [b_ps[ob_]])
                P.add("dve", lambda e, oa=oa: e.reciprocal(out=REC[0:64, :], in_=PS[oa][64:128, :]),
                      [b_ps[oa]], [b_rec])
                tt("dve", MIX[0:64, c, :], PS[oa][0:64, :], REC[0:64, :], ALU.mult, [b_ps[oa], b_rec], [b_mix[c]])
                P.add("dve", lambda e, ob_=ob_: e.reciprocal(out=REC[64:128, :], in_=PS[ob_][0:64, :]),
                      [b_ps[ob_]], [b_rec])
                tt("dve", MIX[64:128, c, :], PS[ob_][64:128, :], REC[64:128, :], ALU.mult,
                   [b_ps[ob_], b_rec], [b_mix[c]])

        def mid(it, nxt):
            par = tile_par[it]
            X = XS[par]
            bx = b_xs[par]
            wo = [ring_get() for _ in range(4)]
            for t in range(4):
                for hf in range(2):
                    ab = 4 + (2 * t + hf) % 4
                    for kc in range(8):
                        wv_, wb_ = wo[kc // 2]
                        wo_v = wv_.rearrange("p (k n) -> p k n", k=2)
                        mm(PS[ab][:], MIX[:, kc, t * 128:(t + 1) * 128], wo_v[:, kc % 2, hf * 512:(hf + 1) * 512],
                           kc == 0, kc == 7, [wb_, b_mix[kc]], [b_ps[ab]])
                    xs = X[:, t, hf * 512:(hf + 1) * 512]
                    tt("dve", xs, xs, PS[ab][:], ALU.add, [bx[t], b_ps[ab]], [bx[t]])
            ring_release(4)
            n_stats(b_srcs4(it), 0)
            n_apr(b_srcs4(it), 0, 0)
            n_tr(4, 0, GL)
            for r in range(16):
                wv_, wb_ = ring_get()
                wu_v = wv_.rearrange("p (f k m) -> p f k m", f=2, k=8)
                for ff in range(2):
                    fc = 2 * r + ff
                    ub = wbank()
                    for kc in range(8):
                        mm(PS[ub][:], wu_v[:, ff, kc, :], AT[:, kc, 0:512], kc == 0, kc == 7,
                           [wb_, b_at[kc]], [b_ps[ub]])
                    rs = fc % 2
                    act(RELU[:, rs, :], PS[ub][:], AF.Relu, [b_ps[ub]], [b_relu[rs]])
                    tt("dve" if fc % 4 else "pool", H[:, fc, :], RELU[:, rs, :], RELU[:, rs, :], ALU.mult,
                       [b_relu[rs]], [b_h[fc]])
                ring_release(1)
            if nxt is not None:
                n_stats(b_srcs6(nxt), 1)
                n_apr(b_srcs6(nxt), 1, "alt")
            for hf in range(2):
                for r in range(8):
                    wv_, wb_ = ring_get()
                    wd_v = wv_.rearrange("p (f n) -> p f n", f=4)
                    for ff in range(4):
                        fc = 4 * r + ff
                        for t in range(4):
                            mm(PS[4 + t][:], H[:, fc, t * 128:(t + 1) * 128], wd_v[:, ff, :], fc == 0, fc == 31,
                               [wb_, b_h[fc]], [b_ps[4 + t]])
                    ring_release(1)
                for t in range(4):
                    xs = X[:, t, hf * 512:(hf + 1) * 512]
                    tt("dve", xs, xs, PS[4 + t][:], ALU.add, [bx[t], b_ps[4 + t]], [bx[t]])
            n_stats(b_srcs4(it), 0)
            n_apr(b_srcs4(it), 0, 12)
            n_tr(4, 12, GP)
            for kc in range(2):
                bk = wbank()
                pbf = PS[bk][:].bitcast(BF16)
                for t in range(4):
                    tr(pbf[:, t * 128:(t + 1) * 128], PBF[:, t, kc * 128:(kc + 1) * 128], [b_pbf], [b_ps[bk]])
                cp(evac_eng(), PTT[:, kc, :], pbf[:, 0:512], [b_ps[bk]], [b_ptt])
            wg = [ring_get() for _ in range(4)]
            wp_v, wp_b = ring_get()
            wp_v = wp_v.rearrange("p (k n) -> p k n", k=2)
            for t in range(4):
                for hf in range(2):
                    gb = 4 + (2 * t + hf) % 2
                    pb = 6 + (2 * t + hf) % 2
                    for kc in range(8):
                        wv_, wb_ = wg[kc // 2]
                        wg_v = wv_.rearrange("p (k n) -> p k n", k=2)
                        mm(PS[gb][:], AT[:, kc, t * 128:(t + 1) * 128], wg_v[:, kc % 2, hf * 512:(hf + 1) * 512],
                           kc == 0, kc == 7, [wb_, b_at[kc]], [b_ps[gb]])
                    for kc in range(2):
                        mm(PS[pb][:], PTT[:, kc, t * 128:(t + 1) * 128], wp_v[:, kc, hf * 512:(hf + 1) * 512],
                           kc == 0, kc == 1, [wp_b, b_ptt], [b_ps[pb]])
                    k2 = (2 * t + hf) % 2
                    G = GT[:, k2 * 512:(k2 + 1) * 512]
                    gbufs = b_gt[2 * k2:2 * k2 + 2]
                    act(G, PS[gb][:], AF.Sigmoid, [b_ps[gb]], gbufs)
                    tt("dve", G, G, PS[pb][:], ALU.mult, gbufs + [b_ps[pb]], gbufs)
                    xs = X[:, t, hf * 512:(hf + 1) * 512]
                    tt("dve", xs, xs, G, ALU.add, [bx[t]] + gbufs, [bx[t]])
            ring_release(5)

        def final(it):
            par = tile_par[it]
            n_stats(b_srcs4(it), 0)
            for t in range(4):
                xs = XS[par][:, t, :]
                ys = YS[:, t, :]
                P.add("dve", lambda e, xs=xs, ys=ys, t=t: e.scalar_tensor_tensor(
                    out=ys, in0=xs, scalar=ST[:, 16 + t:17 + t], in1=FNG[:], op0=ALU.mult, op1=ALU.mult),
                    [b_xs[par][t], b_st[0], b_fng], b_h[16 + 4 * t:20 + 4 * t])
            dst = yout[j][it * 512:(it + 1) * 512, :].rearrange("(c p) d -> p c d", p=128)
            st = dma("sp", dst, YS, "y", reads=b_h[16:32], writes=[b_y[j]])
            last_store.append(st)

        b_load(0)
        ld_cs(0)
        n_stats(b_srcs6(0), 1)
        n_apr(b_srcs6(0), 1, 0)
        head(0)
        for it in range(NT):
            nxt = it + 1 if it + 1 < NT else None
            if nxt is not None:
                b_load(nxt)
                ld_cs(nxt)
            attention(it)
            mid(it, nxt)
            if nxt is not None:
                head(nxt, "alt")
            final(it)

    assert ring_state["got"] == ring_total and ring_state["issued"] == ring_total, ring_state
    P.emit(final_waits=last_store[-1:])
    return nc


def _rope_tables(T):
    t = np.arange(T)
    row = (t // 64).astype(np.float32)
    col = (t % 64).astype(np.float32)
    inv = (np.float32(10000.0) ** (-np.arange(16, dtype=np.float32) / np.float32(16))).astype(np.float32)
    ar = (row[:, None] * inv[None, :]).astype(np.float32)
    ac = (col[:, None] * inv[None, :]).astype(np.float32)
    ang = np.concatenate([ar, ar, ac, ac], axis=1)
    cos = np.cos(ang).astype(np.float32).T
    sin = np.sin(ang).astype(np.float32).T
    return np.ascontiguousarray(np.tile(cos, (2, 1))), np.ascontiguousarray(np.tile(sin, (2, 1)))


def _band_block(T, w, src0, dst0):
    half = w // 2
    out = np.zeros((128, 128), np.float32)
    for jj in range(128):
        tp = dst0 + jj
        if tp < 0 or tp >= T:
            continue
        lo = max(tp - half, 0)
        hi = min(tp + half, T)
        cnt = hi - lo
        for t in range(lo, hi):
            i = t - src0
            if 0 <= i < 128:
                out[i, jj] += 1.0 / cnt
        i = tp - src0
        if 0 <= i < 128:
            out[i, jj] -= 1.0
    return out


def _band_mats(T, OWN, half):
    wins = (2, 4, 8, 16)
    g0 = half * OWN
    BIG = 1 << 20
    mid0 = 4096
    res = np.zeros((128, 5, 4, 128), np.float32)
    for g, w in enumerate(wins):
        res[:, 0, g] = _band_block(BIG, w, mid0 - 128, mid0)
        res[:, 1, g] = _band_block(BIG, w, mid0 + 128, mid0)
        res[:, 2, g] = _band_block(T, w, g0, g0)
        res[:, 3, g] = _band_block(BIG, w, mid0, mid0)
        l0 = g0 + OWN - 128
        res[:, 4, g] = _band_block(T, w, l0, l0)
    return np.ascontiguousarray(res.reshape(128, 2560))


def _weight_chunks(w_in, w_out, w_up, w_down, w_gate, w_proj):
    ch = np.zeros((NCHUNK, 128, 2048), np.float32)
    qcols = np.zeros((4, 128), np.int64)
    for jq in range(4):
        qcols[jq, :64] = jq * 64 + np.arange(64)
        qcols[jq, 64:] = (jq + 4) * 64 + np.arange(64)
    wr = w_in.reshape(8, 128, 1280)
    n = 0
    for r in range(2):
        blk = np.stack([wr[:, :, qcols[2 * r + f]] for f in range(2)], axis=0)
        ch[n] = blk.transpose(2, 0, 1, 3).reshape(128, 2048)
        n += 1
    for r in range(2):
        blk = wr[4 * r:4 * r + 4, :, 768:1280]
        ch[n] = blk.transpose(1, 0, 2).reshape(128, 2048)
        n += 1
    rows = np.zeros(1024, np.int64)
    for c in range(4):
        rows[c * 128:c * 128 + 64] = c * 64 + np.arange(64)
        rows[c * 128 + 64:(c + 1) * 128] = (c + 4) * 64 + np.arange(64)
    rows[512:] = np.arange(512, 1024)
    wo = w_out[rows, :].reshape(8, 128, 1024)
    for r in range(4):
        ch[n] = wo[2 * r:2 * r + 2].transpose(1, 0, 2).reshape(128, 2048)
        n += 1
    wu = w_up.reshape(8, 128, 32, 128)
    for r in range(16):
        blk = wu[:, :, 2 * r:2 * r + 2, :]
        ch[n] = blk.transpose(1, 2, 0, 3).reshape(128, 2048)
        n += 1
    wd = w_down.reshape(32, 128, 2, 512)
    for hf in range(2):
        for r in range(8):
            blk = wd[4 * r:4 * r + 4, :, hf, :]
            ch[n] = blk.transpose(1, 0, 2).reshape(128, 2048)
            n += 1
    wg = w_gate.reshape(8, 128, 1024)
    for r in range(4):
        ch[n] = wg[2 * r:2 * r + 2].transpose(1, 0, 2).reshape(128, 2048)
        n += 1
    ch[n] = w_proj.reshape(2, 128, 1024).transpose(1, 0, 2).reshape(128, 2048)
    n += 1
    assert n == NCHUNK
    wkv = np.zeros((128, 2048), np.float32)
    wkv[:, 0:1024] = wr[:, :, 512:640].transpose(1, 0, 2).reshape(128, 1024)
    wkv[:, 1024:2048] = wr[:, :, 640:768].transpose(1, 0, 2).reshape(128, 1024)
    return ch, wkv


def _consts(norm_mix_g, norm_mlp_g, norm_ple_g, q_norm_g, k_norm_g, pool_scale, w_pool, final_norm_g):
    cb = np.zeros((128, 896), np.float32)
    cb[:, 0:128] = np.eye(128, dtype=np.float32)
    rt = np.zeros((128, 128), np.float32)
    for m in range(128):
        if m % 32 < 16:
            rt[m + 16, m] = -1.0
        else:
            rt[m - 16, m] = 1.0
    cb[:, 128:256] = rt
    blk = np.zeros((128, 128), np.float32)
    blk[:64, :64] = 1.0
    blk[64:, 64:] = 1.0
    cb[:, 256:384] = blk
    cb[:, 384:896] = w_pool.transpose(1, 0, 2).reshape(128, 512)
    cf = np.zeros((128, 32), np.float32)
    cf[:, 0:8] = norm_mix_g.reshape(8, 128).T
    cf[:, 8:16] = norm_mlp_g.reshape(8, 128).T
    cf[:, 16:24] = norm_ple_g.reshape(8, 128).T
    cf[:, 24] = np.tile(q_norm_g, 2)
    cf[:, 25] = np.tile(k_norm_g, 2)
    cf[:, 26:30] = pool_scale.reshape(4, 128).T
    fng = np.ascontiguousarray(np.broadcast_to(final_norm_g[None, :], (128, 1024))).astype(np.float32)
    return cb, cf, fng


def make_in_maps(jobs_data, weights):
    (norm_mix_g, w_in, q_norm_g, k_norm_g, w_pool, pool_scale, w_out, norm_mlp_g, w_up, w_down,
     norm_ple_g, w_ple_gate, w_ple_proj, final_norm_g) = weights
    ch, wkv = _weight_chunks(w_in, w_out, w_up, w_down, w_ple_gate, w_ple_proj)
    cb, cf, fng = _consts(norm_mix_g, norm_mlp_g, norm_ple_g, q_norm_g, k_norm_g, pool_scale, w_pool, final_norm_g)
    in_maps = []
    tabs = {}
    for core_jobs in jobs_data:
        m = {"wts": ch, "wkv": wkv, "cstb": cb, "cstf": cf, "fng": fng}
        for j, jd in enumerate(core_jobs):
            x, p, half = jd["x"], jd["p"], jd["half"]
            T = x.shape[0]
            OWN = T // 2
            o0 = half * OWN
            own = slice(o0, o0 + OWN)
            oth = slice((1 - half) * OWN, (1 - half) * OWN + OWN)
            m[f"xkeys{j}"] = np.ascontiguousarray(np.concatenate([x[own], x[oth]], axis=0))
            xp = np.zeros((OWN + 256, 1024), np.float32)
            xp[128:128 + OWN] = x[own]
            if o0 >= 128:
                xp[0:128] = x[o0 - 128:o0]
            if o0 + OWN + 128 <= T:
                xp[128 + OWN:] = x[o0 + OWN:o0 + OWN + 128]
            m[f"xpad{j}"] = xp
            m[f"p{j}"] = np.ascontiguousarray(p[own])
            if T not in tabs:
                tabs[T] = _rope_tables(T)
            cos, sin = tabs[T]
            m[f"cos{j}"] = np.ascontiguousarray(np.concatenate([cos[:, own], cos[:, oth]], axis=1))
            m[f"sin{j}"] = np.ascontiguousarray(np.concatenate([sin[:, own], sin[:, oth]], axis=1))
            m[f"band{j}"] = _band_mats(T, OWN, half)
        in_maps.append(m)
    return in_maps


_NC_CACHE = {}


def run(x_prompt, x_sample, p_prompt, p_sample, weights, n_cores=8):
    Ts, Tp = x_sample.shape[1], x_prompt.shape[1]
    jobs = [(Ts, Ts // 2), (Tp, Tp // 2)]
    key = tuple(jobs)
    if key not in _NC_CACHE:
        _NC_CACHE[key] = build_program(jobs)
    nc = _NC_CACHE[key]
    jobs_data = []
    for c in range(n_cores):
        s, half = c // 2, c % 2
        jobs_data.append([
            {"x": x_sample[s], "p": p_sample[s], "half": half},
            {"x": x_prompt[s], "p": p_prompt[s], "half": half},
        ])
    in_maps = make_in_maps(jobs_data, weights)
    res = run_bass_kernel_spmd(nc, in_maps, core_ids=list(range(n_cores)))
    y_s = np.zeros_like(x_sample)
    y_p = np.zeros_like(x_prompt)
    for c in range(n_cores):
        s, half = c // 2, c % 2
        r = res.results[c]
        y_s[s, half * (Ts // 2):(half + 1) * (Ts // 2)] = r["y0"]
        y_p[s, half * (Tp // 2):(half + 1) * (Tp // 2)] = r["y1"]
    return y_p, y_s


def kernel(x_prompt, x_sample, p_prompt, p_sample, norm_mix_g, w_in, q_norm_g, k_norm_g, w_pool, pool_scale,
           w_out, norm_mlp_g, w_up, w_down, norm_ple_g, w_ple_gate, w_ple_proj, final_norm_g):
    f = lambda a: np.asarray(a, dtype=np.float32)
    weights = (f(norm_mix_g)[0], f(w_in)[0], f(q_norm_g)[0], f(k_norm_g)[0], f(w_pool)[0], f(pool_scale)[0],
               f(w_out)[0], f(norm_mlp_g)[0], f(w_up)[0], f(w_down)[0], f(norm_ple_g)[0], f(w_ple_gate)[0],
               f(w_ple_proj)[0], f(final_norm_g))
    y_p, y_s = run(f(x_prompt), f(x_sample), f(p_prompt)[0], f(p_sample)[0], weights)
    return (y_p, y_s)
```
